# Optimizing a Trainium2 kernel written in Bass

```python
import jax, jax.numpy as jnp
from jax import lax
import numpy as np

D_MODEL = 4096
BATCH = 2
SEQ = 4096
DEPTH = 4

N_A_LAYERS = DEPTH // 2
N_B_LAYERS = DEPTH - N_A_LAYERS
MLSTM_HEADS = 8
MLSTM_V_DIM = D_MODEL // MLSTM_HEADS
MLSTM_QK_DIM = MLSTM_V_DIM // 2
MLSTM_CHUNK = 64
GATE_SOFTCAP = 15.0
QK_W = MLSTM_HEADS * MLSTM_QK_DIM
V_W = MLSTM_HEADS * MLSTM_V_DIM
A_IN_W = 2 * QK_W + 2 * V_W + 2 * MLSTM_HEADS
SB_HEADS = 32
SB_HEAD_DIM = D_MODEL // SB_HEADS
SB_BLOCK = 128
FFN_DIM = ((D_MODEL * 8 // 3 + 63) // 64) * 64
CONV_WIDTH = 3
PLE_DIM = 256
NORM_EPS = 1e-6

kernel_name = "yoco_mlstm_stickbreaking_convffn_trunk"


def rmsnorm(x, g):
    xf = x.astype(jnp.float32)
    y = xf * lax.rsqrt(jnp.mean(xf * xf, axis=-1, keepdims=True) + NORM_EPS)
    return (y * g.astype(jnp.float32)).astype(x.dtype)


def softcap(x, cap):
    return cap * jnp.tanh(x / cap)


def mlstm(xn, w_in, gate_bias, head_norm, w_out):
    B, S, _ = xn.shape
    NH, DK, DV, L = MLSTM_HEADS, MLSTM_QK_DIM, MLSTM_V_DIM, MLSTM_CHUNK
    NC = S // L
    proj = (xn @ w_in).astype(jnp.float32)
    q, k, v, og, ig, fg = jnp.split(
        proj, [QK_W, 2 * QK_W, 2 * QK_W + V_W, 2 * QK_W + 2 * V_W, 2 * QK_W + 2 * V_W + NH], axis=-1)
    gb = gate_bias.astype(jnp.float32)
    log_i = softcap(ig + gb[:NH], GATE_SOFTCAP)
    log_f = jax.nn.log_sigmoid(softcap(fg + gb[NH:], GATE_SOFTCAP))

    def to_chunks(t, d):
        return t.reshape(B, NC, L, NH, d).transpose(1, 0, 3, 2, 4)

    qc = to_chunks(q * (DK ** -0.5), DK)
    kc = to_chunks(k, DK)
    vc = to_chunks(v, DV)
    ic = log_i.reshape(B, NC, L, NH).transpose(1, 0, 3, 2)
    fc = log_f.reshape(B, NC, L, NH).transpose(1, 0, 3, 2)
    causal = jnp.tril(jnp.ones((L, L), dtype=bool))

    def step(carry, inp):
        C, n, m = carry
        qb, kb, vb, ib, fb = inp
        b = jnp.cumsum(fb, axis=-1)
        g = b[..., -1]
        Dm = jnp.where(causal, b[..., :, None] - b[..., None, :] + ib[..., None, :], -jnp.inf)
        m_inter = b + m[..., None]
        m_t = jnp.maximum(m_inter, jnp.max(Dm, axis=-1))
        w_intra = jnp.exp(Dm - m_t[..., None])
        w_inter = jnp.exp(m_inter - m_t)
        P = jnp.einsum('bhtd,bhsd->bhts', qb, kb) * w_intra
        num = (w_inter[..., None] * jnp.einsum('bhtd,bhdv->bhtv', qb, C)
               + jnp.einsum('bhts,bhsv->bhtv', P, vb))
        den = w_inter * jnp.einsum('bhtd,bhd->bht', qb, n) + jnp.sum(P, axis=-1)
        h = num / jnp.maximum(jnp.abs(den), jnp.exp(-m_t))[..., None]
        a = g[..., None] - b + ib
        m_new = jnp.maximum(g + m, jnp.max(a, axis=-1))
        wa = jnp.exp(a - m_new[..., None])
        decay = jnp.exp(g + m - m_new)
        C_new = decay[..., None, None] * C + jnp.einsum('bhs,bhsd,bhsv->bhdv', wa, kb, vb)
        n_new = decay[..., None] * n + jnp.einsum('bhs,bhsd->bhd', wa, kb)
        return (C_new, n_new, m_new), h

    init = (jnp.zeros((B, NH, DK, DV), jnp.float32),
            jnp.zeros((B, NH, DK), jnp.float32),
            jnp.zeros((B, NH), jnp.float32))
    _, hc = lax.scan(step, init, (qc, kc, vc, ic, fc))
    hs = hc.transpose(1, 0, 3, 2, 4).reshape(B, S, NH, DV)
    hs = hs * lax.rsqrt(jnp.mean(hs * hs, axis=-1, keepdims=True) + NORM_EPS)
    hs = hs.reshape(B, S, NH * DV) * head_norm.astype(jnp.float32)
    out = (hs * jax.nn.sigmoid(og)).astype(xn.dtype)
    return out @ w_out


def stick_breaking(q, k, v):
    B, S, H, Dh = q.shape
    scale = Dh ** -0.5
    outs = []
    for blk in range(S // SB_BLOCK):
        t0 = blk * SB_BLOCK
        t1 = t0 + SB_BLOCK
        qb = q[:, t0:t1]
        kb = k[:, :t1]
        vb = v[:, :t1]
        z = jnp.einsum('bqhd,bkhd->bhqk', qb, kb).astype(jnp.float32) * scale
        qpos = t0 + jnp.arange(SB_BLOCK)[:, None]
        kpos = jnp.arange(t1)[None, :]
        mask = kpos < qpos
        log_1mb = jnp.where(mask, jax.nn.log_sigmoid(-z), 0.0)
        suffix = lax.cumsum(log_1mb, axis=3, reverse=True) - log_1mb
        A = jnp.where(mask, jnp.exp(jax.nn.log_sigmoid(z) + suffix), 0.0)
        outs.append(jnp.einsum('bhqk,bkhd->bqhd', A.astype(v.dtype), vb))
    return jnp.concatenate(outs, axis=1)


def conv_ffn(xn, w_up, conv_w, conv_b, w_down):
    S = xn.shape[1]
    u = xn @ w_up
    up = jnp.pad(u, ((0, 0), (CONV_WIDTH - 1, 0), (0, 0)))
    c = (conv_w[0] * up[:, 0:S] + conv_w[1] * up[:, 1:S + 1]
         + conv_w[2] * up[:, 2:S + 2] + conv_b)
    gate, val = jnp.split(c, 2, axis=-1)
    return (jax.nn.silu(gate) * val) @ w_down


def per_layer_embed(h, p_l, w_pe, g_norm, w_gate):
    gate = jax.nn.sigmoid((rmsnorm(h, g_norm) @ w_gate).astype(jnp.float32))
    return ((p_l @ w_pe).astype(jnp.float32) * gate).astype(h.dtype)


def setup_inputs(seed: int = 0) -> dict:
    key = jax.random.key(seed)
    ks = iter(jax.random.split(key, 32))
    f32 = jnp.float32

    def nrm(shape, scale):
        return jax.random.normal(next(ks), shape, f32) * scale

    def gain(shape):
        return 1.0 + 0.02 * jax.random.normal(next(ks), shape, f32)

    D, F, NA, NB, L = D_MODEL, FFN_DIM, N_A_LAYERS, N_B_LAYERS, DEPTH
    gate_bias = jnp.concatenate(
        [-2.0 + 0.1 * jax.random.normal(next(ks), (NA, MLSTM_HEADS), f32),
         3.0 + 0.5 * jax.random.normal(next(ks), (NA, MLSTM_HEADS), f32)], axis=-1)
    return {
        "x": nrm((BATCH, SEQ, D), 1.0),
        "p": nrm((DEPTH, BATCH, SEQ, PLE_DIM), 1.0),
        "a_norm_pre": gain((NA, D)),
        "a_w_in": nrm((NA, D, A_IN_W), D ** -0.5),
        "a_gate_bias": gate_bias,
        "a_head_norm": gain((NA, V_W)),
        "a_w_out": nrm((NA, V_W, D), V_W ** -0.5),
        "a_norm_post": gain((NA, D)),
        "kv_norm": gain((D,)),
        "kv_w": nrm((D, 2 * SB_HEADS * SB_HEAD_DIM), D ** -0.5),
        "b_norm_pre": gain((NB, D)),
        "b_w_q": nrm((NB, D, SB_HEADS * SB_HEAD_DIM), D ** -0.5),
        "b_w_out": nrm((NB, SB_HEADS * SB_HEAD_DIM, D), D ** -0.5),
        "b_norm_post": gain((NB, D)),
        "f_norm_pre": gain((L, D)),
        "f_w_up": nrm((L, D, 2 * F), D ** -0.5),
        "f_conv_w": nrm((L, CONV_WIDTH, 2 * F), CONV_WIDTH ** -0.5),
        "f_conv_b": nrm((L, 2 * F), 0.01),
        "f_w_down": nrm((L, F, D), F ** -0.5),
        "f_norm_post": gain((L, D)),
        "ple_w": nrm((L, PLE_DIM, D), PLE_DIM ** -0.5),
        "ple_gate_norm": gain((L, D)),
        "ple_gate_w": nrm((L, D, D), D ** -0.5),
    }


def reference(x, p, a_norm_pre, a_w_in, a_gate_bias, a_head_norm, a_w_out, a_norm_post,
              kv_norm, kv_w, b_norm_pre, b_w_q, b_w_out, b_norm_post,
              f_norm_pre, f_w_up, f_conv_w, f_conv_b, f_w_down, f_norm_post,
              ple_w, ple_gate_norm, ple_gate_w):
    B, S, D = x.shape
    H, Dh = SB_HEADS, SB_HEAD_DIM
    h = x
    k_sh = None
    v_sh = None
    for layer in range(DEPTH):
        if layer < N_A_LAYERS:
            a = layer
            mix = mlstm(rmsnorm(h, a_norm_pre[a]), a_w_in[a], a_gate_bias[a], a_head_norm[a], a_w_out[a])
            h = h + rmsnorm(mix, a_norm_post[a])
        else:
            if layer == N_A_LAYERS:
                kv = rmsnorm(h, kv_norm) @ kv_w
                k_sh = kv[..., :H * Dh].reshape(B, S, H, Dh)
                v_sh = kv[..., H * Dh:].reshape(B, S, H, Dh)
            j = layer - N_A_LAYERS
            q = (rmsnorm(h, b_norm_pre[j]) @ b_w_q[j]).reshape(B, S, H, Dh)
            mix = stick_breaking(q, k_sh, v_sh).reshape(B, S, H * Dh) @ b_w_out[j]
            h = h + rmsnorm(mix, b_norm_post[j])
        ffn = conv_ffn(rmsnorm(h, f_norm_pre[layer]), f_w_up[layer], f_conv_w[layer],
                       f_conv_b[layer], f_w_down[layer])
        h = h + rmsnorm(ffn, f_norm_post[layer])
        h = h + per_layer_embed(h, p[layer], ple_w[layer], ple_gate_norm[layer], ple_gate_w[layer])
    return h
```

```python
import numpy as np
from contextlib import ExitStack
import concourse.bass as bass
import concourse.mybir as mybir
from concourse.bass_utils import run_bass_kernel_spmd

F32 = mybir.dt.float32
BF16 = mybir.dt.bfloat16
AF = mybir.ActivationFunctionType
ALU = mybir.AluOpType

NCORES = 8
CC_MAX_BYTES = 4 << 20
EPS = 1e-6
CAP = 15.0


class Cfg:
    def __init__(self, D=4096, B=2, S=4096, depth=4, TP=4):
        self.D, self.B, self.S, self.depth, self.TP = D, B, S, depth, TP
        self.NCU = TP * B
        self.NT = S
        self.HA = 8 // TP
        self.KC = D // 128
        self.NA = depth // 2
        self.DV = D // 8
        self.DK = self.DV // 2
        self.DKC = self.DK // 128
        self.DVC = self.DV // 128
        self.QKW = 8 * self.DK
        self.VW = 8 * self.DV
        self.H = D // 128
        self.HPC = self.H // TP
        self.AW = self.HPC * 128
        self.F = ((D * 8 // 3 + 63) // 64) * 64
        self.FC = self.F // TP
        self.FCH = (self.FC + 127) // 128
        self.FP = self.FCH * 128
        self.DC = D // TP
        self.PR = depth * 256 // TP
        self.DCC = self.DC // 128
        self.TMW = 2 * self.DV + self.DK
        self.TMN = self.TMW + 2


class Dep:
    __slots__ = ("w", "r")

    def __init__(self):
        self.w = None
        self.r = {}


class Ctx:
    def __init__(self, nc, TP=4, NG=2):
        self.nc = nc
        self.TP, self.NG = TP, NG
        self.es = ExitStack()
        self.eng = {"pe": nc.tensor, "act": nc.scalar, "dve": nc.vector,
                    "pool": nc.gpsimd, "sp": nc.sync}
        self.psem, self.cnt, self.waited = {}, {}, {}
        for e in self.eng:
            self.psem[e] = self.es.enter_context(nc.semaphore("ps_" + e))
            self.cnt[e] = 0
            self.waited[e] = {}
        self.slots, self.slot_cnt, self.dma_idx = {}, {}, {}
        for q, k in {"sp": 12, "pool": 8}.items():
            self.slots[q] = [self.es.enter_context(nc.semaphore(f"dq_{q}{i}")) for i in range(k)]
            for s in self.slots[q]:
                self.slot_cnt[s] = 0
            self.dma_idx[q] = 0
        self.ccsem = self.es.enter_context(nc.semaphore("cc"))
        self.cc_cnt = 0
        self.uid = 0

    def sb(self, stack, shape, dt, name="t"):
        self.uid += 1
        return stack.enter_context(self.nc.sbuf_tensor(f"{name}_{self.uid}", list(shape), dt))

    def ps(self, stack, shape, dt, name="p"):
        self.uid += 1
        return stack.enter_context(self.nc.psum_tensor(f"{name}_{self.uid}", list(shape), dt))

    def _wait(self, e, toks):
        need = {}
        for (s, v) in toks:
            if need.get(s, 0) < v:
                need[s] = v
        for s, v in need.items():
            if self.waited[e].get(s, 0) >= v:
                continue
            if e == "pe" and s is self.psem["pe"]:
                continue
            self.eng[e].wait_ge(s, v)
            self.waited[e][s] = v

    @staticmethod
    def _deps(reads, writes):
        toks = []
        for d in reads:
            if d.w is not None:
                toks.append(d.w)
        for d in writes:
            if d.w is not None:
                toks.append(d.w)
            toks.extend(d.r.items())
        return toks

    @staticmethod
    def _commit(tok, reads, writes):
        s, v = tok
        for d in reads:
            if d.r.get(s, 0) < v:
                d.r[s] = v
        for d in writes:
            d.w = tok
            d.r = {}

    def op(self, e, fn, reads=(), writes=()):
        self._wait(e, self._deps(reads, writes))
        ins = fn(self.eng[e])
        self.cnt[e] += 1
        ins.then_inc(self.psem[e], 1)
        tok = (self.psem[e], self.cnt[e])
        self._commit(tok, reads, writes)
        return tok

    def dma(self, q, out, in_, reads=(), writes=()):
        toks = self._deps(reads, writes)
        sl = self.slots[q]
        slot = sl[self.dma_idx[q] % len(sl)]
        self.dma_idx[q] += 1
        prev = self.slot_cnt[slot]
        if prev > 0:
            toks.append((slot, prev))
        self._wait(q, toks)
        self.eng[q].dma_start(out=out, in_=in_).then_inc(slot, 16)
        self.slot_cnt[slot] = prev + 16
        tok = (slot, prev + 16)
        self._commit(tok, reads, writes)
        return tok

    def collective(self, kind, op, in_ap, out_ap, tmp=None):
        self.barrier()
        TP = self.TP
        groups = [[g * TP + r for r in range(TP)] for g in range(self.NG)]
        R, C = in_ap.shape
        esz = 4
        if kind == "AllReduce":
            rc = max(1, CC_MAX_BYTES // (C * esz))
            for r0 in range(0, R, rc):
                r1 = min(R, r0 + rc)
                self.nc.gpsimd.collective_compute(kind, op, replica_groups=groups,
                                                  ins=[in_ap[r0:r1, :]], outs=[out_ap[r0:r1, :]]).then_inc(self.ccsem, 1)
                self.cc_cnt += 1
            self.barrier()
        else:
            rc = max(1, CC_MAX_BYTES // (C * esz * TP))
            chunks = []
            o = 0
            for r0 in range(0, R, rc):
                r1 = min(R, r0 + rc)
                n = r1 - r0
                self.nc.gpsimd.collective_compute(kind, op, replica_groups=groups,
                                                  ins=[in_ap[r0:r1, :]], outs=[tmp[o:o + TP * n, :]]).then_inc(self.ccsem, 1)
                self.cc_cnt += 1
                chunks.append((r0, n, o))
                o += TP * n
            self.barrier()
            ov = out_ap.rearrange("(r n) c -> r n c", r=TP)
            for (r0, n, o) in chunks:
                self.dma("sp", ov[:, r0:r0 + n, :], tmp[o:o + TP * n, :].rearrange("(r n) c -> r n c", r=TP))
            self.barrier()

    def barrier(self):
        toks = [(self.psem[e], self.cnt[e]) for e in self.eng if self.cnt[e] > 0]
        toks += [(s, c) for s, c in self.slot_cnt.items() if c > 0]
        if self.cc_cnt:
            toks.append((self.ccsem, self.cc_cnt))
        for e in self.eng:
            self._wait(e, toks)


def make_consts():
    j = np.arange(128)[:, None]
    t = np.arange(128)[None, :]
    c = {}
    c["ones"] = np.ones((128, 128), np.float32)
    c["negones"] = -np.ones((128, 128), np.float32)
    c["ident"] = np.eye(128, dtype=np.float32)
    c["U"] = (j <= t).astype(np.float32)
    c["negSLE"] = -(j >= t).astype(np.float32)
    c["negmask"] = np.where(j <= t, 0.0, -30000.0).astype(np.float32)
    t5 = np.arange(512)[None, :]
    for jj in range(4):
        c[f"amask{jj}"] = ((jj * 128 + j) < t5).astype(np.float32)
    return c


CONST_ORDER = ["ones", "negones", "ident", "U", "negSLE", "negmask", "amask0", "amask1", "amask2", "amask3"]


def consts_array():
    c = make_consts()
    return np.ascontiguousarray(np.concatenate([c[k] for k in CONST_ORDER], axis=1))


class Builder:
    def __init__(self, cfg, debug=()):
        self.cfg = cfg
        self.debug = set(debug)
        self.nc = bass.Bass("TRN2", target_bir_lowering=False)
        self.c = Ctx(self.nc, cfg.TP, cfg.B)
        self.inputs = {}
        self.scr = {}

    def inp(self, name, shape):
        t = self.nc.dram_tensor(name, list(shape), F32, kind="ExternalInput").ap()
        self.inputs[name] = tuple(shape)
        return t

    def scratch(self, name, shape, dt, force_internal=False):
        kind = "ExternalOutput" if (name in self.debug and not force_internal) else "Internal"
        t = self.nc.dram_tensor(name, list(shape), dt, kind=kind).ap()
        self.scr[name] = t
        return t

    def load_w(self, stack, W, K, N, name="w"):
        c = self.c
        kc_n = (K + 127) // 128
        wt = c.sb(stack, [128, kc_n, N], BF16, name)
        d = Dep()
        step = 8
        for k0 in range(0, kc_n, step):
            k1 = min(kc_n, k0 + step)
            c.dma("pool", wt[:, k0:k1, :], W[k0 * 128:k1 * 128, :].rearrange("(kc p) n -> p kc n", p=128),
                  writes=[d])
        return wt, d

    def load_f32(self, stack, src, shape, name="v"):
        t = self.c.sb(stack, shape, F32, name)
        d = Dep()
        self.c.dma("sp", t[:], src, writes=[d])
        return t, d

    def setup(self):
        cfg, c, nc = self.cfg, self.c, self.nc
        self.gs = ExitStack()
        ncst = len(CONST_ORDER)
        cw = 128 * 6 + 512 * 4
        cin = self.inp("consts", [128, cw])
        cf, d_cf = self.load_f32(self.gs, cin[:, :], [128, cw], "cf")
        self.d_const = Dep()
        off = {}
        o = 0
        for k in CONST_ORDER:
            w = 512 if k.startswith("amask") else 128
            off[k] = (o, w)
            o += w
        self.cf = cf
        self.onesf = cf[:, off["ones"][0]:off["ones"][0] + 128]
        self.Uf = cf[:, off["U"][0]:off["U"][0] + 128]
        self.negmask = cf[:, off["negmask"][0]:off["negmask"][0] + 128]
        cb = c.sb(self.gs, [128, cw], BF16, "cb")
        c.op("dve", lambda e: e.tensor_copy(cb[:], cf[:]), reads=[d_cf], writes=[self.d_const])
        self.d_const_f = d_cf
        self.ones_bf = cb[:, off["ones"][0]:off["ones"][0] + 128]
        self.negones_bf = cb[:, off["negones"][0]:off["negones"][0] + 128]
        self.ident_bf = cb[:, off["ident"][0]:off["ident"][0] + 128]
        self.negSLE_bf = cb[:, off["negSLE"][0]:off["negSLE"][0] + 128]
        self.amask_bf = [cb[:, off[f"amask{j}"][0]:off[f"amask{j}"][0] + 512] for j in range(4)]
        self.eps = c.sb(self.gs, [128, 1], F32, "eps")
        self.d_eps = Dep()
        c.op("pool", lambda e: e.memset(self.eps[:], EPS), writes=[self.d_eps])

    def norm_phase(self, src, g1, resid, g2, xn_out, TT=256):
        cfg, c = self.cfg, self.c
        KC, NT, D = cfg.KC, cfg.NT, cfg.D
        with ExitStack() as st:
            g1t = g2t = None
            if g1 is not None:
                g1t, d_g1 = self.load_f32(st, g1[:, :], [128, KC], "g1")
            if g2 is not None:
                g2t, d_g2 = self.load_f32(st, g2[:, :], [128, KC], "g2")
            two = resid is not None
            srct = [c.sb(st, [128, KC, TT], F32, "src") for _ in range(2)]
            d_src = [Dep(), Dep()]
            if two:
                ht = [c.sb(st, [128, KC, TT], F32, "h") for _ in range(1)]
                d_h = [Dep()]
            sq = c.sb(st, [128, KC, TT], BF16, "sq")
            d_sq = Dep()
            if xn_out is not None:
                xn = [c.sb(st, [128, KC, TT], BF16, "xn") for _ in range(2)]
                d_xn = [Dep(), Dep()]
            rs = [c.sb(st, [128, TT], F32, "rs") for _ in range(2)]
            d_rs = [Dep(), Dep()]
            pss = [c.ps(st, [128, 512], F32, "nps") for _ in range(2)]
            d_ps = [Dep(), Dep()]
            sv = src.rearrange("(kc p) t -> p kc t", p=128)
            if two:
                hv = resid.rearrange("(kc p) t -> p kc t", p=128)
            if xn_out is not None:
                xv = xn_out.rearrange("(kc p) t -> p kc t", p=128)
            nt = NT // TT
            irs = 0

            def rstd_of(tile, d_tile):
                nonlocal irs
                k = irs % 2
                irs += 1
                c.op("act", lambda e: e.activation(out=sq[:], in_=tile[:], func=AF.Square),
                     reads=[d_tile], writes=[d_sq])
                for kc in range(KC):
                    c.op("pe", lambda e: e.matmul(pss[k][:, :TT], self.ones_bf, sq[:, kc, :],
                                                  start=(kc == 0), stop=(kc == KC - 1)),
                         reads=[self.d_const, d_sq], writes=[d_ps[k]])
                c.op("act", lambda e: e.activation(out=rs[k][:], in_=pss[k][:, :TT], func=AF.Sqrt,
                                                   bias=self.eps[:, 0:1], scale=1.0 / D),
                     reads=[d_ps[k], self.d_eps], writes=[d_rs[k]])
                c.op("dve", lambda e: e.reciprocal(rs[k][:], rs[k][:]), reads=[d_rs[k]], writes=[d_rs[k]])
                return rs[k], d_rs[k]

            c.dma("sp", srct[0][:], sv[:, :, 0:TT], writes=[d_src[0]])
            for it in range(nt):
                b = it % 2
                t0 = it * TT
                if two:
                    c.dma("sp", ht[0][:], hv[:, :, t0:t0 + TT], writes=[d_h[0]])
                if it + 1 < nt:
                    c.dma("sp", srct[1 - b][:], sv[:, :, t0 + TT:t0 + 2 * TT], writes=[d_src[1 - b]])
                cur, d_cur = srct[b], d_src[b]
                if two:
                    r1, d_r1 = rstd_of(cur, d_cur)
                    for kc in range(KC):
                        c.op("dve", lambda e: e.scalar_tensor_tensor(cur[:, kc, :], cur[:, kc, :], g1t[:, kc:kc + 1],
                                                                     r1[:], ALU.mult, ALU.mult),
                             reads=[d_g1, d_r1], writes=[d_cur])
                    c.op("pool", lambda e: e.tensor_tensor(ht[0][:], ht[0][:], cur[:], ALU.add),
                         reads=[d_cur], writes=[d_h[0]])
                    c.dma("pool", hv[:, :, t0:t0 + TT], ht[0][:], reads=[d_h[0]])
                    cur, d_cur = ht[0], d_h[0]
                if xn_out is not None:
                    r2, d_r2 = rstd_of(cur, d_cur)
                    for kc in range(KC):
                        c.op("dve", lambda e: e.scalar_tensor_tensor(xn[b][:, kc, :], cur[:, kc, :], g2t[:, kc:kc + 1],
                                                                     r2[:], ALU.mult, ALU.mult),
                             reads=[d_cur, d_g2, d_r2], writes=[d_xn[b]])
                    c.dma("pool", xv[:, :, t0:t0 + TT], xn[b][:], reads=[d_xn[b]])
        c.barrier()

    def lin_fm(self, xT, K, W, nch_list, epilogue, TT=512, wname="w"):
        cfg, c = self.cfg, self.c
        NT = cfg.NT
        kc_n = (K + 127) // 128
        maxc = max(128, (65536 // (kc_n * 2)) // 128 * 128)
        if len(nch_list) * 128 > maxc:
            per = maxc // 128
            for i in range(0, len(nch_list), per):
                self.lin_fm(xT, K, W, nch_list[i:i + per], epilogue, TT, wname)
            return
        gc0 = nch_list[0][0]
        gc1 = nch_list[-1][0] + nch_list[-1][1]
        with ExitStack() as st:
            wt, d_w = self.load_w(st, W[:, gc0:gc1], K, gc1 - gc0, wname)
            xt = [c.sb(st, [128, kc_n, TT], BF16, "xt") for _ in range(2)]
            d_xt = [Dep(), Dep()]
            pss = [c.ps(st, [128, 512], F32, "lps") for _ in range(4)]
            d_ps = [Dep() for _ in range(4)]
            xv = xT.rearrange("(kc p) t -> p kc t", p=128)
            nt = NT // TT
            ip = 0
            c.dma("sp", xt[0][:], xv[:, :, 0:TT], writes=[d_xt[0]])
            for it in range(nt):
                b = it % 2
                t0 = it * TT
                if it + 1 < nt:
                    c.dma("sp", xt[1 - b][:], xv[:, :, t0 + TT:t0 + 2 * TT], writes=[d_xt[1 - b]])
                for idx, (c0, wd) in enumerate(nch_list):
                    p = ip % 4
                    ip += 1
                    for kc in range(kc_n):
                        c.op("pe", lambda e: e.matmul(pss[p][:wd, :TT], wt[:, kc, c0 - gc0:c0 - gc0 + wd], xt[b][:, kc, :],
                                                      start=(kc == 0), stop=(kc == kc_n - 1)),
                             reads=[d_w, d_xt[b]], writes=[d_ps[p]])
                    epilogue(idx, c0, wd, pss[p], d_ps[p], t0, TT)
        c.barrier()

    def lin_fm_ar(self, xT, K, W, part, red, TT=512):
        cfg, c = self.cfg, self.c
        NT, D, TP = cfg.NT, cfg.D, cfg.TP
        kc_n = (K + 127) // 128
        AR_R = max(128, min(D, (CC_MAX_BYTES // (NT * 4)) // 128 * 128))
        GR = min(D, max(AR_R, 1024))
        ngr = D // GR
        groups = [[g * TP + r for r in range(TP)] for g in range(cfg.B)]
        with ExitStack() as st:
            wts = [c.sb(st, [128, kc_n, GR], BF16, "wr") for _ in range(2)]
            d_w = [Dep(), Dep()]
            xt = [c.sb(st, [128, kc_n, TT], BF16, "xt") for _ in range(2)]
            d_xt = [Dep(), Dep()]
            pss = [c.ps(st, [128, 512], F32, "lps") for _ in range(4)]
            d_ps = [Dep() for _ in range(4)]
            ot = [c.sb(st, [128, 512], F32, "ot") for _ in range(3)]
            d_ot = [Dep() for _ in range(3)]
            xv = xT.rearrange("(kc p) t -> p kc t", p=128)
            nt = NT // TT
            seq = [(g, it) for g in range(ngr) for it in range(nt)]

            def loadw(g):
                for k0 in range(0, kc_n, 8):
                    k1 = min(kc_n, k0 + 8)
                    c.dma("pool", wts[g % 2][:, k0:k1, :],
                          W[k0 * 128:k1 * 128, g * GR:(g + 1) * GR].rearrange("(kc p) n -> p kc n", p=128), writes=[d_w[g % 2]])

            def loadx(i):
                g, it = seq[i]
                c.dma("sp", xt[i % 2][:], xv[:, :, it * TT:(it + 1) * TT], writes=[d_xt[i % 2]])
            loadw(0)
            loadx(0)
            ip = io = 0
            for g in range(ngr):
                if g + 1 < ngr:
                    loadw(g + 1)
                wt = wts[g % 2]
                toks = []
                for it in range(nt):
                    i = g * nt + it
                    b = i % 2
                    t0 = it * TT
                    if i + 1 < len(seq):
                        loadx(i + 1)
                    for j in range(GR // 128):
                        p = ip % 4
                        ip += 1
                        k = io % 3
                        io += 1
                        for kc in range(kc_n):
                            c.op("pe", lambda e: e.matmul(pss[p][:, :TT], wt[:, kc, j * 128:(j + 1) * 128], xt[b][:, kc, :],
                                                          start=(kc == 0), stop=(kc == kc_n - 1)),
                                 reads=[d_w[g % 2], d_xt[b]], writes=[d_ps[p]])
                        c.op("act", lambda e: e.copy(out=ot[k][:, :TT], in_=pss[p][:, :TT]), reads=[d_ps[p]], writes=[d_ot[k]])
                        r0 = g * GR + j * 128
                        toks.append(c.dma("pool", part[r0:r0 + 128, t0:t0 + TT], ot[k][:, :TT], reads=[d_ot[k]]))
                if TP > 1:
                    c._wait("pool", toks)
                    for a0 in range(g * GR, (g + 1) * GR, AR_R):
                        self.nc.gpsimd.collective_compute("AllReduce", ALU.add, replica_groups=groups,
                                                          ins=[part[a0:a0 + AR_R, :]],
                                                          outs=[red[a0:a0 + AR_R, :]]).then_inc(c.ccsem, 1)
                        c.cc_cnt += 1
        c.barrier()

    def store_epilogue(self, st, out, dt, scale=None, row_off=0, n=3):
        c = self.c
        ot = [c.sb(st, [128, 512], dt, "ot") for _ in range(n)]
        d_ot = [Dep() for _ in range(n)]
        state = {"i": 0}

        def ep(idx, c0, wd, ps, d_ps, t0, TT):
            k = state["i"] % n
            state["i"] += 1
            if scale is None:
                c.op("act", lambda e: e.copy(out=ot[k][:wd, :TT], in_=ps[:wd, :TT]), reads=[d_ps], writes=[d_ot[k]])
            else:
                c.op("act", lambda e: e.mul(out=ot[k][:wd, :TT], in_=ps[:wd, :TT], mul=scale(c0)),
                     reads=[d_ps], writes=[d_ot[k]])
            c.dma("pool", out[row_off + c0:row_off + c0 + wd, t0:t0 + TT], ot[k][:wd, :TT], reads=[d_ot[k]])
        return ep

    def lin_tm_A(self, xT, W, gb, tm_out, gates_out, TT=512):
        cfg, c = self.cfg, self.c
        KC, NT, DV, DK = cfg.KC, cfg.NT, cfg.DV, cfg.DK
        N = cfg.TMN
        groups = []
        o = 0
        while o < N:
            w = min(512, N - o)
            groups.append((o, w))
            o += w
        with ExitStack() as st:
            wt, d_w = self.load_w(st, W, cfg.D, N, "wtm")
            gbt, d_gb = self.load_f32(st, gb[:, :], [128, 2], "gb")
            gbs = c.sb(st, [128, 2], F32, "gbs")
            d_gbs = Dep()
            c.op("dve", lambda e: e.tensor_scalar(gbs[:], gbt[:], 1.0 / CAP, None, ALU.mult), reads=[d_gb], writes=[d_gbs])
            xt = [c.sb(st, [128, KC, TT], BF16, "xt") for _ in range(2)]
            d_xt = [Dep(), Dep()]
            pss = [c.ps(st, [128, 512], F32, "tps") for _ in range(4)]
            d_ps = [Dep() for _ in range(4)]
            ot = [c.sb(st, [128, cfg.TMW], BF16, "tmo") for _ in range(2)]
            d_ot = [Dep(), Dep()]
            gt = [c.sb(st, [128, 4], F32, "gto") for _ in range(2)]
            d_gt = [Dep(), Dep()]
            xv = xT.rearrange("(kc p) t -> p kc t", p=128)
            nt = NT // TT
            ip = 0
            io = 0
            c.dma("sp", xt[0][:], xv[:, :, 0:TT], writes=[d_xt[0]])
            for it in range(nt):
                b = it % 2
                t0 = it * TT
                if it + 1 < nt:
                    c.dma("sp", xt[1 - b][:], xv[:, :, t0 + TT:t0 + 2 * TT], writes=[d_xt[1 - b]])
                for tb in range(TT // 128):
                    k = io % 2
                    io += 1
                    for (c0, wd) in groups:
                        p = ip % 4
                        ip += 1
                        for kc in range(KC):
                            c.op("pe", lambda e: e.matmul(pss[p][:, :wd], xt[b][:, kc, tb * 128:(tb + 1) * 128],
                                                          wt[:, kc, c0:c0 + wd], start=(kc == 0), stop=(kc == KC - 1)),
                                 reads=[d_w, d_xt[b]], writes=[d_ps[p]])
                        for (s0, s1, kind) in ((0, DV, "v"), (DV, 2 * DV, "og"), (2 * DV, 2 * DV + DK, "k"),
                                               (cfg.TMW, cfg.TMW + 2, "g")):
                            a0, a1 = max(s0, c0), min(s1, c0 + wd)
                            if a0 >= a1:
                                continue
                            src = pss[p][:, a0 - c0:a1 - c0]
                            if kind in ("v", "k"):
                                c.op("act", lambda e: e.copy(out=ot[k][:, a0:a1], in_=src), reads=[d_ps[p]], writes=[d_ot[k]])
                            elif kind == "og":
                                c.op("act", lambda e: e.activation(out=ot[k][:, a0:a1], in_=src, func=AF.Sigmoid),
                                     reads=[d_ps[p]], writes=[d_ot[k]])
                            else:
                                c.op("act", lambda e: e.activation(out=gt[k][:, 0:1], in_=src[:, 0:1], func=AF.Tanh,
                                                                   bias=gbs[:, 0:1], scale=1.0 / CAP),
                                     reads=[d_ps[p], d_gbs], writes=[d_gt[k]])
                                c.op("act", lambda e: e.activation(out=gt[k][:, 1:2], in_=src[:, 1:2], func=AF.Tanh,
                                                                   bias=gbs[:, 1:2], scale=1.0 / CAP),
                                     reads=[d_ps[p], d_gbs], writes=[d_gt[k]])
                                c.op("act", lambda e: e.activation(out=gt[k][:, 2:3], in_=gt[k][:, 1:2], func=AF.Exp, scale=-CAP),
                                     reads=[d_gt[k]], writes=[d_gt[k]])
                                c.op("act", lambda e: e.activation(out=gt[k][:, 3:4], in_=gt[k][:, 2:3], func=AF.Ln, bias=1.0),
                                     reads=[d_gt[k]], writes=[d_gt[k]])
                                c.op("dve", lambda e: e.tensor_scalar(gt[k][:, 0:1], gt[k][:, 0:1], CAP, None, ALU.mult),
                                     reads=[d_gt[k]], writes=[d_gt[k]])
                                c.op("dve", lambda e: e.tensor_scalar(gt[k][:, 1:2], gt[k][:, 3:4], -1.0, None, ALU.mult),
                                     reads=[d_gt[k]], writes=[d_gt[k]])
                    r0 = t0 + tb * 128
                    c.dma("pool", tm_out[r0:r0 + 128, :], ot[k][:], reads=[d_ot[k]])
                    c.dma("pool", gates_out[r0:r0 + 128, :], gt[k][:, 0:2], reads=[d_gt[k]])
        c.barrier()

    def lin_tm_plain(self, xT, W, N, out, TT=512):
        cfg, c = self.cfg, self.c
        KC, NT = cfg.KC, cfg.NT
        if N > 1024:
            for o in range(0, N, 1024):
                w = min(1024, N - o)
                self.lin_tm_plain(xT, W[:, o:o + w], w, out[:, o:o + w], TT)
            return
        groups = [(o, min(512, N - o)) for o in range(0, N, 512)]
        with ExitStack() as st:
            wt, d_w = self.load_w(st, W, cfg.D, N, "wtp")
            xt = [c.sb(st, [128, KC, TT], BF16, "xt") for _ in range(2)]
            d_xt = [Dep(), Dep()]
            pss = [c.ps(st, [128, 512], F32, "tps") for _ in range(4)]
            d_ps = [Dep() for _ in range(4)]
            ot = [c.sb(st, [128, N], BF16, "tmo") for _ in range(2)]
            d_ot = [Dep(), Dep()]
            xv = xT.rearrange("(kc p) t -> p kc t", p=128)
            nt = NT // TT
            ip = io = 0
            c.dma("sp", xt[0][:], xv[:, :, 0:TT], writes=[d_xt[0]])
            for it in range(nt):
                b = it % 2
                t0 = it * TT
                if it + 1 < nt:
                    c.dma("sp", xt[1 - b][:], xv[:, :, t0 + TT:t0 + 2 * TT], writes=[d_xt[1 - b]])
                for tb in range(TT // 128):
                    k = io % 2
                    io += 1
                    for (c0, wd) in groups:
                        p = ip % 4
                        ip += 1
                        for kc in range(KC):
                            c.op("pe", lambda e: e.matmul(pss[p][:, :wd], xt[b][:, kc, tb * 128:(tb + 1) * 128],
                                                          wt[:, kc, c0:c0 + wd], start=(kc == 0), stop=(kc == KC - 1)),
                                 reads=[d_w, d_xt[b]], writes=[d_ps[p]])
                        c.op("act", lambda e: e.copy(out=ot[k][:, c0:c0 + wd], in_=pss[p][:, :wd]),
                             reads=[d_ps[p]], writes=[d_ot[k]])
                    r0 = t0 + tb * 128
                    c.dma("pool", out[r0:r0 + 128, :], ot[k][:], reads=[d_ot[k]])
        c.barrier()

    def mlstm_phase(self, qkT, tm, gates, hn, hsT):
        cfg, c = self.cfg, self.c
        DK, DV, DKC, DVC, S, B = cfg.DK, cfg.DV, cfg.DKC, cfg.DVC, cfg.S, 1
        L = 128
        NCH = S // L
        with ExitStack() as st:
            hnB, d_hn = self.load_f32(st, hn[:, :], [128, DV], "hnB")
            Cs = c.sb(st, [128, DKC, DV], F32, "C")
            Cb = c.sb(st, [128, DKC, DV], BF16, "Cb")
            ns = c.sb(st, [128, DKC, 1], F32, "n")
            nb = c.sb(st, [128, DKC, 1], BF16, "nb")
            d_C, d_Cb, d_n, d_nb = Dep(), Dep(), Dep(), Dep()
            NB = 2
            qk = [c.sb(st, [128, 2, DKC, L], BF16, "qk") for _ in range(NB)]
            tmt = [c.sb(st, [128, cfg.TMW], BF16, "tm") for _ in range(NB)]
            gt = [c.sb(st, [128, 2], F32, "g") for _ in range(NB)]
            d_qk = [Dep() for _ in range(NB)]
            d_tm = [Dep() for _ in range(NB)]
            d_g = [Dep() for _ in range(NB)]

            def T(shape, dt, name):
                return [c.sb(st, shape, dt, name) for _ in range(NB)], [Dep() for _ in range(NB)]
            lfB, d_lfB = T([128, 128], F32, "lfB")
            sm, d_sm = T([128, 16], F32, "sm")
            eB, d_eB = T([128, 128], F32, "eB")
            tmpm, d_tmpm = T([128, 128], F32, "tmpm")
            WT, d_WT = T([128, 128], F32, "WT")
            PT, d_PT = T([128, 128], BF16, "PT")
            qa, d_qa = T([128, DKC, 128], BF16, "qa")
            junk, d_junk = T([128, DV], F32, "junk")
            o1, d_o1 = T([128, DV], F32, "o1")
            o2, d_o2 = T([128, DV], BF16, "o2")
            hso, d_hso = T([128, DVC, 128], BF16, "hso")
            kw, d_kw = T([128, DK], BF16, "kw")
            ps_b = [c.ps(st, [128, 512], F32, "psb") for _ in range(2)]
            d_psb = [[Dep() for _ in range(4)] for _ in range(2)]
            ps_st = c.ps(st, [128, 512], F32, "psst")
            d_psst = Dep()
            ps_num = [c.ps(st, [128, 512], F32, "psn") for _ in range(2)]
            d_psn = [Dep(), Dep()]
            ps_tp = c.ps(st, [128, 1024], BF16, "pstp")
            d_pstp = Dep()
            ps_dc = [c.ps(st, [128, 512], F32, "psdc") for _ in range(2)]
            d_psdc = [Dep(), Dep()]
            qv = qkT.rearrange("(a j p) t -> p a j t", a=2, p=128)
            hv = hsT.rearrange("(j p) t -> p j t", p=128)
            dcst = [self.d_const, self.d_const_f]

            def load(gi):
                bb, ch = divmod(gi, NCH)
                r0 = bb * S + ch * L
                k = gi % NB
                c.dma("sp", qk[k][:], qv[:, :, :, r0:r0 + L], writes=[d_qk[k]])
                c.dma("sp", tmt[k][:], tm[r0:r0 + L, :], writes=[d_tm[k]])
                c.dma("sp", gt[k][:], gates[r0:r0 + L, :], writes=[d_g[k]])

            total = B * NCH
            load(0)
            for gi in range(total):
                bb, ch = divmod(gi, NCH)
                r0 = bb * S + ch * L
                k = gi % NB
                if gi + 1 < total:
                    load(gi + 1)
                if ch == 0:
                    c.op("pool", lambda e: e.memset(Cs[:], 0.0), writes=[d_C])
                    c.op("pool", lambda e: e.memset(Cb[:], 0.0), writes=[d_Cb])
                    c.op("pool", lambda e: e.memset(ns[:], 0.0), writes=[d_n])
                    c.op("pool", lambda e: e.memset(nb[:], 0.0), writes=[d_nb])
                pb = ps_b[k]
                dpb = d_psb[k]
                qT = qk[k][:, 0]
                kT = qk[k][:, 1]
                v = tmt[k][:, 0:DV]
                ogs = tmt[k][:, DV:2 * DV]
                kk = tmt[k][:, 2 * DV:2 * DV + DK]
                li = gt[k][:, 0:1]
                lf = gt[k][:, 1:2]
                s = sm[k]
                d_s = d_sm[k]
                c.op("dve", lambda e: e.tensor_scalar(lfB[k][:], self.onesf, lf, None, ALU.mult),
                     reads=[d_g[k]] + dcst, writes=[d_lfB[k]])
                c.op("pe", lambda e: e.matmul(pb[:, 0:128], lfB[k][:], self.Uf, start=True, stop=True),
                     reads=[d_lfB[k]] + dcst, writes=[dpb[0]])
                c.op("pe", lambda e: e.matmul(pb[:, 128:129], self.Uf, lf, start=True, stop=True),
                     reads=[d_g[k]] + dcst, writes=[dpb[1]])
                c.op("dve", lambda e: e.tensor_tensor(s[:, 0:1], li, pb[:, 128:129], ALU.subtract),
                     reads=[d_g[k], dpb[1]], writes=[d_s])
                c.op("act", lambda e: e.copy(out=s[:, 1:2], in_=pb[:, 127:128]), reads=[dpb[0]], writes=[d_s])
                c.op("act", lambda e: e.activation(out=s[:, 2:3], in_=pb[:, 127:128], func=AF.Exp), reads=[dpb[0]], writes=[d_s])
                c.op("act", lambda e: e.activation(out=s[:, 3:4], in_=s[:, 0:1], func=AF.Exp, bias=s[:, 1:2]),
                     reads=[d_s], writes=[d_s])
                c.op("act", lambda e: e.activation(out=eB[k][:], in_=pb[:, 0:128], func=AF.Exp), reads=[dpb[0]], writes=[d_eB[k]])
                c.op("dve", lambda e: e.tensor_tensor(tmpm[k][:], pb[:, 0:128], self.negmask, ALU.add),
                     reads=[dpb[0]] + dcst, writes=[d_tmpm[k]])
                c.op("act", lambda e: e.activation(out=WT[k][:], in_=tmpm[k][:], func=AF.Exp, bias=s[:, 0:1]),
                     reads=[d_tmpm[k], d_s], writes=[d_WT[k]])
                for j in range(DKC):
                    c.op("pe", lambda e: e.matmul(ps_st[:, 0:128], kT[:, j, :], qT[:, j, :], start=(j == 0), stop=(j == DKC - 1)),
                         reads=[d_qk[k]], writes=[d_psst])
                c.op("dve", lambda e: e.tensor_tensor(PT[k][:], ps_st[:, 0:128], WT[k][:], ALU.mult),
                     reads=[d_psst, d_WT[k]], writes=[d_PT[k]])
                for j in range(DKC):
                    c.op("pool", lambda e: e.tensor_tensor(qa[k][:, j, :], qT[:, j, :], eB[k][:], ALU.mult),
                         reads=[d_qk[k], d_eB[k]], writes=[d_qa[k]])
                pn = ps_num[k]
                for j in range(DKC):
                    c.op("pe", lambda e: e.matmul(pn[:, 0:DV], qa[k][:, j, :], Cb[:, j, :], start=(j == 0), stop=False),
                         reads=[d_qa[k], d_Cb], writes=[d_psn[k]])
                c.op("pe", lambda e: e.matmul(pn[:, 0:DV], PT[k][:], v, start=False, stop=True),
                     reads=[d_PT[k], d_tm[k]], writes=[d_psn[k]])
                for j in range(DKC):
                    c.op("pe", lambda e: e.matmul(pb[:, 132:133], qa[k][:, j, :], nb[:, j, :], start=(j == 0), stop=False),
                         reads=[d_qa[k], d_nb], writes=[dpb[2]])
                c.op("pe", lambda e: e.matmul(pb[:, 132:133], PT[k][:], self.ones_bf[:, 0:1], start=False, stop=True),
                     reads=[d_PT[k]] + dcst, writes=[dpb[2]])
                c.op("act", lambda e: e.activation(out=s[:, 4:5], in_=pb[:, 132:133], func=AF.Abs),
                     reads=[dpb[2]], writes=[d_s])
                c.op("dve", lambda e: e.tensor_scalar(s[:, 4:5], s[:, 4:5], 1.0, None, ALU.max),
                     reads=[d_s], writes=[d_s])
                c.op("dve", lambda e: e.reciprocal(s[:, 5:6], s[:, 4:5]), reads=[d_s], writes=[d_s])
                c.op("pool", lambda e: e.memset(s[:, 6:7], 0.0), writes=[d_s])
                c.op("act", lambda e: e.activation(out=junk[k][:], in_=pn[:, 0:DV], func=AF.Square, scale=s[:, 5:6],
                                                   accum_out=s[:, 6:7]),
                     reads=[d_psn[k], d_s], writes=[d_junk[k], d_s])
                c.op("act", lambda e: e.activation(out=s[:, 7:8], in_=s[:, 6:7], func=AF.Sqrt, bias=self.eps[:, 0:1], scale=1.0 / DV),
                     reads=[d_s, self.d_eps], writes=[d_s])
                c.op("dve", lambda e: e.reciprocal(s[:, 7:8], s[:, 7:8]), reads=[d_s], writes=[d_s])
                c.op("dve", lambda e: e.tensor_tensor(s[:, 8:9], s[:, 7:8], s[:, 5:6], ALU.mult), reads=[d_s], writes=[d_s])
                c.op("dve", lambda e: e.scalar_tensor_tensor(o1[k][:], pn[:, 0:DV], s[:, 8:9], hnB[:], ALU.mult, ALU.mult),
                     reads=[d_psn[k], d_s, d_hn], writes=[d_o1[k]])
                c.op("pool", lambda e: e.tensor_tensor(o2[k][:], o1[k][:], ogs, ALU.mult),
                     reads=[d_o1[k], d_tm[k]], writes=[d_o2[k]])
                for jj in range(DVC):
                    c.op("pe", lambda e: e.transpose(ps_tp[:, jj * 128:(jj + 1) * 128], o2[k][:, jj * 128:(jj + 1) * 128], self.ident_bf),
                         reads=[d_o2[k]] + dcst, writes=[d_pstp])
                c.op("act", lambda e: e.copy(out=hso[k][:], in_=ps_tp[:, 0:DVC * 128].rearrange("p (j t) -> p j t", j=DVC)),
                     reads=[d_pstp], writes=[d_hso[k]])
                c.dma("pool", hv[:, :, r0:r0 + L], hso[k][:], reads=[d_hso[k]])
                c.op("dve", lambda e: e.tensor_scalar(kw[k][:], kk, s[:, 3:4], None, ALU.mult),
                     reads=[d_tm[k], d_s], writes=[d_kw[k]])
                for j in range(DKC):
                    pd = ps_dc[j % 2]
                    c.op("pe", lambda e: e.matmul(pd[:, 0:DV], kw[k][:, j * 128:(j + 1) * 128], v, start=True, stop=True),
                         reads=[d_kw[k], d_tm[k]], writes=[d_psdc[j % 2]])
                    c.op("pe", lambda e: e.matmul(pb[:, 136 + j:137 + j], kw[k][:, j * 128:(j + 1) * 128], self.ones_bf[:, 0:1],
                                                  start=True, stop=True),
                         reads=[d_kw[k]] + dcst, writes=[dpb[3]])
                    c.op("dve", lambda e: e.scalar_tensor_tensor(Cs[:, j, :], Cs[:, j, :], s[:, 2:3], pd[:, 0:DV], ALU.mult, ALU.add),
                         reads=[d_s, d_psdc[j % 2]], writes=[d_C])
                    c.op("dve", lambda e: e.scalar_tensor_tensor(ns[:, j, :], ns[:, j, :], s[:, 2:3], pb[:, 136 + j:137 + j], ALU.mult, ALU.add),
                         reads=[d_s, dpb[3]], writes=[d_n])
                c.op("act", lambda e: e.copy(out=Cb[:], in_=Cs[:]), reads=[d_C], writes=[d_Cb])
                c.op("act", lambda e: e.copy(out=nb[:], in_=ns[:]), reads=[d_n], writes=[d_nb])
        c.barrier()

    def attn_phase(self, qT, kT, vtm, attT):
        cfg, c = self.cfg, self.c
        S, B, HPC = cfg.S, cfg.B, cfg.HPC
        NB = S // 128
        NSB = S // 512
        with ExitStack() as st:
            qt = [c.sb(st, [128, S], BF16, "aq") for _ in range(2)]
            kt = [c.sb(st, [128, S], BF16, "ak") for _ in range(2)]
            vt = [c.sb(st, [128, NB, 128], BF16, "av") for _ in range(2)]
            d_q = [Dep(), Dep()]
            d_k = [Dep(), Dep()]
            d_v = [Dep(), Dep()]

            def T(n, shape, dt, name):
                return [c.sb(st, shape, dt, name) for _ in range(n)], [Dep() for _ in range(n)]
            ee, d_ee = T(2, [128, 512], F32, "ee")
            Lp, d_Lp = T(2, [128, 512], BF16, "Lp")
            Lm, d_Lm = T(2, [128, 512], BF16, "Lm")
            arg, d_arg = T(2, [128, 512], F32, "arg")
            R, d_R = T(2, [128, 512], F32, "R")
            A, d_A = T(2, [128, 512], BF16, "A")
            Am, d_Am = T(2, [128, 512], BF16, "Am")
            osb, d_osb = T(2, [128, 512], BF16, "osb")
            ps_z = [c.ps(st, [128, 512], F32, "psz") for _ in range(2)]
            d_psz = [Dep(), Dep()]
            ps_1 = [c.ps(st, [128, 512], F32, "ps1") for _ in range(2)]
            d_ps1 = [Dep(), Dep()]
            ps_2 = [c.ps(st, [128, 512], F32, "ps2") for _ in range(2)]
            d_ps2 = [Dep(), Dep()]
            ps_o = [c.ps(st, [128, 512], F32, "pso") for _ in range(2)]
            d_pso = [Dep(), Dep()]
            dcst = [self.d_const]
            units = [(0, hh) for hh in range(HPC)]

            def load(ui):
                bb, hh = units[ui]
                k = ui % 2
                c.dma("sp", qt[k][:], qT[hh * 128:(hh + 1) * 128, bb * S:(bb + 1) * S], writes=[d_q[k]])
                c.dma("sp", kt[k][:], kT[hh * 128:(hh + 1) * 128, bb * S:(bb + 1) * S], writes=[d_k[k]])
                c.dma("sp", vt[k][:], vtm[bb * S:(bb + 1) * S, hh * 128:(hh + 1) * 128].rearrange("(j p) d -> p j d", p=128),
                      writes=[d_v[k]])
            tiles = []
            gi = 0
            for ui, (bb, hh) in enumerate(units):
                for I in range(NSB):
                    Jtop = 4 * I + 3
                    for J in range(Jtop, -1, -1):
                        tiles.append(dict(ui=ui, u=ui % 2, bb=bb, hh=hh, I=I, J=J, first=(J == Jtop), last=(J == 0),
                                          diag=(J >= 4 * I), jj=J - 4 * I, o=gi % 2))
                    gi += 1
            NBUF = 3
            ee3, d_ee3 = T(NBUF, [128, 512], F32, "ee3")
            Lp3, d_Lp3 = T(NBUF, [128, 512], BF16, "Lp3")
            Lm3, d_Lm3 = T(NBUF, [128, 512], BF16, "Lm3")
            A3, d_A3 = T(NBUF, [128, 512], BF16, "A3")
            Am3, d_Am3 = T(NBUF, [128, 512], BF16, "Am3")
            loaded = set()
            rstate = {"prev": None}

            def qs_ks(t):
                return (qt[t["u"]][:, t["I"] * 512:(t["I"] + 1) * 512], kt[t["u"]][:, t["J"] * 128:(t["J"] + 1) * 128])

            def stageA(n):
                t = tiles[n]
                if t["ui"] not in loaded:
                    loaded.add(t["ui"])
                    if t["ui"] == 0:
                        load(0)
                    if t["ui"] + 1 < len(units):
                        load(t["ui"] + 1)
                u, x, m = t["u"], n % 2, n % NBUF
                qs, ks = qs_ks(t)
                c.op("pe", lambda e: e.matmul(ps_z[x][:], ks, qs, start=True, stop=True),
                     reads=[d_q[u], d_k[u]], writes=[d_psz[x]])
                c.op("act", lambda e: e.activation(out=ee3[m][:], in_=ps_z[x][:], func=AF.Exp),
                     reads=[d_psz[x]], writes=[d_ee3[m]])
                c.op("act", lambda e: e.activation(out=Lp3[m][:], in_=ee3[m][:], func=AF.Ln, bias=1.0),
                     reads=[d_ee3[m]], writes=[d_Lp3[m]])
                if t["diag"]:
                    c.op("pool", lambda e: e.tensor_tensor(Lm3[m][:], Lp3[m][:], self.amask_bf[t["jj"]], ALU.mult),
                         reads=[d_Lp3[m]] + dcst, writes=[d_Lm3[m]])

            def stageB(n):
                t = tiles[n]
                u, x, m = t["u"], n % 2, n % NBUF
                qs, ks = qs_ks(t)
                lm, d_lm = (Lm3[m], d_Lm3[m]) if t["diag"] else (Lp3[m], d_Lp3[m])
                c.op("pe", lambda e: e.matmul(ps_1[x][:], ks, qs, start=True, stop=False),
                     reads=[d_q[u], d_k[u]], writes=[d_ps1[x]])
                c.op("pe", lambda e: e.matmul(ps_1[x][:], self.negSLE_bf, lm[:], start=False, stop=True),
                     reads=[d_lm] + dcst, writes=[d_ps1[x]])
                if not t["last"]:
                    c.op("pe", lambda e: e.matmul(ps_2[x][:], self.negones_bf, lm[:], start=True, stop=True),
                         reads=[d_lm] + dcst, writes=[d_ps2[x]])
                if t["first"]:
                    c.op("act", lambda e: e.activation(out=A3[m][:], in_=ps_1[x][:], func=AF.Exp),
                         reads=[d_ps1[x]], writes=[d_A3[m]])
                    rn = 0
                    if not t["last"]:
                        c.op("dve", lambda e: e.tensor_copy(R[rn][:], ps_2[x][:]), reads=[d_ps2[x]], writes=[d_R[rn]])
                else:
                    rp = rstate["prev"]
                    c.op("dve", lambda e: e.tensor_tensor(arg[x][:], ps_1[x][:], R[rp][:], ALU.add),
                         reads=[d_ps1[x], d_R[rp]], writes=[d_arg[x]])
                    c.op("act", lambda e: e.activation(out=A3[m][:], in_=arg[x][:], func=AF.Exp),
                         reads=[d_arg[x]], writes=[d_A3[m]])
                    rn = 1 - rp
                    if not t["last"]:
                        c.op("dve", lambda e: e.tensor_tensor(R[rn][:], ps_2[x][:], R[rp][:], ALU.add),
                             reads=[d_ps2[x], d_R[rp]], writes=[d_R[rn]])
                rstate["prev"] = rn
                if t["diag"]:
                    c.op("pool", lambda e: e.tensor_tensor(Am3[m][:], A3[m][:], self.amask_bf[t["jj"]], ALU.mult),
                         reads=[d_A3[m]] + dcst, writes=[d_Am3[m]])

            def stageC(n):
                t = tiles[n]
                u, m, o = t["u"], n % NBUF, t["o"]
                am, d_am = (Am3[m], d_Am3[m]) if t["diag"] else (A3[m], d_A3[m])
                c.op("pe", lambda e: e.matmul(ps_o[o][:], vt[u][:, t["J"], :], am[:], start=t["first"], stop=t["last"]),
                     reads=[d_v[u], d_am], writes=[d_pso[o]])
                if t["last"]:
                    bb, hh, I = t["bb"], t["hh"], t["I"]
                    c.op("act", lambda e: e.copy(out=osb[o][:], in_=ps_o[o][:]), reads=[d_pso[o]], writes=[d_osb[o]])
                    c.dma("pool", attT[hh * 128:(hh + 1) * 128, bb * S + I * 512:bb * S + (I + 1) * 512], osb[o][:],
                          reads=[d_osb[o]])

            NTL = len(tiles)
            for n in range(NTL + 2):
                if n < NTL:
                    stageA(n)
                if 0 <= n - 1 < NTL:
                    stageB(n - 1)
                if 0 <= n - 2 < NTL:
                    stageC(n - 2)
        c.barrier()

    def ffn_up_phase(self, xT, Wup, cw, aT, TT=512, G=3):
        cfg, c = self.cfg, self.c
        KC, NT, S, FCH = cfg.KC, cfg.NT, cfg.S, cfg.FCH
        cwt_stack = ExitStack()
        cwt, d_cw = self.load_f32(cwt_stack, cw[:, :], [128, FCH * 8], "cw")
        xv = xT.rearrange("(kc p) t -> p kc t", p=128)
        for g0 in range(0, FCH, G):
            g1 = min(FCH, g0 + G)
            ng = g1 - g0
            with ExitStack() as st:
                wt, d_w = self.load_w(st, Wup[:, g0 * 256:g1 * 256], cfg.D, ng * 256, "wup")
                xt = [c.sb(st, [128, KC, TT], BF16, "xt") for _ in range(2)]
                d_xt = [Dep(), Dep()]
                pss = [c.ps(st, [128, 512], F32, "fps") for _ in range(4)]
                d_ps = [Dep() for _ in range(4)]
                ub = [[c.sb(st, [128, TT + 2], F32, "u") for _ in range(2)] for _ in range(ng)]
                d_ub = [[Dep(), Dep()] for _ in range(ng)]
                cv = [[c.sb(st, [128, TT], F32, "cv") for _ in range(2)] for _ in range(2)]
                d_cv = [[Dep(), Dep()] for _ in range(2)]
                sg = [c.sb(st, [128, TT], F32, "sg") for _ in range(2)]
                d_sg = [Dep(), Dep()]
                ao = [c.sb(st, [128, TT], BF16, "ao") for _ in range(2)]
                d_ao = [Dep(), Dep()]
                nt = NT // TT
                ip = io = 0
                c.dma("sp", xt[0][:], xv[:, :, 0:TT], writes=[d_xt[0]])
                for it in range(nt):
                    b = it % 2
                    t0 = it * TT
                    seq_start = (t0 % S == 0)
                    if it + 1 < nt:
                        c.dma("sp", xt[1 - b][:], xv[:, :, t0 + TT:t0 + 2 * TT], writes=[d_xt[1 - b]])
                    for ch in range(ng):
                        k = io % 2
                        io += 1
                        for gv in range(2):
                            p = ip % 4
                            ip += 1
                            c0 = ch * 256 + gv * 128
                            for kc in range(KC):
                                c.op("pe", lambda e: e.matmul(pss[p][:, :TT], wt[:, kc, c0:c0 + 128], xt[b][:, kc, :],
                                                              start=(kc == 0), stop=(kc == KC - 1)),
                                     reads=[d_w, d_xt[b]], writes=[d_ps[p]])
                            u, d_u = ub[ch][gv], d_ub[ch][gv]
                            if seq_start:
                                c.op("pool", lambda e: e.memset(u[:, 0:2], 0.0), writes=[d_u])
                            else:
                                c.op("pool", lambda e: e.tensor_copy(u[:, 0:2], u[:, TT:TT + 2]), reads=[d_u], writes=[d_u])
                            c.op("act", lambda e: e.copy(out=u[:, 2:TT + 2], in_=pss[p][:, :TT]), reads=[d_ps[p]], writes=[d_u])
                            base = ((g0 + ch) * 2 + gv) * 4
                            cc, d_cc = cv[gv][k], d_cv[gv][k]
                            c.op("dve", lambda e: e.tensor_scalar(cc[:], u[:, 2:TT + 2], cwt[:, base + 2:base + 3],
                                                                  cwt[:, base + 3:base + 4], ALU.mult, ALU.add),
                                 reads=[d_u, d_cw], writes=[d_cc])
                            c.op("dve", lambda e: e.scalar_tensor_tensor(cc[:], u[:, 1:TT + 1], cwt[:, base + 1:base + 2], cc[:],
                                                                         ALU.mult, ALU.add),
                                 reads=[d_u, d_cw], writes=[d_cc])
                            c.op("dve", lambda e: e.scalar_tensor_tensor(cc[:], u[:, 0:TT], cwt[:, base + 0:base + 1], cc[:],
                                                                         ALU.mult, ALU.add),
                                 reads=[d_u, d_cw], writes=[d_cc])
                        c.op("act", lambda e: e.activation(out=sg[k][:], in_=cv[0][k][:], func=AF.Silu),
                             reads=[d_cv[0][k]], writes=[d_sg[k]])
                        c.op("pool", lambda e: e.tensor_tensor(ao[k][:], sg[k][:], cv[1][k][:], ALU.mult),
                             reads=[d_sg[k], d_cv[1][k]], writes=[d_ao[k]])
                        r0 = (g0 + ch) * 128
                        c.dma("pool", aT[r0:r0 + 128, t0:t0 + TT], ao[k][:], reads=[d_ao[k]])
            c.barrier()
        cwt_stack.close()

    def ple_phase(self, xT, Wg, Wpe, pT, hT, out, TT=512):
        cfg, c = self.cfg, self.c
        KC, NT, DC = cfg.KC, cfg.NT, cfg.DC
        xv = xT.rearrange("(kc p) t -> p kc t", p=128)
        pv = pT.rearrange("(kc p) t -> p kc t", p=128)
        if cfg.TP == 1:
            own = hT
        else:
            if getattr(self, "_own_hT", None) is None:
                pid = self.nc.sync.partition_id()
                self._own_hT = hT[bass.ds((pid % cfg.TP) * DC, DC), :]
            c.dma("sp", self.h_own[:, :], self._own_hT[:, :])
            c.barrier()
            own = self.h_own
        for g0 in range(0, DC, 1024):
            g1 = min(DC, g0 + 1024)
            nj = (g1 - g0) // 128
            with ExitStack() as st:
                wg, d_wg = self.load_w(st, Wg[:, g0:g1], cfg.D, g1 - g0, "wg")
                wp, d_wp = self.load_w(st, Wpe[:, g0:g1], 256, g1 - g0, "wp")
                xt = [c.sb(st, [128, KC, TT], BF16, "xt") for _ in range(2)]
                d_xt = [Dep(), Dep()]
                pt = [c.sb(st, [128, 2, TT], BF16, "pt") for _ in range(2)]
                d_pt = [Dep(), Dep()]
                hr = [c.sb(st, [128, nj, TT], F32, "hr") for _ in range(2)]
                d_hrj = [[Dep() for _ in range(nj)] for _ in range(2)]
                pss = [c.ps(st, [128, 512], F32, "pps") for _ in range(4)]
                d_ps = [Dep() for _ in range(4)]
                gs = [c.sb(st, [128, TT], F32, "gs") for _ in range(2)]
                d_gs = [Dep(), Dep()]
                ot = [c.sb(st, [128, TT], F32, "po") for _ in range(2)]
                d_ot = [Dep(), Dep()]
                nt = NT // TT
                ip = io = 0

                def load(it):
                    b = it % 2
                    t0 = it * TT
                    c.dma("sp", xt[b][:], xv[:, :, t0:t0 + TT], writes=[d_xt[b]])
                    c.dma("pool", pt[b][:], pv[:, :, t0:t0 + TT], writes=[d_pt[b]])
                    for j in range(nj):
                        src = own[g0 + j * 128:g0 + (j + 1) * 128, t0:t0 + TT]
                        c.dma("sp", hr[b][:, j, :], src, writes=[d_hrj[b][j]])
                load(0)
                for it in range(nt):
                    b = it % 2
                    t0 = it * TT
                    if it + 1 < nt:
                        load(it + 1)
                    for j in range(nj):
                        pg = ip % 4
                        pe_ = (ip + 1) % 4
                        ip += 2
                        k = io % 2
                        io += 1
                        for kc in range(KC):
                            c.op("pe", lambda e: e.matmul(pss[pg][:, :TT], wg[:, kc, j * 128:(j + 1) * 128], xt[b][:, kc, :],
                                                          start=(kc == 0), stop=(kc == KC - 1)),
                                 reads=[d_wg, d_xt[b]], writes=[d_ps[pg]])
                        for kc in range(2):
                            c.op("pe", lambda e: e.matmul(pss[pe_][:, :TT], wp[:, kc, j * 128:(j + 1) * 128], pt[b][:, kc, :],
                                                          start=(kc == 0), stop=(kc == 1)),
                                 reads=[d_wp, d_pt[b]], writes=[d_ps[pe_]])
                        c.op("act", lambda e: e.activation(out=gs[k][:], in_=pss[pg][:, :TT], func=AF.Sigmoid),
                             reads=[d_ps[pg]], writes=[d_gs[k]])
                        c.op("dve", lambda e: e.tensor_tensor(ot[k][:], pss[pe_][:, :TT], gs[k][:], ALU.mult),
                             reads=[d_ps[pe_], d_gs[k]], writes=[d_ot[k]])
                        c.op("dve", lambda e: e.tensor_tensor(ot[k][:], ot[k][:], hr[b][:, j, :], ALU.add),
                             reads=[d_hrj[b][j]], writes=[d_ot[k]])
                        c.dma("pool", out[g0 + j * 128:g0 + (j + 1) * 128, t0:t0 + TT], ot[k][:], reads=[d_ot[k]])
            c.barrier()

    def build(self, stop_after=None):
        cfg, c, nc = self.cfg, self.c, self.nc
        D, NT, KC, DC, TP = cfg.D, cfg.NT, cfg.KC, cfg.DC, cfg.TP
        self.setup()
        xs = self.inp("xs", [D, NT])
        ps_in = self.inp("ps", [cfg.depth * 256, NT])
        y = nc.dram_tensor("y", [DC, NT], F32, kind="ExternalOutput").ap()
        hT = self.scratch("hT", [D, NT], F32, force_internal=True)
        xs_i = self.scratch("xs_i", [DC, NT], F32, force_internal=True)
        ps_i = self.scratch("ps_i", [cfg.PR, NT], F32, force_internal=True)
        pT = ps_in
        xnT = self.scratch("xnT", [D, NT], BF16)
        part = self.scratch("part", [D, NT], F32, force_internal=True)
        red = part if TP == 1 else self.scratch("red", [D, NT], F32, force_internal=True)
        ple_in = self.scratch("ple_in", [DC, NT], F32, force_internal=True)
        agtmp = self.scratch("agtmp", [D, NT], F32, force_internal=True)
        self.h_own = self.scratch("h_own", [DC, NT], F32, force_internal=True)
        qkT = self.scratch("qkT", [2 * cfg.DK, NT], BF16)
        tmA = self.scratch("tmA", [NT, cfg.TMW], BF16)
        gatesA = self.scratch("gatesA", [NT, 2], F32)
        hsT = self.scratch("hsT", [cfg.HA * cfg.DV, NT], BF16)
        aT = self.scratch("aT", [cfg.FP, NT], BF16)
        kshT = self.scratch("kshT", [cfg.AW, NT], BF16)
        vsh = self.scratch("vsh", [NT, cfg.AW], BF16)
        qT = self.scratch("qT", [cfg.AW, NT], BF16)
        attT = self.scratch("attT", [cfg.AW, NT], BF16)

        def snapshot(name):
            if name in self.debug:
                t = nc.dram_tensor(name, [D, NT], F32, kind="ExternalOutput").ap()
                c.dma("sp", t[:, :], hT[:, :])
                c.barrier()

        def allreduce():
            if TP > 1:
                c.collective("AllReduce", ALU.add, part[:, :], red[:, :])

        for r0 in range(0, D, 512):
            c.dma("sp", hT[r0:r0 + 512, :], xs[r0:r0 + 512, :])
        c.barrier()
        snapshot("h_in")

        def gvec(name):
            return self.inp(name, [128, KC])

        for layer in range(cfg.depth):
            L = f"{layer}"
            if layer < cfg.NA:
                self.norm_phase(hT, None, None, gvec("a_norm_pre" + L), xnT)
                for hl in range(cfg.HA):
                    Lh = f"{layer}_{hl}"
                    wqk = self.inp("a_wqk" + Lh, [D, 2 * cfg.DK])
                    with ExitStack() as st:
                        ep = self.store_epilogue(st, qkT, BF16, scale=lambda c0: (cfg.DK ** -0.5 if c0 < cfg.DK else 1.0))
                        self.lin_fm(xnT, D, wqk, [(i * 128, 128) for i in range(2 * cfg.DKC)], ep)
                    self.lin_tm_A(xnT, self.inp("a_wtm" + Lh, [D, cfg.TMN]), self.inp("a_gb" + Lh, [128, 2]), tmA, gatesA)
                    self.mlstm_phase(qkT, tmA, gatesA, self.inp("a_hn" + Lh, [128, cfg.DV]),
                                     hsT[hl * cfg.DV:(hl + 1) * cfg.DV, :])
                mixK, mixX, wo = cfg.HA * cfg.DV, hsT, self.inp("a_wout" + L, [cfg.HA * cfg.DV, D])
                gpost = gvec("a_norm_post" + L)
            else:
                j = layer - cfg.NA
                if j == 0:
                    self.norm_phase(hT, None, None, gvec("kv_norm"), xnT)
                    with ExitStack() as st:
                        ep = self.store_epilogue(st, kshT, BF16)
                        self.lin_fm(xnT, D, self.inp("kv_wk", [D, cfg.AW]), [(i * 128, 128) for i in range(cfg.HPC)], ep)
                    self.lin_tm_plain(xnT, self.inp("kv_wv", [D, cfg.AW]), cfg.AW, vsh)
                self.norm_phase(hT, None, None, gvec("b_norm_pre" + L), xnT)
                with ExitStack() as st:
                    ep = self.store_epilogue(st, qT, BF16, scale=lambda c0: 128 ** -0.5)
                    self.lin_fm(xnT, D, self.inp("b_wq" + L, [D, cfg.AW]), [(i * 128, 128) for i in range(cfg.HPC)], ep)
                self.attn_phase(qT, kshT, vsh, attT)
                mixK, mixX, wo = cfg.AW, attT, self.inp("b_wout" + L, [cfg.AW, D])
                gpost = gvec("b_norm_post" + L)
            if stop_after == ("mix", layer):
                break
            self.lin_fm_ar(mixX, mixK, wo, part, red)
            self.norm_phase(red, gpost, hT, gvec("f_norm_pre" + L), xnT)
            snapshot("h_mix" + L)
            if stop_after == ("n2", layer):
                break
            self.ffn_up_phase(xnT, self.inp("f_wup" + L, [D, cfg.FCH * 256]), self.inp("f_cw" + L, [128, cfg.FCH * 8]), aT)
            self.lin_fm_ar(aT, cfg.FP, self.inp("f_wdown" + L, [cfg.FP, D]), part, red,
                           TT=(512 if cfg.FCH <= 32 else 256))
            self.norm_phase(red, gvec("f_norm_post" + L), hT, gvec("ple_gate_norm" + L), xnT)
            snapshot("h_ffn" + L)
            if stop_after == ("n3", layer):
                break
            last = layer == cfg.depth - 1
            if last:
                dst = y
            elif TP == 1:
                dst = hT
            else:
                dst = ple_in
            self.ple_phase(xnT, self.inp("ple_gw" + L, [D, DC]), self.inp("ple_w" + L, [256, DC]),
                           pT[layer * 256:(layer + 1) * 256, :], hT, dst)
            if not last:
                if TP > 1:
                    c.collective("AllGather", ALU.bypass, ple_in[:, :], hT[:, :], agtmp)
                snapshot("h_out" + L)
            if stop_after == ("layer", layer):
                break
        c.barrier()
        self.gs.close()
        c.es.close()
        return nc


def _gv(v, KC):
    return np.ascontiguousarray(np.asarray(v, np.float32).reshape(KC, 128).T)


def shard_inputs(cfg, inp):
    D, NT, KC, DC, TP = cfg.D, cfg.NT, cfg.KC, cfg.DC, cfg.TP
    DK, DV, QKW, VW, HA = cfg.DK, cfg.DV, cfg.QKW, cfg.VW, cfg.HA
    f = lambda a: np.asarray(a, np.float32)
    x = f(inp["x"])
    p = f(inp["p"])
    consts = consts_array()
    shared = []
    for r in range(TP):
        m = {"consts": consts}
        for layer in range(cfg.depth):
            L = f"{layer}"
            if layer < cfg.NA:
                a = layer
                w_in = f(inp["a_w_in"][a])
                gb = f(inp["a_gate_bias"][a])
                hn = f(inp["a_head_norm"][a])
                for hl in range(HA):
                    h = r * HA + hl
                    Lh = f"{layer}_{hl}"
                    q = w_in[:, h * DK:(h + 1) * DK]
                    k = w_in[:, QKW + h * DK:QKW + (h + 1) * DK]
                    v = w_in[:, 2 * QKW + h * DV:2 * QKW + (h + 1) * DV]
                    og = w_in[:, 2 * QKW + VW + h * DV:2 * QKW + VW + (h + 1) * DV]
                    ig = w_in[:, 2 * QKW + 2 * VW + h:2 * QKW + 2 * VW + h + 1]
                    fg = w_in[:, 2 * QKW + 2 * VW + 8 + h:2 * QKW + 2 * VW + 8 + h + 1]
                    m["a_wqk" + Lh] = np.ascontiguousarray(np.concatenate([q, k], 1))
                    m["a_wtm" + Lh] = np.ascontiguousarray(np.concatenate([v, og, k, ig, fg], 1))
                    m["a_gb" + Lh] = np.ascontiguousarray(np.broadcast_to(np.array([gb[h], gb[8 + h]], np.float32)[None, :], (128, 2)))
                    m["a_hn" + Lh] = np.ascontiguousarray(np.broadcast_to(hn[h * DV:(h + 1) * DV][None, :], (128, DV)))
                m["a_norm_pre" + L] = _gv(inp["a_norm_pre"][a], KC)
                m["a_wout" + L] = np.ascontiguousarray(f(inp["a_w_out"][a])[r * HA * DV:(r + 1) * HA * DV, :])
                m["a_norm_post" + L] = _gv(inp["a_norm_post"][a], KC)
            else:
                j = layer - cfg.NA
                AW = cfg.AW
                if j == 0:
                    m["kv_norm"] = _gv(inp["kv_norm"], KC)
                    kvw = f(inp["kv_w"])
                    m["kv_wk"] = np.ascontiguousarray(kvw[:, r * AW:(r + 1) * AW])
                    m["kv_wv"] = np.ascontiguousarray(kvw[:, cfg.H * 128 + r * AW:cfg.H * 128 + (r + 1) * AW])
                m["b_norm_pre" + L] = _gv(inp["b_norm_pre"][j], KC)
                m["b_wq" + L] = np.ascontiguousarray(f(inp["b_w_q"][j])[:, r * AW:(r + 1) * AW])
                m["b_wout" + L] = np.ascontiguousarray(f(inp["b_w_out"][j])[r * AW:(r + 1) * AW, :])
                m["b_norm_post" + L] = _gv(inp["b_norm_post"][j], KC)
            F, FC, FCH, FP = cfg.F, cfg.FC, cfg.FCH, cfg.FP
            m["f_norm_pre" + L] = _gv(inp["f_norm_pre"][layer], KC)
            wup = f(inp["f_w_up"][layer])
            wu = np.zeros((D, FCH, 2, 128), np.float32)
            gpad = np.zeros((D, FP), np.float32)
            vpad = np.zeros((D, FP), np.float32)
            gpad[:, :FC] = wup[:, r * FC:(r + 1) * FC]
            vpad[:, :FC] = wup[:, F + r * FC:F + (r + 1) * FC]
            wu[:, :, 0, :] = gpad.reshape(D, FCH, 128)
            wu[:, :, 1, :] = vpad.reshape(D, FCH, 128)
            del gpad, vpad
            m["f_wup" + L] = wu.reshape(D, FCH * 256)
            cwl = np.zeros((FCH, 128, 2, 4), np.float32)
            convw = f(inp["f_conv_w"][layer])
            convb = f(inp["f_conv_b"][layer])
            for gv in range(2):
                sl = slice(gv * F + r * FC, gv * F + (r + 1) * FC)
                tmp = np.zeros((FP, 4), np.float32)
                tmp[:FC, 0:3] = convw[:, sl].T
                tmp[:FC, 3] = convb[sl]
                cwl[:, :, gv, :] = tmp.reshape(FCH, 128, 4)
            m["f_cw" + L] = np.ascontiguousarray(cwl.transpose(1, 0, 2, 3)).reshape(128, FCH * 8)
            wd = np.zeros((FP, D), np.float32)
            wd[:FC] = f(inp["f_w_down"][layer])[r * FC:(r + 1) * FC]
            m["f_wdown" + L] = wd
            m["f_norm_post" + L] = _gv(inp["f_norm_post"][layer], KC)
            m["ple_gate_norm" + L] = _gv(inp["ple_gate_norm"][layer], KC)
            m["ple_gw" + L] = np.ascontiguousarray(f(inp["ple_gate_w"][layer])[:, r * DC:(r + 1) * DC])
            m["ple_w" + L] = np.ascontiguousarray(f(inp["ple_w"][layer])[:, r * DC:(r + 1) * DC])
        shared.append(m)
    maps = []
    for g in range(cfg.B):
        xT = np.ascontiguousarray(x[g].T)
        pT = np.ascontiguousarray(p[:, g].transpose(0, 2, 1)).reshape(cfg.depth * 256, NT)
        for r in range(TP):
            m = dict(shared[r])
            m["xs"] = xT
            m["ps"] = pT
            maps.append(m)
    return maps


def run(cfg, inputs, debug=(), stop_after=None, trace=False):
    b = Builder(cfg, debug=debug)
    nc = b.build(stop_after=stop_after)
    maps = shard_inputs(cfg, inputs)
    maps = [{k: v for k, v in m.items() if k in b.inputs} for m in maps]
    for m in maps:
        for k, shp in b.inputs.items():
            assert m[k].shape == shp, (k, m[k].shape, shp)
    res = run_bass_kernel_spmd(nc, maps, core_ids=list(range(cfg.NCU)), trace=trace)
    return res


def assemble(cfg, results, key="y"):
    out = np.empty((cfg.B, cfg.S, cfg.D), np.float32)
    for g in range(cfg.B):
        for r in range(cfg.TP):
            out[g, :, r * cfg.DC:(r + 1) * cfg.DC] = results[g * cfg.TP + r][key].T
    return out


def kernel(**inputs):
    cfg = Cfg()
    res = run(cfg, inputs)
    return assemble(cfg, res.results)
```

```python
import numpy as np
from contextlib import ExitStack
import concourse.bass as bass
import concourse.mybir as mybir
from concourse.bass_utils import run_bass_kernel_spmd

F32 = mybir.dt.float32
BF16 = mybir.dt.bfloat16
AF = mybir.ActivationFunctionType
ALU = mybir.AluOpType

NCORES = 8
CC_MAX_BYTES = 4 << 20
EPS = 1e-6
CAP = 15.0


class Cfg:
    def __init__(self, D=4096, B=2, S=4096, depth=4, TP=4):
        self.D, self.B, self.S, self.depth, self.TP = D, B, S, depth, TP
        self.NCU = TP * B
        self.NT = S
        self.HA = 8 // TP
        self.KC = D // 128
        self.NA = depth // 2
        self.DV = D // 8
        self.DK = self.DV // 2
        self.DKC = self.DK // 128
        self.DVC = self.DV // 128
        self.QKW = 8 * self.DK
        self.VW = 8 * self.DV
        self.H = D // 128
        self.HPC = self.H // TP
        self.AW = self.HPC * 128
        self.F = ((D * 8 // 3 + 63) // 64) * 64
        self.FC = self.F // TP
        self.FCH = (self.FC + 127) // 128
        self.FP = self.FCH * 128
        self.DC = D // TP
        self.PR = depth * 256 // TP
        self.DCC = self.DC // 128
        self.TMW = 2 * self.DV + self.DK
        self.TMN = self.TMW + 2


class Dep:
    __slots__ = ("w", "r")

    def __init__(self):
        self.w = None
        self.r = {}


class Ctx:
    def __init__(self, nc, TP=4, NG=2):
        self.nc = nc
        self.TP, self.NG = TP, NG
        self.es = ExitStack()
        self.eng = {"pe": nc.tensor, "act": nc.scalar, "dve": nc.vector,
                    "pool": nc.gpsimd, "sp": nc.sync}
        self.psem, self.cnt, self.waited = {}, {}, {}
        for e in self.eng:
            self.psem[e] = self.es.enter_context(nc.semaphore("ps_" + e))
            self.cnt[e] = 0
            self.waited[e] = {}
        self.slots, self.slot_cnt, self.dma_idx = {}, {}, {}
        for q, k in {"sp": 12, "pool": 8}.items():
            self.slots[q] = [self.es.enter_context(nc.semaphore(f"dq_{q}{i}")) for i in range(k)]
            for s in self.slots[q]:
                self.slot_cnt[s] = 0
            self.dma_idx[q] = 0
        self.ccsem = self.es.enter_context(nc.semaphore("cc"))
        self.cc_cnt = 0
        self.uid = 0

    def sb(self, stack, shape, dt, name="t"):
        self.uid += 1
        return stack.enter_context(self.nc.sbuf_tensor(f"{name}_{self.uid}", list(shape), dt))

    def ps(self, stack, shape, dt, name="p"):
        self.uid += 1
        return stack.enter_context(self.nc.psum_tensor(f"{name}_{self.uid}", list(shape), dt))

    def _wait(self, e, toks):
        need = {}
        for (s, v) in toks:
            if need.get(s, 0) < v:
                need[s] = v
        for s, v in need.items():
            if self.waited[e].get(s, 0) >= v:
                continue
            if e == "pe" and s is self.psem["pe"]:
                continue
            self.eng[e].wait_ge(s, v)
            self.waited[e][s] = v

    @staticmethod
    def _deps(reads, writes):
        toks = []
        for d in reads:
            if d.w is not None:
                toks.append(d.w)
        for d in writes:
            if d.w is not None:
                toks.append(d.w)
            toks.extend(d.r.items())
        return toks

    @staticmethod
    def _commit(tok, reads, writes):
        s, v = tok
        for d in reads:
            if d.r.get(s, 0) < v:
                d.r[s] = v
        for d in writes:
            d.w = tok
            d.r = {}

    def op(self, e, fn, reads=(), writes=()):
        self._wait(e, self._deps(reads, writes))
        ins = fn(self.eng[e])
        self.cnt[e] += 1
        ins.then_inc(self.psem[e], 1)
        tok = (self.psem[e], self.cnt[e])
        self._commit(tok, reads, writes)
        return tok

    def dma(self, q, out, in_, reads=(), writes=()):
        toks = self._deps(reads, writes)
        sl = self.slots[q]
        slot = sl[self.dma_idx[q] % len(sl)]
        self.dma_idx[q] += 1
        prev = self.slot_cnt[slot]
        if prev > 0:
            toks.append((slot, prev))
        self._wait(q, toks)
        self.eng[q].dma_start(out=out, in_=in_).then_inc(slot, 16)
        self.slot_cnt[slot] = prev + 16
        tok = (slot, prev + 16)
        self._commit(tok, reads, writes)
        return tok

    def collective(self, kind, op, in_ap, out_ap, tmp=None):
        self.barrier()
        TP = self.TP
        groups = [[g * TP + r for r in range(TP)] for g in range(self.NG)]
        R, C = in_ap.shape
        esz = 4
        if kind == "AllReduce":
            rc = max(1, CC_MAX_BYTES // (C * esz))
            for r0 in range(0, R, rc):
                r1 = min(R, r0 + rc)
                self.nc.gpsimd.collective_compute(kind, op, replica_groups=groups,
                                                  ins=[in_ap[r0:r1, :]], outs=[out_ap[r0:r1, :]]).then_inc(self.ccsem, 1)
                self.cc_cnt += 1
            self.barrier()
        else:
            rc = max(1, CC_MAX_BYTES // (C * esz * TP))
            chunks = []
            o = 0
            for r0 in range(0, R, rc):
                r1 = min(R, r0 + rc)
                n = r1 - r0
                self.nc.gpsimd.collective_compute(kind, op, replica_groups=groups,
                                                  ins=[in_ap[r0:r1, :]], outs=[tmp[o:o + TP * n, :]]).then_inc(self.ccsem, 1)
                self.cc_cnt += 1
                chunks.append((r0, n, o))
                o += TP * n
            self.barrier()
            ov = out_ap.rearrange("(r n) c -> r n c", r=TP)
            for (r0, n, o) in chunks:
                self.dma("sp", ov[:, r0:r0 + n, :], tmp[o:o + TP * n, :].rearrange("(r n) c -> r n c", r=TP))
            self.barrier()

    def wait_cc(self, e, count):
        if count > 0:
            self._wait(e, [(self.ccsem, count)])

    def barrier(self, cc=True):
        toks = [(self.psem[e], self.cnt[e]) for e in self.eng if self.cnt[e] > 0]
        toks += [(s, c) for s, c in self.slot_cnt.items() if c > 0]
        if self.cc_cnt and cc:
            toks.append((self.ccsem, self.cc_cnt))
        for e in self.eng:
            self._wait(e, toks)


def make_consts():
    j = np.arange(128)[:, None]
    t = np.arange(128)[None, :]
    c = {}
    c["ones"] = np.ones((128, 128), np.float32)
    c["negones"] = -np.ones((128, 128), np.float32)
    c["ident"] = np.eye(128, dtype=np.float32)
    c["U"] = (j <= t).astype(np.float32)
    c["negSLE"] = -(j >= t).astype(np.float32)
    c["negmask"] = np.where(j <= t, 0.0, -30000.0).astype(np.float32)
    t5 = np.arange(512)[None, :]
    for jj in range(4):
        c[f"amask{jj}"] = ((jj * 128 + j) < t5).astype(np.float32)
    return c


CONST_ORDER = ["ones", "negones", "ident", "U", "negSLE", "negmask", "amask0", "amask1", "amask2", "amask3"]


def consts_array():
    c = make_consts()
    return np.ascontiguousarray(np.concatenate([c[k] for k in CONST_ORDER], axis=1))


class Builder:
    def __init__(self, cfg, debug=()):
        self.cfg = cfg
        self.debug = set(debug)
        self.nc = bass.Bass("TRN2", target_bir_lowering=False)
        self.c = Ctx(self.nc, cfg.TP, cfg.B)
        self.inputs = {}
        self.scr = {}

    def inp(self, name, shape):
        t = self.nc.dram_tensor(name, list(shape), F32, kind="ExternalInput").ap()
        self.inputs[name] = tuple(shape)
        return t

    def scratch(self, name, shape, dt, force_internal=False):
        kind = "ExternalOutput" if (name in self.debug and not force_internal) else "Internal"
        t = self.nc.dram_tensor(name, list(shape), dt, kind=kind).ap()
        self.scr[name] = t
        return t

    def load_w(self, stack, W, K, N, name="w"):
        c = self.c
        kc_n = (K + 127) // 128
        wt = c.sb(stack, [128, kc_n, N], BF16, name)
        d = Dep()
        step = 8
        for k0 in range(0, kc_n, step):
            k1 = min(kc_n, k0 + step)
            c.dma("pool", wt[:, k0:k1, :], W[k0 * 128:k1 * 128, :].rearrange("(kc p) n -> p kc n", p=128),
                  writes=[d])
        return wt, d

    def load_f32(self, stack, src, shape, name="v"):
        t = self.c.sb(stack, shape, F32, name)
        d = Dep()
        self.c.dma("sp", t[:], src, writes=[d])
        return t, d

    def setup(self):
        cfg, c, nc = self.cfg, self.c, self.nc
        self.gs = ExitStack()
        ncst = len(CONST_ORDER)
        cw = 128 * 6 + 512 * 4
        cin = self.inp("consts", [128, cw])
        cf, d_cf = self.load_f32(self.gs, cin[:, :], [128, cw], "cf")
        self.d_const = Dep()
        off = {}
        o = 0
        for k in CONST_ORDER:
            w = 512 if k.startswith("amask") else 128
            off[k] = (o, w)
            o += w
        self.cf = cf
        self.onesf = cf[:, off["ones"][0]:off["ones"][0] + 128]
        self.Uf = cf[:, off["U"][0]:off["U"][0] + 128]
        self.negmask = cf[:, off["negmask"][0]:off["negmask"][0] + 128]
        cb = c.sb(self.gs, [128, cw], BF16, "cb")
        c.op("dve", lambda e: e.tensor_copy(cb[:], cf[:]), reads=[d_cf], writes=[self.d_const])
        self.d_const_f = d_cf
        self.ones_bf = cb[:, off["ones"][0]:off["ones"][0] + 128]
        self.negones_bf = cb[:, off["negones"][0]:off["negones"][0] + 128]
        self.ident_bf = cb[:, off["ident"][0]:off["ident"][0] + 128]
        self.negSLE_bf = cb[:, off["negSLE"][0]:off["negSLE"][0] + 128]
        self.amask_bf = [cb[:, off[f"amask{j}"][0]:off[f"amask{j}"][0] + 512] for j in range(4)]
        self.eps = c.sb(self.gs, [128, 1], F32, "eps")
        self.d_eps = Dep()
        c.op("pool", lambda e: e.memset(self.eps[:], EPS), writes=[self.d_eps])

    def norm_phase(self, src, g1, resid, g2, xn_out, TT=256):
        cfg, c = self.cfg, self.c
        KC, NT, D = cfg.KC, cfg.NT, cfg.D
        with ExitStack() as st:
            g1t = g2t = None
            if g1 is not None:
                g1t, d_g1 = self.load_f32(st, g1[:, :], [128, KC], "g1")
            if g2 is not None:
                g2t, d_g2 = self.load_f32(st, g2[:, :], [128, KC], "g2")
            two = resid is not None
            halves = src if isinstance(src, list) else None
            srct = [c.sb(st, [128, KC, TT], F32, "src") for _ in range(2)]
            d_src = [Dep(), Dep()]
            if two:
                ht = [c.sb(st, [128, KC, TT], F32, "h") for _ in range(1)]
                d_h = [Dep()]
            sqs = [c.sb(st, [128, KC, TT], BF16, "sq") for _ in range(2)]
            d_sqs = [Dep(), Dep()]
            if xn_out is not None:
                xn = [c.sb(st, [128, KC, TT], BF16, "xn") for _ in range(2)]
                d_xn = [Dep(), Dep()]
            rs = [c.sb(st, [128, TT], F32, "rs") for _ in range(2)]
            d_rs = [Dep(), Dep()]
            pss = [c.ps(st, [128, 512], F32, "nps") for _ in range(2)]
            d_ps = [Dep(), Dep()]
            HN = NT // 2
            if halves is None:
                sv = src.rearrange("(kc p) t -> p kc t", p=128)
            else:
                svh = [h_[0].rearrange("(kc p) t -> p kc t", p=128) for h_ in halves]
            gated = set()

            def load_src(it):
                t0 = it * TT
                if halves is None:
                    ap = sv[:, :, t0:t0 + TT]
                else:
                    hh = t0 // HN
                    if hh not in gated:
                        gated.add(hh)
                        c.wait_cc("sp", halves[hh][1])
                    ap = svh[hh][:, :, t0 - hh * HN:t0 - hh * HN + TT]
                c.dma("sp", srct[it % 2][:], ap, writes=[d_src[it % 2]])
            if two:
                hv = resid.rearrange("(kc p) t -> p kc t", p=128)
            if xn_out is not None:
                xv = xn_out.rearrange("(kc p) t -> p kc t", p=128)
            nt = NT // TT
            irs = 0

            def rstd_of(tile, d_tile):
                nonlocal irs
                k = irs % 2
                irs += 1
                sq, d_sq = sqs[k], d_sqs[k]
                c.op("act", lambda e: e.activation(out=sq[:], in_=tile[:], func=AF.Square),
                     reads=[d_tile], writes=[d_sq])
                for kc in range(KC):
                    c.op("pe", lambda e: e.matmul(pss[k][:, :TT], self.ones_bf, sq[:, kc, :],
                                                  start=(kc == 0), stop=(kc == KC - 1)),
                         reads=[self.d_const, d_sq], writes=[d_ps[k]])
                c.op("act", lambda e: e.activation(out=rs[k][:], in_=pss[k][:, :TT], func=AF.Sqrt,
                                                   bias=self.eps[:, 0:1], scale=1.0 / D),
                     reads=[d_ps[k], self.d_eps], writes=[d_rs[k]])
                c.op("dve", lambda e: e.reciprocal(rs[k][:], rs[k][:]), reads=[d_rs[k]], writes=[d_rs[k]])
                return rs[k], d_rs[k]

            load_src(0)
            for it in range(nt):
                b = it % 2
                t0 = it * TT
                if two:
                    c.dma("sp", ht[0][:], hv[:, :, t0:t0 + TT], writes=[d_h[0]])
                if it + 1 < nt:
                    load_src(it + 1)
                cur, d_cur = srct[b], d_src[b]
                if two:
                    r1, d_r1 = rstd_of(cur, d_cur)
                    for kc in range(KC):
                        c.op("dve", lambda e: e.scalar_tensor_tensor(cur[:, kc, :], cur[:, kc, :], g1t[:, kc:kc + 1],
                                                                     r1[:], ALU.mult, ALU.mult),
                             reads=[d_g1, d_r1], writes=[d_cur])
                    c.op("pool", lambda e: e.tensor_tensor(cur[:], cur[:], ht[0][:], ALU.add),
                         reads=[d_h[0]], writes=[d_cur])
                    c.dma("pool", hv[:, :, t0:t0 + TT], cur[:], reads=[d_cur])
                if xn_out is not None:
                    r2, d_r2 = rstd_of(cur, d_cur)
                    for kc in range(KC):
                        c.op("dve", lambda e: e.scalar_tensor_tensor(xn[b][:, kc, :], cur[:, kc, :], g2t[:, kc:kc + 1],
                                                                     r2[:], ALU.mult, ALU.mult),
                             reads=[d_cur, d_g2, d_r2], writes=[d_xn[b]])
                    c.dma("pool", xv[:, :, t0:t0 + TT], xn[b][:], reads=[d_xn[b]])
        c.barrier()

    def lin_fm(self, xT, K, W, nch_list, epilogue, TT=512, wname="w"):
        cfg, c = self.cfg, self.c
        NT = cfg.NT
        kc_n = (K + 127) // 128
        maxc = max(128, (65536 // (kc_n * 2)) // 128 * 128)
        if len(nch_list) * 128 > maxc:
            per = maxc // 128
            for i in range(0, len(nch_list), per):
                self.lin_fm(xT, K, W, nch_list[i:i + per], epilogue, TT, wname)
            return
        gc0 = nch_list[0][0]
        gc1 = nch_list[-1][0] + nch_list[-1][1]
        with ExitStack() as st:
            wt, d_w = self.load_w(st, W[:, gc0:gc1], K, gc1 - gc0, wname)
            xt = [c.sb(st, [128, kc_n, TT], BF16, "xt") for _ in range(2)]
            d_xt = [Dep(), Dep()]
            pss = [c.ps(st, [128, 512], F32, "lps") for _ in range(4)]
            d_ps = [Dep() for _ in range(4)]
            xv = xT.rearrange("(kc p) t -> p kc t", p=128)
            nt = NT // TT
            ip = 0
            c.dma("sp", xt[0][:], xv[:, :, 0:TT], writes=[d_xt[0]])
            for it in range(nt):
                b = it % 2
                t0 = it * TT
                if it + 1 < nt:
                    c.dma("sp", xt[1 - b][:], xv[:, :, t0 + TT:t0 + 2 * TT], writes=[d_xt[1 - b]])
                for idx, (c0, wd) in enumerate(nch_list):
                    p = ip % 4
                    ip += 1
                    for kc in range(kc_n):
                        c.op("pe", lambda e: e.matmul(pss[p][:wd, :TT], wt[:, kc, c0 - gc0:c0 - gc0 + wd], xt[b][:, kc, :],
                                                      start=(kc == 0), stop=(kc == kc_n - 1)),
                             reads=[d_w, d_xt[b]], writes=[d_ps[p]])
                    epilogue(idx, c0, wd, pss[p], d_ps[p], t0, TT)
        c.barrier()

    def lin_fm_ar(self, xT, K, W, part2, red2, TT=512):
        cfg, c = self.cfg, self.c
        NT, D, TP = cfg.NT, cfg.D, cfg.TP
        HN = NT // 2
        kc_n = (K + 127) // 128
        AR_R = max(128, min(D, (CC_MAX_BYTES // (HN * 4)) // 128 * 128))
        GR = min(D, max(AR_R, 1024))
        ngr = D // GR
        groups = [[g * TP + r for r in range(TP)] for g in range(cfg.B)]
        counts = []
        with ExitStack() as st:
            wts = [c.sb(st, [128, kc_n, GR], BF16, "wr") for _ in range(2)]
            d_w = [Dep(), Dep()]
            xt = [c.sb(st, [128, kc_n, TT], BF16, "xt") for _ in range(2)]
            d_xt = [Dep(), Dep()]
            pss = [c.ps(st, [128, 512], F32, "lps") for _ in range(4)]
            d_ps = [Dep() for _ in range(4)]
            ot = [c.sb(st, [128, 512], F32, "ot") for _ in range(3)]
            d_ot = [Dep() for _ in range(3)]
            xv = xT.rearrange("(kc p) t -> p kc t", p=128)
            nth = HN // TT
            seq = [(hf, g, it) for hf in range(2) for g in range(ngr) for it in range(nth)]
            wseq = [(hf, g) for hf in range(2) for g in range(ngr)]

            def loadw(wi):
                g = wseq[wi][1]
                for k0 in range(0, kc_n, 8):
                    k1 = min(kc_n, k0 + 8)
                    c.dma("pool", wts[wi % 2][:, k0:k1, :],
                          W[k0 * 128:k1 * 128, g * GR:(g + 1) * GR].rearrange("(kc p) n -> p kc n", p=128), writes=[d_w[wi % 2]])

            def loadx(i):
                hf, g, it = seq[i]
                t0 = hf * HN + it * TT
                c.dma("sp", xt[i % 2][:], xv[:, :, t0:t0 + TT], writes=[d_xt[i % 2]])
            loadw(0)
            loadx(0)
            ip = io = 0
            i = 0
            for wi, (hf, g) in enumerate(wseq):
                if wi + 1 < len(wseq):
                    loadw(wi + 1)
                wt = wts[wi % 2]
                toks = []
                for it in range(nth):
                    b = i % 2
                    t0 = it * TT
                    if i + 1 < len(seq):
                        loadx(i + 1)
                    for j in range(GR // 128):
                        p = ip % 4
                        ip += 1
                        k = io % 3
                        io += 1
                        for kc in range(kc_n):
                            c.op("pe", lambda e: e.matmul(pss[p][:, :TT], wt[:, kc, j * 128:(j + 1) * 128], xt[b][:, kc, :],
                                                          start=(kc == 0), stop=(kc == kc_n - 1)),
                                 reads=[d_w[wi % 2], d_xt[b]], writes=[d_ps[p]])
                        c.op("act", lambda e: e.copy(out=ot[k][:, :TT], in_=pss[p][:, :TT]), reads=[d_ps[p]], writes=[d_ot[k]])
                        r0 = g * GR + j * 128
                        toks.append(c.dma("pool", part2[hf][r0:r0 + 128, t0:t0 + TT], ot[k][:, :TT], reads=[d_ot[k]]))
                    i += 1
                if TP > 1:
                    c._wait("pool", toks)
                    for a0 in range(g * GR, (g + 1) * GR, AR_R):
                        self.nc.gpsimd.collective_compute("AllReduce", ALU.add, replica_groups=groups,
                                                          ins=[part2[hf][a0:a0 + AR_R, :]],
                                                          outs=[red2[hf][a0:a0 + AR_R, :]]).then_inc(c.ccsem, 1)
                        c.cc_cnt += 1
                if g == ngr - 1:
                    counts.append(c.cc_cnt)
        c.barrier(cc=False)
        return counts

    def store_epilogue(self, st, out, dt, scale=None, row_off=0, n=3):
        c = self.c
        ot = [c.sb(st, [128, 512], dt, "ot") for _ in range(n)]
        d_ot = [Dep() for _ in range(n)]
        state = {"i": 0}

        def ep(idx, c0, wd, ps, d_ps, t0, TT):
            k = state["i"] % n
            state["i"] += 1
            if scale is None:
                c.op("act", lambda e: e.copy(out=ot[k][:wd, :TT], in_=ps[:wd, :TT]), reads=[d_ps], writes=[d_ot[k]])
            else:
                c.op("act", lambda e: e.mul(out=ot[k][:wd, :TT], in_=ps[:wd, :TT], mul=scale(c0)),
                     reads=[d_ps], writes=[d_ot[k]])
            c.dma("pool", out[row_off + c0:row_off + c0 + wd, t0:t0 + TT], ot[k][:wd, :TT], reads=[d_ot[k]])
        return ep

    def lin_tm_A(self, xT, W, gb, tm_out, gates_out, TT=512):
        cfg, c = self.cfg, self.c
        KC, NT, DV, DK = cfg.KC, cfg.NT, cfg.DV, cfg.DK
        N = cfg.TMN
        groups = []
        o = 0
        while o < N:
            w = min(512, N - o)
            groups.append((o, w))
            o += w
        with ExitStack() as st:
            wt, d_w = self.load_w(st, W, cfg.D, N, "wtm")
            gbt, d_gb = self.load_f32(st, gb[:, :], [128, 2], "gb")
            gbs = c.sb(st, [128, 2], F32, "gbs")
            d_gbs = Dep()
            c.op("dve", lambda e: e.tensor_scalar(gbs[:], gbt[:], 1.0 / CAP, None, ALU.mult), reads=[d_gb], writes=[d_gbs])
            xt = [c.sb(st, [128, KC, TT], BF16, "xt") for _ in range(2)]
            d_xt = [Dep(), Dep()]
            pss = [c.ps(st, [128, 512], F32, "tps") for _ in range(4)]
            d_ps = [Dep() for _ in range(4)]
            ot = [c.sb(st, [128, cfg.TMW], BF16, "tmo") for _ in range(2)]
            d_ot = [Dep(), Dep()]
            gt = [c.sb(st, [128, 4], F32, "gto") for _ in range(2)]
            d_gt = [Dep(), Dep()]
            xv = xT.rearrange("(kc p) t -> p kc t", p=128)
            nt = NT // TT
            ip = 0
            io = 0
            c.dma("sp", xt[0][:], xv[:, :, 0:TT], writes=[d_xt[0]])
            for it in range(nt):
                b = it % 2
                t0 = it * TT
                if it + 1 < nt:
                    c.dma("sp", xt[1 - b][:], xv[:, :, t0 + TT:t0 + 2 * TT], writes=[d_xt[1 - b]])
                for tb in range(TT // 128):
                    k = io % 2
                    io += 1
                    for (c0, wd) in groups:
                        p = ip % 4
                        ip += 1
                        for kc in range(KC):
                            c.op("pe", lambda e: e.matmul(pss[p][:, :wd], xt[b][:, kc, tb * 128:(tb + 1) * 128],
                                                          wt[:, kc, c0:c0 + wd], start=(kc == 0), stop=(kc == KC - 1)),
                                 reads=[d_w, d_xt[b]], writes=[d_ps[p]])
                        for (s0, s1, kind) in ((0, DV, "v"), (DV, 2 * DV, "og"), (2 * DV, 2 * DV + DK, "k"),
                                               (cfg.TMW, cfg.TMW + 2, "g")):
                            a0, a1 = max(s0, c0), min(s1, c0 + wd)
                            if a0 >= a1:
                                continue
                            src = pss[p][:, a0 - c0:a1 - c0]
                            if kind in ("v", "k"):
                                c.op("act", lambda e: e.copy(out=ot[k][:, a0:a1], in_=src), reads=[d_ps[p]], writes=[d_ot[k]])
                            elif kind == "og":
                                c.op("act", lambda e: e.activation(out=ot[k][:, a0:a1], in_=src, func=AF.Sigmoid),
                                     reads=[d_ps[p]], writes=[d_ot[k]])
                            else:
                                c.op("act", lambda e: e.activation(out=gt[k][:, 0:1], in_=src[:, 0:1], func=AF.Tanh,
                                                                   bias=gbs[:, 0:1], scale=1.0 / CAP),
                                     reads=[d_ps[p], d_gbs], writes=[d_gt[k]])
                                c.op("act", lambda e: e.activation(out=gt[k][:, 1:2], in_=src[:, 1:2], func=AF.Tanh,
                                                                   bias=gbs[:, 1:2], scale=1.0 / CAP),
                                     reads=[d_ps[p], d_gbs], writes=[d_gt[k]])
                                c.op("act", lambda e: e.activation(out=gt[k][:, 2:3], in_=gt[k][:, 1:2], func=AF.Exp, scale=-CAP),
                                     reads=[d_gt[k]], writes=[d_gt[k]])
                                c.op("act", lambda e: e.activation(out=gt[k][:, 3:4], in_=gt[k][:, 2:3], func=AF.Ln, bias=1.0),
                                     reads=[d_gt[k]], writes=[d_gt[k]])
                                c.op("dve", lambda e: e.tensor_scalar(gt[k][:, 0:1], gt[k][:, 0:1], CAP, None, ALU.mult),
                                     reads=[d_gt[k]], writes=[d_gt[k]])
                                c.op("dve", lambda e: e.tensor_scalar(gt[k][:, 1:2], gt[k][:, 3:4], -1.0, None, ALU.mult),
                                     reads=[d_gt[k]], writes=[d_gt[k]])
                    r0 = t0 + tb * 128
                    c.dma("pool", tm_out[r0:r0 + 128, :], ot[k][:], reads=[d_ot[k]])
                    c.dma("pool", gates_out[r0:r0 + 128, :], gt[k][:, 0:2], reads=[d_gt[k]])
        c.barrier()

    def lin_tm_plain(self, xT, W, N, out, TT=512):
        cfg, c = self.cfg, self.c
        KC, NT = cfg.KC, cfg.NT
        if N > 1024:
            for o in range(0, N, 1024):
                w = min(1024, N - o)
                self.lin_tm_plain(xT, W[:, o:o + w], w, out[:, o:o + w], TT)
            return
        groups = [(o, min(512, N - o)) for o in range(0, N, 512)]
        with ExitStack() as st:
            wt, d_w = self.load_w(st, W, cfg.D, N, "wtp")
            xt = [c.sb(st, [128, KC, TT], BF16, "xt") for _ in range(2)]
            d_xt = [Dep(), Dep()]
            pss = [c.ps(st, [128, 512], F32, "tps") for _ in range(4)]
            d_ps = [Dep() for _ in range(4)]
            ot = [c.sb(st, [128, N], BF16, "tmo") for _ in range(2)]
            d_ot = [Dep(), Dep()]
            xv = xT.rearrange("(kc p) t -> p kc t", p=128)
            nt = NT // TT
            ip = io = 0
            c.dma("sp", xt[0][:], xv[:, :, 0:TT], writes=[d_xt[0]])
            for it in range(nt):
                b = it % 2
                t0 = it * TT
                if it + 1 < nt:
                    c.dma("sp", xt[1 - b][:], xv[:, :, t0 + TT:t0 + 2 * TT], writes=[d_xt[1 - b]])
                for tb in range(TT // 128):
                    k = io % 2
                    io += 1
                    for (c0, wd) in groups:
                        p = ip % 4
                        ip += 1
                        for kc in range(KC):
                            c.op("pe", lambda e: e.matmul(pss[p][:, :wd], xt[b][:, kc, tb * 128:(tb + 1) * 128],
                                                          wt[:, kc, c0:c0 + wd], start=(kc == 0), stop=(kc == KC - 1)),
                                 reads=[d_w, d_xt[b]], writes=[d_ps[p]])
                        c.op("act", lambda e: e.copy(out=ot[k][:, c0:c0 + wd], in_=pss[p][:, :wd]),
                             reads=[d_ps[p]], writes=[d_ot[k]])
                    r0 = t0 + tb * 128
                    c.dma("pool", out[r0:r0 + 128, :], ot[k][:], reads=[d_ot[k]])
        c.barrier()

    def mlstm_phase(self, qkT, tm, gates, hn, hsT):
        cfg, c = self.cfg, self.c
        DK, DV, DKC, DVC, S, B = cfg.DK, cfg.DV, cfg.DKC, cfg.DVC, cfg.S, 1
        L = 128
        NCH = S // L
        with ExitStack() as st:
            hnB, d_hn = self.load_f32(st, hn[:, :], [128, DV], "hnB")
            Cs = c.sb(st, [128, DKC, DV], F32, "C")
            Cb = c.sb(st, [128, DKC, DV], BF16, "Cb")
            ns = c.sb(st, [128, DKC, 1], F32, "n")
            nb = c.sb(st, [128, DKC, 1], BF16, "nb")
            d_C, d_Cb, d_n, d_nb = Dep(), Dep(), Dep(), Dep()
            NB = 2
            qk = [c.sb(st, [128, 2, DKC, L], BF16, "qk") for _ in range(NB)]
            tmt = [c.sb(st, [128, cfg.TMW], BF16, "tm") for _ in range(NB)]
            gt = [c.sb(st, [128, 2], F32, "g") for _ in range(NB)]
            d_qk = [Dep() for _ in range(NB)]
            d_tm = [Dep() for _ in range(NB)]
            d_g = [Dep() for _ in range(NB)]

            def T(shape, dt, name):
                return [c.sb(st, shape, dt, name) for _ in range(NB)], [Dep() for _ in range(NB)]
            lfB, d_lfB = T([128, 128], F32, "lfB")
            sm, d_sm = T([128, 16], F32, "sm")
            eB, d_eB = T([128, 128], F32, "eB")
            tmpm, d_tmpm = T([128, 128], F32, "tmpm")
            WT, d_WT = T([128, 128], F32, "WT")
            PT, d_PT = T([128, 128], BF16, "PT")
            qa, d_qa = T([128, DKC, 128], BF16, "qa")
            junk, d_junk = T([128, DV], F32, "junk")
            o1, d_o1 = T([128, DV], F32, "o1")
            o2, d_o2 = T([128, DV], BF16, "o2")
            hso, d_hso = T([128, DVC, 128], BF16, "hso")
            kw, d_kw = T([128, DK], BF16, "kw")
            ps_b = [c.ps(st, [128, 512], F32, "psb") for _ in range(2)]
            d_psb = [[Dep() for _ in range(4)] for _ in range(2)]
            ps_st = c.ps(st, [128, 512], F32, "psst")
            d_psst = Dep()
            ps_num = [c.ps(st, [128, 512], F32, "psn") for _ in range(2)]
            d_psn = [Dep(), Dep()]
            ps_tp = c.ps(st, [128, 1024], BF16, "pstp")
            d_pstp = Dep()
            ps_dc = [c.ps(st, [128, 512], F32, "psdc") for _ in range(2)]
            d_psdc = [Dep(), Dep()]
            qv = qkT.rearrange("(a j p) t -> p a j t", a=2, p=128)
            hv = hsT.rearrange("(j p) t -> p j t", p=128)
            dcst = [self.d_const, self.d_const_f]

            def load(gi):
                bb, ch = divmod(gi, NCH)
                r0 = bb * S + ch * L
                k = gi % NB
                c.dma("sp", qk[k][:], qv[:, :, :, r0:r0 + L], writes=[d_qk[k]])
                c.dma("sp", tmt[k][:], tm[r0:r0 + L, :], writes=[d_tm[k]])
                c.dma("sp", gt[k][:], gates[r0:r0 + L, :], writes=[d_g[k]])

            total = B * NCH
            load(0)
            for gi in range(total):
                bb, ch = divmod(gi, NCH)
                r0 = bb * S + ch * L
                k = gi % NB
                if gi + 1 < total:
                    load(gi + 1)
                if ch == 0:
                    c.op("pool", lambda e: e.memset(Cs[:], 0.0), writes=[d_C])
                    c.op("pool", lambda e: e.memset(Cb[:], 0.0), writes=[d_Cb])
                    c.op("pool", lambda e: e.memset(ns[:], 0.0), writes=[d_n])
                    c.op("pool", lambda e: e.memset(nb[:], 0.0), writes=[d_nb])
                pb = ps_b[k]
                dpb = d_psb[k]
                qT = qk[k][:, 0]
                kT = qk[k][:, 1]
                v = tmt[k][:, 0:DV]
                ogs = tmt[k][:, DV:2 * DV]
                kk = tmt[k][:, 2 * DV:2 * DV + DK]
                li = gt[k][:, 0:1]
                lf = gt[k][:, 1:2]
                s = sm[k]
                d_s = d_sm[k]
                c.op("dve", lambda e: e.tensor_scalar(lfB[k][:], self.onesf, lf, None, ALU.mult),
                     reads=[d_g[k]] + dcst, writes=[d_lfB[k]])
                c.op("pe", lambda e: e.matmul(pb[:, 0:128], lfB[k][:], self.Uf, start=True, stop=True),
                     reads=[d_lfB[k]] + dcst, writes=[dpb[0]])
                c.op("pe", lambda e: e.matmul(pb[:, 128:129], self.Uf, lf, start=True, stop=True),
                     reads=[d_g[k]] + dcst, writes=[dpb[1]])
                c.op("dve", lambda e: e.tensor_tensor(s[:, 0:1], li, pb[:, 128:129], ALU.subtract),
                     reads=[d_g[k], dpb[1]], writes=[d_s])
                c.op("act", lambda e: e.copy(out=s[:, 1:2], in_=pb[:, 127:128]), reads=[dpb[0]], writes=[d_s])
                c.op("act", lambda e: e.activation(out=s[:, 2:3], in_=pb[:, 127:128], func=AF.Exp), reads=[dpb[0]], writes=[d_s])
                c.op("act", lambda e: e.activation(out=s[:, 3:4], in_=s[:, 0:1], func=AF.Exp, bias=s[:, 1:2]),
                     reads=[d_s], writes=[d_s])
                c.op("act", lambda e: e.activation(out=eB[k][:], in_=pb[:, 0:128], func=AF.Exp), reads=[dpb[0]], writes=[d_eB[k]])
                c.op("dve", lambda e: e.tensor_tensor(tmpm[k][:], pb[:, 0:128], self.negmask, ALU.add),
                     reads=[dpb[0]] + dcst, writes=[d_tmpm[k]])
                c.op("act", lambda e: e.activation(out=WT[k][:], in_=tmpm[k][:], func=AF.Exp, bias=s[:, 0:1]),
                     reads=[d_tmpm[k], d_s], writes=[d_WT[k]])
                for j in range(DKC):
                    c.op("pe", lambda e: e.matmul(ps_st[:, 0:128], kT[:, j, :], qT[:, j, :], start=(j == 0), stop=(j == DKC - 1)),
                         reads=[d_qk[k]], writes=[d_psst])
                c.op("dve", lambda e: e.tensor_tensor(PT[k][:], ps_st[:, 0:128], WT[k][:], ALU.mult),
                     reads=[d_psst, d_WT[k]], writes=[d_PT[k]])
                for j in range(DKC):
                    c.op("pool", lambda e: e.tensor_tensor(qa[k][:, j, :], qT[:, j, :], eB[k][:], ALU.mult),
                         reads=[d_qk[k], d_eB[k]], writes=[d_qa[k]])
                pn = ps_num[k]
                for j in range(DKC):
                    c.op("pe", lambda e: e.matmul(pn[:, 0:DV], qa[k][:, j, :], Cb[:, j, :], start=(j == 0), stop=False),
                         reads=[d_qa[k], d_Cb], writes=[d_psn[k]])
                c.op("pe", lambda e: e.matmul(pn[:, 0:DV], PT[k][:], v, start=False, stop=True),
                     reads=[d_PT[k], d_tm[k]], writes=[d_psn[k]])
                for j in range(DKC):
                    c.op("pe", lambda e: e.matmul(pb[:, 132:133], qa[k][:, j, :], nb[:, j, :], start=(j == 0), stop=False),
                         reads=[d_qa[k], d_nb], writes=[dpb[2]])
                c.op("pe", lambda e: e.matmul(pb[:, 132:133], PT[k][:], self.ones_bf[:, 0:1], start=False, stop=True),
                     reads=[d_PT[k]] + dcst, writes=[dpb[2]])
                c.op("act", lambda e: e.activation(out=s[:, 4:5], in_=pb[:, 132:133], func=AF.Abs),
                     reads=[dpb[2]], writes=[d_s])
                c.op("dve", lambda e: e.tensor_scalar(s[:, 4:5], s[:, 4:5], 1.0, None, ALU.max),
                     reads=[d_s], writes=[d_s])
                c.op("dve", lambda e: e.reciprocal(s[:, 5:6], s[:, 4:5]), reads=[d_s], writes=[d_s])
                c.op("pool", lambda e: e.memset(s[:, 6:7], 0.0), writes=[d_s])
                c.op("act", lambda e: e.activation(out=junk[k][:], in_=pn[:, 0:DV], func=AF.Square, scale=s[:, 5:6],
                                                   accum_out=s[:, 6:7]),
                     reads=[d_psn[k], d_s], writes=[d_junk[k], d_s])
                c.op("act", lambda e: e.activation(out=s[:, 7:8], in_=s[:, 6:7], func=AF.Sqrt, bias=self.eps[:, 0:1], scale=1.0 / DV),
                     reads=[d_s, self.d_eps], writes=[d_s])
                c.op("dve", lambda e: e.reciprocal(s[:, 7:8], s[:, 7:8]), reads=[d_s], writes=[d_s])
                c.op("dve", lambda e: e.tensor_tensor(s[:, 8:9], s[:, 7:8], s[:, 5:6], ALU.mult), reads=[d_s], writes=[d_s])
                c.op("dve", lambda e: e.scalar_tensor_tensor(o1[k][:], pn[:, 0:DV], s[:, 8:9], hnB[:], ALU.mult, ALU.mult),
                     reads=[d_psn[k], d_s, d_hn], writes=[d_o1[k]])
                c.op("pool", lambda e: e.tensor_tensor(o2[k][:], o1[k][:], ogs, ALU.mult),
                     reads=[d_o1[k], d_tm[k]], writes=[d_o2[k]])
                for jj in range(DVC):
                    c.op("pe", lambda e: e.transpose(ps_tp[:, jj * 128:(jj + 1) * 128], o2[k][:, jj * 128:(jj + 1) * 128], self.ident_bf),
                         reads=[d_o2[k]] + dcst, writes=[d_pstp])
                c.op("act", lambda e: e.copy(out=hso[k][:], in_=ps_tp[:, 0:DVC * 128].rearrange("p (j t) -> p j t", j=DVC)),
                     reads=[d_pstp], writes=[d_hso[k]])
                c.dma("pool", hv[:, :, r0:r0 + L], hso[k][:], reads=[d_hso[k]])
                c.op("dve", lambda e: e.tensor_scalar(kw[k][:], kk, s[:, 3:4], None, ALU.mult),
                     reads=[d_tm[k], d_s], writes=[d_kw[k]])
                for j in range(DKC):
                    pd = ps_dc[j % 2]
                    c.op("pe", lambda e: e.matmul(pd[:, 0:DV], kw[k][:, j * 128:(j + 1) * 128], v, start=True, stop=True),
                         reads=[d_kw[k], d_tm[k]], writes=[d_psdc[j % 2]])
                    c.op("pe", lambda e: e.matmul(pb[:, 136 + j:137 + j], kw[k][:, j * 128:(j + 1) * 128], self.ones_bf[:, 0:1],
                                                  start=True, stop=True),
                         reads=[d_kw[k]] + dcst, writes=[dpb[3]])
                    c.op("dve", lambda e: e.scalar_tensor_tensor(Cs[:, j, :], Cs[:, j, :], s[:, 2:3], pd[:, 0:DV], ALU.mult, ALU.add),
                         reads=[d_s, d_psdc[j % 2]], writes=[d_C])
                    c.op("dve", lambda e: e.scalar_tensor_tensor(ns[:, j, :], ns[:, j, :], s[:, 2:3], pb[:, 136 + j:137 + j], ALU.mult, ALU.add),
                         reads=[d_s, dpb[3]], writes=[d_n])
                c.op("act", lambda e: e.copy(out=Cb[:], in_=Cs[:]), reads=[d_C], writes=[d_Cb])
                c.op("act", lambda e: e.copy(out=nb[:], in_=ns[:]), reads=[d_n], writes=[d_nb])
        c.barrier()

    def attn_phase(self, qT, kT, vtm, attT):
        cfg, c = self.cfg, self.c
        S, B, HPC = cfg.S, cfg.B, cfg.HPC
        NB = S // 128
        NSB = S // 512
        with ExitStack() as st:
            qt = [c.sb(st, [128, S], BF16, "aq") for _ in range(2)]
            kt = [c.sb(st, [128, S], BF16, "ak") for _ in range(2)]
            vt = [c.sb(st, [128, NB, 128], BF16, "av") for _ in range(2)]
            d_q = [Dep(), Dep()]
            d_k = [Dep(), Dep()]
            d_v = [Dep(), Dep()]

            def T(n, shape, dt, name):
                return [c.sb(st, shape, dt, name) for _ in range(n)], [Dep() for _ in range(n)]
            ee, d_ee = T(2, [128, 512], F32, "ee")
            Lp, d_Lp = T(2, [128, 512], BF16, "Lp")
            Lm, d_Lm = T(2, [128, 512], BF16, "Lm")
            arg, d_arg = T(2, [128, 512], F32, "arg")
            R, d_R = T(2, [128, 512], F32, "R")
            A, d_A = T(2, [128, 512], BF16, "A")
            Am, d_Am = T(2, [128, 512], BF16, "Am")
            osb, d_osb = T(2, [128, 512], BF16, "osb")
            ps_z = [c.ps(st, [128, 512], F32, "psz") for _ in range(2)]
            d_psz = [Dep(), Dep()]
            ps_1 = [c.ps(st, [128, 512], F32, "ps1") for _ in range(2)]
            d_ps1 = [Dep(), Dep()]
            ps_2 = [c.ps(st, [128, 512], F32, "ps2") for _ in range(2)]
            d_ps2 = [Dep(), Dep()]
            ps_o = [c.ps(st, [128, 512], F32, "pso") for _ in range(2)]
            d_pso = [Dep(), Dep()]
            dcst = [self.d_const]
            units = [(0, hh) for hh in range(HPC)]

            def load(ui):
                bb, hh = units[ui]
                k = ui % 2
                c.dma("sp", qt[k][:], qT[hh * 128:(hh + 1) * 128, bb * S:(bb + 1) * S], writes=[d_q[k]])
                c.dma("sp", kt[k][:], kT[hh * 128:(hh + 1) * 128, bb * S:(bb + 1) * S], writes=[d_k[k]])
                c.dma("sp", vt[k][:], vtm[bb * S:(bb + 1) * S, hh * 128:(hh + 1) * 128].rearrange("(j p) d -> p j d", p=128),
                      writes=[d_v[k]])
            tiles = []
            gi = 0
            for ui, (bb, hh) in enumerate(units):
                for I in range(NSB):
                    Jtop = 4 * I + 3
                    for J in range(Jtop, -1, -1):
                        tiles.append(dict(ui=ui, u=ui % 2, bb=bb, hh=hh, I=I, J=J, first=(J == Jtop), last=(J == 0),
                                          diag=(J >= 4 * I), jj=J - 4 * I, o=gi % 2))
                    gi += 1
            NBUF = 3
            ee3, d_ee3 = T(NBUF, [128, 512], F32, "ee3")
            Lp3, d_Lp3 = T(NBUF, [128, 512], BF16, "Lp3")
            Lm3, d_Lm3 = T(NBUF, [128, 512], BF16, "Lm3")
            A3, d_A3 = T(NBUF, [128, 512], BF16, "A3")
            Am3, d_Am3 = T(NBUF, [128, 512], BF16, "Am3")
            loaded = set()
            rstate = {"prev": None}

            def qs_ks(t):
                return (qt[t["u"]][:, t["I"] * 512:(t["I"] + 1) * 512], kt[t["u"]][:, t["J"] * 128:(t["J"] + 1) * 128])

            def stageA(n):
                t = tiles[n]
                if t["ui"] not in loaded:
                    loaded.add(t["ui"])
                    if t["ui"] == 0:
                        load(0)
                    if t["ui"] + 1 < len(units):
                        load(t["ui"] + 1)
                u, x, m = t["u"], n % 2, n % NBUF
                qs, ks = qs_ks(t)
                c.op("pe", lambda e: e.matmul(ps_z[x][:], ks, qs, start=True, stop=True),
                     reads=[d_q[u], d_k[u]], writes=[d_psz[x]])
                c.op("act", lambda e: e.activation(out=ee3[m][:], in_=ps_z[x][:], func=AF.Exp),
                     reads=[d_psz[x]], writes=[d_ee3[m]])
                c.op("act", lambda e: e.activation(out=Lp3[m][:], in_=ee3[m][:], func=AF.Ln, bias=1.0),
                     reads=[d_ee3[m]], writes=[d_Lp3[m]])
                if t["diag"]:
                    c.op("pool", lambda e: e.tensor_tensor(Lm3[m][:], Lp3[m][:], self.amask_bf[t["jj"]], ALU.mult),
                         reads=[d_Lp3[m]] + dcst, writes=[d_Lm3[m]])

            def stageB(n):
                t = tiles[n]
                u, x, m = t["u"], n % 2, n % NBUF
                qs, ks = qs_ks(t)
                lm, d_lm = (Lm3[m], d_Lm3[m]) if t["diag"] else (Lp3[m], d_Lp3[m])
                c.op("pe", lambda e: e.matmul(ps_1[x][:], ks, qs, start=True, stop=False),
                     reads=[d_q[u], d_k[u]], writes=[d_ps1[x]])
                c.op("pe", lambda e: e.matmul(ps_1[x][:], self.negSLE_bf, lm[:], start=False, stop=True),
                     reads=[d_lm] + dcst, writes=[d_ps1[x]])
                if not t["last"]:
                    c.op("pe", lambda e: e.matmul(ps_2[x][:], self.negones_bf, lm[:], start=True, stop=True),
                         reads=[d_lm] + dcst, writes=[d_ps2[x]])
                if t["first"]:
                    c.op("act", lambda e: e.activation(out=A3[m][:], in_=ps_1[x][:], func=AF.Exp),
                         reads=[d_ps1[x]], writes=[d_A3[m]])
                    rn = 0
                    if not t["last"]:
                        c.op("dve", lambda e: e.tensor_copy(R[rn][:], ps_2[x][:]), reads=[d_ps2[x]], writes=[d_R[rn]])
                else:
                    rp = rstate["prev"]
                    c.op("dve", lambda e: e.tensor_tensor(arg[x][:], ps_1[x][:], R[rp][:], ALU.add),
                         reads=[d_ps1[x], d_R[rp]], writes=[d_arg[x]])
                    c.op("act", lambda e: e.activation(out=A3[m][:], in_=arg[x][:], func=AF.Exp),
                         reads=[d_arg[x]], writes=[d_A3[m]])
                    rn = 1 - rp
                    if not t["last"]:
                        c.op("dve", lambda e: e.tensor_tensor(R[rn][:], ps_2[x][:], R[rp][:], ALU.add),
                             reads=[d_ps2[x], d_R[rp]], writes=[d_R[rn]])
                rstate["prev"] = rn
                if t["diag"]:
                    c.op("pool", lambda e: e.tensor_tensor(Am3[m][:], A3[m][:], self.amask_bf[t["jj"]], ALU.mult),
                         reads=[d_A3[m]] + dcst, writes=[d_Am3[m]])

            def stageC(n):
                t = tiles[n]
                u, m, o = t["u"], n % NBUF, t["o"]
                am, d_am = (Am3[m], d_Am3[m]) if t["diag"] else (A3[m], d_A3[m])
                c.op("pe", lambda e: e.matmul(ps_o[o][:], vt[u][:, t["J"], :], am[:], start=t["first"], stop=t["last"]),
                     reads=[d_v[u], d_am], writes=[d_pso[o]])
                if t["last"]:
                    bb, hh, I = t["bb"], t["hh"], t["I"]
                    c.op("act", lambda e: e.copy(out=osb[o][:], in_=ps_o[o][:]), reads=[d_pso[o]], writes=[d_osb[o]])
                    c.dma("pool", attT[hh * 128:(hh + 1) * 128, bb * S + I * 512:bb * S + (I + 1) * 512], osb[o][:],
                          reads=[d_osb[o]])

            NTL = len(tiles)
            for n in range(NTL + 2):
                if n < NTL:
                    stageA(n)
                if 0 <= n - 1 < NTL:
                    stageB(n - 1)
                if 0 <= n - 2 < NTL:
                    stageC(n - 2)
        c.barrier()

    def ffn_up_phase(self, xT, Wup, cw, aT, TT=512, G=3):
        cfg, c = self.cfg, self.c
        KC, NT, S, FCH = cfg.KC, cfg.NT, cfg.S, cfg.FCH
        cwt_stack = ExitStack()
        cwt, d_cw = self.load_f32(cwt_stack, cw[:, :], [128, FCH * 8], "cw")
        xv = xT.rearrange("(kc p) t -> p kc t", p=128)
        for g0 in range(0, FCH, G):
            g1 = min(FCH, g0 + G)
            ng = g1 - g0
            with ExitStack() as st:
                wt, d_w = self.load_w(st, Wup[:, g0 * 256:g1 * 256], cfg.D, ng * 256, "wup")
                xt = [c.sb(st, [128, KC, TT], BF16, "xt") for _ in range(2)]
                d_xt = [Dep(), Dep()]
                pss = [c.ps(st, [128, 512], F32, "fps") for _ in range(4)]
                d_ps = [Dep() for _ in range(4)]
                ub = [[c.sb(st, [128, TT + 2], F32, "u") for _ in range(2)] for _ in range(ng)]
                d_ub = [[Dep(), Dep()] for _ in range(ng)]
                cv = [[c.sb(st, [128, TT], F32, "cv") for _ in range(2)] for _ in range(2)]
                d_cv = [[Dep(), Dep()] for _ in range(2)]
                sg = [c.sb(st, [128, TT], F32, "sg") for _ in range(2)]
                d_sg = [Dep(), Dep()]
                ao = [c.sb(st, [128, TT], BF16, "ao") for _ in range(2)]
                d_ao = [Dep(), Dep()]
                nt = NT // TT
                ip = io = 0
                c.dma("sp", xt[0][:], xv[:, :, 0:TT], writes=[d_xt[0]])
                for it in range(nt):
                    b = it % 2
                    t0 = it * TT
                    seq_start = (t0 % S == 0)
                    if it + 1 < nt:
                        c.dma("sp", xt[1 - b][:], xv[:, :, t0 + TT:t0 + 2 * TT], writes=[d_xt[1 - b]])
                    for ch in range(ng):
                        k = io % 2
                        io += 1
                        for gv in range(2):
                            p = ip % 4
                            ip += 1
                            c0 = ch * 256 + gv * 128
                            for kc in range(KC):
                                c.op("pe", lambda e: e.matmul(pss[p][:, :TT], wt[:, kc, c0:c0 + 128], xt[b][:, kc, :],
                                                              start=(kc == 0), stop=(kc == KC - 1)),
                                     reads=[d_w, d_xt[b]], writes=[d_ps[p]])
                            u, d_u = ub[ch][gv], d_ub[ch][gv]
                            if seq_start:
                                c.op("pool", lambda e: e.memset(u[:, 0:2], 0.0), writes=[d_u])
                            else:
                                c.op("pool", lambda e: e.tensor_copy(u[:, 0:2], u[:, TT:TT + 2]), reads=[d_u], writes=[d_u])
                            c.op("act", lambda e: e.copy(out=u[:, 2:TT + 2], in_=pss[p][:, :TT]), reads=[d_ps[p]], writes=[d_u])
                            base = ((g0 + ch) * 2 + gv) * 4
                            cc, d_cc = cv[gv][k], d_cv[gv][k]
                            c.op("dve", lambda e: e.tensor_scalar(cc[:], u[:, 2:TT + 2], cwt[:, base + 2:base + 3],
                                                                  cwt[:, base + 3:base + 4], ALU.mult, ALU.add),
                                 reads=[d_u, d_cw], writes=[d_cc])
                            c.op("dve", lambda e: e.scalar_tensor_tensor(cc[:], u[:, 1:TT + 1], cwt[:, base + 1:base + 2], cc[:],
                                                                         ALU.mult, ALU.add),
                                 reads=[d_u, d_cw], writes=[d_cc])
                            c.op("dve", lambda e: e.scalar_tensor_tensor(cc[:], u[:, 0:TT], cwt[:, base + 0:base + 1], cc[:],
                                                                         ALU.mult, ALU.add),
                                 reads=[d_u, d_cw], writes=[d_cc])
                        c.op("act", lambda e: e.activation(out=sg[k][:], in_=cv[0][k][:], func=AF.Silu),
                             reads=[d_cv[0][k]], writes=[d_sg[k]])
                        c.op("pool", lambda e: e.tensor_tensor(ao[k][:], sg[k][:], cv[1][k][:], ALU.mult),
                             reads=[d_sg[k], d_cv[1][k]], writes=[d_ao[k]])
                        r0 = (g0 + ch) * 128
                        c.dma("pool", aT[r0:r0 + 128, t0:t0 + TT], ao[k][:], reads=[d_ao[k]])
            c.barrier()
        cwt_stack.close()

    def ple_phase(self, xT, Wg, Wpe, pT, hT, out, TT=512):
        cfg, c = self.cfg, self.c
        KC, NT, DC = cfg.KC, cfg.NT, cfg.DC
        xv = xT.rearrange("(kc p) t -> p kc t", p=128)
        pv = pT.rearrange("(kc p) t -> p kc t", p=128)
        if cfg.TP == 1:
            own = hT
        else:
            if getattr(self, "_own_hT", None) is None:
                pid = self.nc.sync.partition_id()
                self._own_hT = hT[bass.ds((pid % cfg.TP) * DC, DC), :]
            c.dma("sp", self.h_own[:, :], self._own_hT[:, :])
            c.barrier()
            own = self.h_own
        for g0 in range(0, DC, 1024):
            g1 = min(DC, g0 + 1024)
            nj = (g1 - g0) // 128
            with ExitStack() as st:
                wg, d_wg = self.load_w(st, Wg[:, g0:g1], cfg.D, g1 - g0, "wg")
                wp, d_wp = self.load_w(st, Wpe[:, g0:g1], 256, g1 - g0, "wp")
                xt = [c.sb(st, [128, KC, TT], BF16, "xt") for _ in range(2)]
                d_xt = [Dep(), Dep()]
                pt = [c.sb(st, [128, 2, TT], BF16, "pt") for _ in range(2)]
                d_pt = [Dep(), Dep()]
                hr = [c.sb(st, [128, nj, TT], F32, "hr") for _ in range(2)]
                d_hrj = [[Dep() for _ in range(nj)] for _ in range(2)]
                pss = [c.ps(st, [128, 512], F32, "pps") for _ in range(4)]
                d_ps = [Dep() for _ in range(4)]
                gs = [c.sb(st, [128, TT], F32, "gs") for _ in range(2)]
                d_gs = [Dep(), Dep()]
                ot = [c.sb(st, [128, TT], F32, "po") for _ in range(2)]
                d_ot = [Dep(), Dep()]
                nt = NT // TT
                ip = io = 0

                def load(it):
                    b = it % 2
                    t0 = it * TT
                    c.dma("sp", xt[b][:], xv[:, :, t0:t0 + TT], writes=[d_xt[b]])
                    c.dma("pool", pt[b][:], pv[:, :, t0:t0 + TT], writes=[d_pt[b]])
                    for j in range(nj):
                        src = own[g0 + j * 128:g0 + (j + 1) * 128, t0:t0 + TT]
                        c.dma("sp", hr[b][:, j, :], src, writes=[d_hrj[b][j]])
                load(0)
                for it in range(nt):
                    b = it % 2
                    t0 = it * TT
                    if it + 1 < nt:
                        load(it + 1)
                    for j in range(nj):
                        pg = ip % 4
                        pe_ = (ip + 1) % 4
                        ip += 2
                        k = io % 2
                        io += 1
                        for kc in range(KC):
                            c.op("pe", lambda e: e.matmul(pss[pg][:, :TT], wg[:, kc, j * 128:(j + 1) * 128], xt[b][:, kc, :],
                                                          start=(kc == 0), stop=(kc == KC - 1)),
                                 reads=[d_wg, d_xt[b]], writes=[d_ps[pg]])
                        for kc in range(2):
                            c.op("pe", lambda e: e.matmul(pss[pe_][:, :TT], wp[:, kc, j * 128:(j + 1) * 128], pt[b][:, kc, :],
                                                          start=(kc == 0), stop=(kc == 1)),
                                 reads=[d_wp, d_pt[b]], writes=[d_ps[pe_]])
                        c.op("act", lambda e: e.activation(out=gs[k][:], in_=pss[pg][:, :TT], func=AF.Sigmoid),
                             reads=[d_ps[pg]], writes=[d_gs[k]])
                        c.op("dve", lambda e: e.tensor_tensor(ot[k][:], pss[pe_][:, :TT], gs[k][:], ALU.mult),
                             reads=[d_ps[pe_], d_gs[k]], writes=[d_ot[k]])
                        c.op("dve", lambda e: e.tensor_tensor(ot[k][:], ot[k][:], hr[b][:, j, :], ALU.add),
                             reads=[d_hrj[b][j]], writes=[d_ot[k]])
                        c.dma("pool", out[g0 + j * 128:g0 + (j + 1) * 128, t0:t0 + TT], ot[k][:], reads=[d_ot[k]])
            c.barrier()

    def build(self, stop_after=None):
        cfg, c, nc = self.cfg, self.c, self.nc
        D, NT, KC, DC, TP = cfg.D, cfg.NT, cfg.KC, cfg.DC, cfg.TP
        self.setup()
        xs = self.inp("xs", [D, NT])
        ps_in = self.inp("ps", [cfg.depth * 256, NT])
        y = nc.dram_tensor("y", [DC, NT], F32, kind="ExternalOutput").ap()
        hT = self.scratch("hT", [D, NT], F32, force_internal=True)
        xs_i = self.scratch("xs_i", [DC, NT], F32, force_internal=True)
        ps_i = self.scratch("ps_i", [cfg.PR, NT], F32, force_internal=True)
        pT = ps_in
        xnT = self.scratch("xnT", [D, NT], BF16)
        part2 = [self.scratch(f"part{i}", [D, NT // 2], F32, force_internal=True) for i in range(2)]
        red2 = part2 if TP == 1 else [self.scratch(f"red{i}", [D, NT // 2], F32, force_internal=True) for i in range(2)]
        ple_in = self.scratch("ple_in", [DC, NT], F32, force_internal=True)
        agtmp = self.scratch("agtmp", [D, NT], F32, force_internal=True)
        self.h_own = self.scratch("h_own", [DC, NT], F32, force_internal=True)
        qkT = self.scratch("qkT", [2 * cfg.DK, NT], BF16)
        tmA = self.scratch("tmA", [NT, cfg.TMW], BF16)
        gatesA = self.scratch("gatesA", [NT, 2], F32)
        hsT = self.scratch("hsT", [cfg.HA * cfg.DV, NT], BF16)
        aT = self.scratch("aT", [cfg.FP, NT], BF16)
        kshT = self.scratch("kshT", [cfg.AW, NT], BF16)
        vsh = self.scratch("vsh", [NT, cfg.AW], BF16)
        qT = self.scratch("qT", [cfg.AW, NT], BF16)
        attT = self.scratch("attT", [cfg.AW, NT], BF16)

        def snapshot(name):
            if name in self.debug:
                t = nc.dram_tensor(name, [D, NT], F32, kind="ExternalOutput").ap()
                c.dma("sp", t[:, :], hT[:, :])
                c.barrier()

        def allreduce():
            if TP > 1:
                c.collective("AllReduce", ALU.add, part[:, :], red[:, :])

        for r0 in range(0, D, 512):
            c.dma("sp", hT[r0:r0 + 512, :], xs[r0:r0 + 512, :])
        c.barrier()
        snapshot("h_in")

        def gvec(name):
            return self.inp(name, [128, KC])

        for layer in range(cfg.depth):
            L = f"{layer}"
            if layer < cfg.NA:
                self.norm_phase(hT, None, None, gvec("a_norm_pre" + L), xnT)
                for hl in range(cfg.HA):
                    Lh = f"{layer}_{hl}"
                    wqk = self.inp("a_wqk" + Lh, [D, 2 * cfg.DK])
                    with ExitStack() as st:
                        ep = self.store_epilogue(st, qkT, BF16, scale=lambda c0: (cfg.DK ** -0.5 if c0 < cfg.DK else 1.0))
                        self.lin_fm(xnT, D, wqk, [(i * 128, 128) for i in range(2 * cfg.DKC)], ep)
                    self.lin_tm_A(xnT, self.inp("a_wtm" + Lh, [D, cfg.TMN]), self.inp("a_gb" + Lh, [128, 2]), tmA, gatesA)
                    self.mlstm_phase(qkT, tmA, gatesA, self.inp("a_hn" + Lh, [128, cfg.DV]),
                                     hsT[hl * cfg.DV:(hl + 1) * cfg.DV, :])
                mixK, mixX, wo = cfg.HA * cfg.DV, hsT, self.inp("a_wout" + L, [cfg.HA * cfg.DV, D])
                gpost = gvec("a_norm_post" + L)
            else:
                j = layer - cfg.NA
                if j == 0:
                    self.norm_phase(hT, None, None, gvec("kv_norm"), xnT)
                    with ExitStack() as st:
                        ep = self.store_epilogue(st, kshT, BF16)
                        self.lin_fm(xnT, D, self.inp("kv_wk", [D, cfg.AW]), [(i * 128, 128) for i in range(cfg.HPC)], ep)
                    self.lin_tm_plain(xnT, self.inp("kv_wv", [D, cfg.AW]), cfg.AW, vsh)
                self.norm_phase(hT, None, None, gvec("b_norm_pre" + L), xnT)
                with ExitStack() as st:
                    ep = self.store_epilogue(st, qT, BF16, scale=lambda c0: 128 ** -0.5)
                    self.lin_fm(xnT, D, self.inp("b_wq" + L, [D, cfg.AW]), [(i * 128, 128) for i in range(cfg.HPC)], ep)
                self.attn_phase(qT, kshT, vsh, attT)
                mixK, mixX, wo = cfg.AW, attT, self.inp("b_wout" + L, [cfg.AW, D])
                gpost = gvec("b_norm_post" + L)
            if stop_after == ("mix", layer):
                break
            cnts = self.lin_fm_ar(mixX, mixK, wo, part2, red2)
            self.norm_phase([(red2[0], cnts[0]), (red2[1], cnts[1])], gpost, hT, gvec("f_norm_pre" + L), xnT)
            snapshot("h_mix" + L)
            if stop_after == ("n2", layer):
                break
            self.ffn_up_phase(xnT, self.inp("f_wup" + L, [D, cfg.FCH * 256]), self.inp("f_cw" + L, [128, cfg.FCH * 8]), aT)
            cnts = self.lin_fm_ar(aT, cfg.FP, self.inp("f_wdown" + L, [cfg.FP, D]), part2, red2,
                                  TT=(512 if cfg.FCH <= 32 else 256))
            self.norm_phase([(red2[0], cnts[0]), (red2[1], cnts[1])], gvec("f_norm_post" + L), hT,
                            gvec("ple_gate_norm" + L), xnT)
            snapshot("h_ffn" + L)
            if stop_after == ("n3", layer):
                break
            last = layer == cfg.depth - 1
            if last:
                dst = y
            elif TP == 1:
                dst = hT
            else:
                dst = ple_in
            self.ple_phase(xnT, self.inp("ple_gw" + L, [D, DC]), self.inp("ple_w" + L, [256, DC]),
                           pT[layer * 256:(layer + 1) * 256, :], hT, dst)
            if not last:
                if TP > 1:
                    c.collective("AllGather", ALU.bypass, ple_in[:, :], hT[:, :], agtmp)
                snapshot("h_out" + L)
            if stop_after == ("layer", layer):
                break
        c.barrier()
        self.gs.close()
        c.es.close()
        return nc


def _gv(v, KC):
    return np.ascontiguousarray(np.asarray(v, np.float32).reshape(KC, 128).T)


def shard_inputs(cfg, inp):
    D, NT, KC, DC, TP = cfg.D, cfg.NT, cfg.KC, cfg.DC, cfg.TP
    DK, DV, QKW, VW, HA = cfg.DK, cfg.DV, cfg.QKW, cfg.VW, cfg.HA
    f = lambda a: np.asarray(a, np.float32)
    x = f(inp["x"])
    p = f(inp["p"])
    consts = consts_array()
    shared = []
    for r in range(TP):
        m = {"consts": consts}
        for layer in range(cfg.depth):
            L = f"{layer}"
            if layer < cfg.NA:
                a = layer
                w_in = f(inp["a_w_in"][a])
                gb = f(inp["a_gate_bias"][a])
                hn = f(inp["a_head_norm"][a])
                for hl in range(HA):
                    h = r * HA + hl
                    Lh = f"{layer}_{hl}"
                    q = w_in[:, h * DK:(h + 1) * DK]
                    k = w_in[:, QKW + h * DK:QKW + (h + 1) * DK]
                    v = w_in[:, 2 * QKW + h * DV:2 * QKW + (h + 1) * DV]
                    og = w_in[:, 2 * QKW + VW + h * DV:2 * QKW + VW + (h + 1) * DV]
                    ig = w_in[:, 2 * QKW + 2 * VW + h:2 * QKW + 2 * VW + h + 1]
                    fg = w_in[:, 2 * QKW + 2 * VW + 8 + h:2 * QKW + 2 * VW + 8 + h + 1]
                    m["a_wqk" + Lh] = np.ascontiguousarray(np.concatenate([q, k], 1))
                    m["a_wtm" + Lh] = np.ascontiguousarray(np.concatenate([v, og, k, ig, fg], 1))
                    m["a_gb" + Lh] = np.ascontiguousarray(np.broadcast_to(np.array([gb[h], gb[8 + h]], np.float32)[None, :], (128, 2)))
                    m["a_hn" + Lh] = np.ascontiguousarray(np.broadcast_to(hn[h * DV:(h + 1) * DV][None, :], (128, DV)))
                m["a_norm_pre" + L] = _gv(inp["a_norm_pre"][a], KC)
                m["a_wout" + L] = np.ascontiguousarray(f(inp["a_w_out"][a])[r * HA * DV:(r + 1) * HA * DV, :])
                m["a_norm_post" + L] = _gv(inp["a_norm_post"][a], KC)
            else:
                j = layer - cfg.NA
                AW = cfg.AW
                if j == 0:
                    m["kv_norm"] = _gv(inp["kv_norm"], KC)
                    kvw = f(inp["kv_w"])
                    m["kv_wk"] = np.ascontiguousarray(kvw[:, r * AW:(r + 1) * AW])
                    m["kv_wv"] = np.ascontiguousarray(kvw[:, cfg.H * 128 + r * AW:cfg.H * 128 + (r + 1) * AW])
                m["b_norm_pre" + L] = _gv(inp["b_norm_pre"][j], KC)
                m["b_wq" + L] = np.ascontiguousarray(f(inp["b_w_q"][j])[:, r * AW:(r + 1) * AW])
                m["b_wout" + L] = np.ascontiguousarray(f(inp["b_w_out"][j])[r * AW:(r + 1) * AW, :])
                m["b_norm_post" + L] = _gv(inp["b_norm_post"][j], KC)
            F, FC, FCH, FP = cfg.F, cfg.FC, cfg.FCH, cfg.FP
            m["f_norm_pre" + L] = _gv(inp["f_norm_pre"][layer], KC)
            wup = f(inp["f_w_up"][layer])
            wu = np.zeros((D, FCH, 2, 128), np.float32)
            gpad = np.zeros((D, FP), np.float32)
            vpad = np.zeros((D, FP), np.float32)
            gpad[:, :FC] = wup[:, r * FC:(r + 1) * FC]
            vpad[:, :FC] = wup[:, F + r * FC:F + (r + 1) * FC]
            wu[:, :, 0, :] = gpad.reshape(D, FCH, 128)
            wu[:, :, 1, :] = vpad.reshape(D, FCH, 128)
            del gpad, vpad
            m["f_wup" + L] = wu.reshape(D, FCH * 256)
            cwl = np.zeros((FCH, 128, 2, 4), np.float32)
            convw = f(inp["f_conv_w"][layer])
            convb = f(inp["f_conv_b"][layer])
            for gv in range(2):
                sl = slice(gv * F + r * FC, gv * F + (r + 1) * FC)
                tmp = np.zeros((FP, 4), np.float32)
                tmp[:FC, 0:3] = convw[:, sl].T
                tmp[:FC, 3] = convb[sl]
                cwl[:, :, gv, :] = tmp.reshape(FCH, 128, 4)
            m["f_cw" + L] = np.ascontiguousarray(cwl.transpose(1, 0, 2, 3)).reshape(128, FCH * 8)
            wd = np.zeros((FP, D), np.float32)
            wd[:FC] = f(inp["f_w_down"][layer])[r * FC:(r + 1) * FC]
            m["f_wdown" + L] = wd
            m["f_norm_post" + L] = _gv(inp["f_norm_post"][layer], KC)
            m["ple_gate_norm" + L] = _gv(inp["ple_gate_norm"][layer], KC)
            m["ple_gw" + L] = np.ascontiguousarray(f(inp["ple_gate_w"][layer])[:, r * DC:(r + 1) * DC])
            m["ple_w" + L] = np.ascontiguousarray(f(inp["ple_w"][layer])[:, r * DC:(r + 1) * DC])
        shared.append(m)
    maps = []
    for g in range(cfg.B):
        xT = np.ascontiguousarray(x[g].T)
        pT = np.ascontiguousarray(p[:, g].transpose(0, 2, 1)).reshape(cfg.depth * 256, NT)
        for r in range(TP):
            m = dict(shared[r])
            m["xs"] = xT
            m["ps"] = pT
            maps.append(m)
    return maps


def run(cfg, inputs, debug=(), stop_after=None, trace=False):
    b = Builder(cfg, debug=debug)
    nc = b.build(stop_after=stop_after)
    maps = shard_inputs(cfg, inputs)
    maps = [{k: v for k, v in m.items() if k in b.inputs} for m in maps]
    for m in maps:
        for k, shp in b.inputs.items():
            assert m[k].shape == shp, (k, m[k].shape, shp)
    res = run_bass_kernel_spmd(nc, maps, core_ids=list(range(cfg.NCU)), trace=trace)
    return res


def assemble(cfg, results, key="y"):
    out = np.empty((cfg.B, cfg.S, cfg.D), np.float32)
    for g in range(cfg.B):
        for r in range(cfg.TP):
            out[g, :, r * cfg.DC:(r + 1) * cfg.DC] = results[g * cfg.TP + r][key].T
    return out


def kernel(**inputs):
    cfg = Cfg()
    res = run(cfg, inputs)
    return assemble(cfg, res.results)
```

```python
import numpy as np
from contextlib import ExitStack
import concourse.bass as bass
import concourse.mybir as mybir
from concourse.bass_utils import run_bass_kernel_spmd

F32 = mybir.dt.float32
BF16 = mybir.dt.bfloat16
AF = mybir.ActivationFunctionType
ALU = mybir.AluOpType

NCORES = 8
CC_MAX_BYTES = 4 << 20
EPS = 1e-6
CAP = 15.0


class Cfg:
    def __init__(self, D=4096, B=2, S=4096, depth=4, TP=4):
        self.D, self.B, self.S, self.depth, self.TP = D, B, S, depth, TP
        self.NCU = TP * B
        self.NT = S
        self.HA = 8 // TP
        self.KC = D // 128
        self.NA = depth // 2
        self.DV = D // 8
        self.DK = self.DV // 2
        self.DKC = self.DK // 128
        self.DVC = self.DV // 128
        self.QKW = 8 * self.DK
        self.VW = 8 * self.DV
        self.H = D // 128
        self.HPC = self.H // TP
        self.AW = self.HPC * 128
        self.F = ((D * 8 // 3 + 63) // 64) * 64
        self.FC = self.F // TP
        self.FCH = (self.FC + 127) // 128
        self.FP = self.FCH * 128
        self.DC = D // TP
        self.PR = depth * 256 // TP
        self.DCC = self.DC // 128
        self.TMW = 2 * self.DV + self.DK
        self.TMN = self.TMW + 2


class Dep:
    __slots__ = ("w", "r")

    def __init__(self):
        self.w = None
        self.r = {}


class Ctx:
    def __init__(self, nc, TP=4, NG=2):
        self.nc = nc
        self.TP, self.NG = TP, NG
        self.es = ExitStack()
        self.eng = {"pe": nc.tensor, "act": nc.scalar, "dve": nc.vector,
                    "pool": nc.gpsimd, "sp": nc.sync}
        self.psem, self.cnt, self.waited = {}, {}, {}
        for e in self.eng:
            self.psem[e] = self.es.enter_context(nc.semaphore("ps_" + e))
            self.cnt[e] = 0
            self.waited[e] = {}
        self.slots, self.slot_cnt, self.dma_idx = {}, {}, {}
        for q, k in {"sp": 12, "pool": 8}.items():
            self.slots[q] = [self.es.enter_context(nc.semaphore(f"dq_{q}{i}")) for i in range(k)]
            for s in self.slots[q]:
                self.slot_cnt[s] = 0
            self.dma_idx[q] = 0
        self.ccsem = self.es.enter_context(nc.semaphore("cc"))
        self.cc_cnt = 0
        self.uid = 0

    def sb(self, stack, shape, dt, name="t"):
        self.uid += 1
        return stack.enter_context(self.nc.sbuf_tensor(f"{name}_{self.uid}", list(shape), dt))

    def ps(self, stack, shape, dt, name="p"):
        self.uid += 1
        return stack.enter_context(self.nc.psum_tensor(f"{name}_{self.uid}", list(shape), dt))

    def _wait(self, e, toks):
        need = {}
        for (s, v) in toks:
            if need.get(s, 0) < v:
                need[s] = v
        for s, v in need.items():
            if self.waited[e].get(s, 0) >= v:
                continue
            if e == "pe" and s is self.psem["pe"]:
                continue
            self.eng[e].wait_ge(s, v)
            self.waited[e][s] = v

    @staticmethod
    def _deps(reads, writes):
        toks = []
        for d in reads:
            if d.w is not None:
                toks.append(d.w)
        for d in writes:
            if d.w is not None:
                toks.append(d.w)
            toks.extend(d.r.items())
        return toks

    @staticmethod
    def _commit(tok, reads, writes):
        s, v = tok
        for d in reads:
            if d.r.get(s, 0) < v:
                d.r[s] = v
        for d in writes:
            d.w = tok
            d.r = {}

    def op(self, e, fn, reads=(), writes=()):
        self._wait(e, self._deps(reads, writes))
        ins = fn(self.eng[e])
        self.cnt[e] += 1
        ins.then_inc(self.psem[e], 1)
        tok = (self.psem[e], self.cnt[e])
        self._commit(tok, reads, writes)
        return tok

    def dma(self, q, out, in_, reads=(), writes=()):
        toks = self._deps(reads, writes)
        sl = self.slots[q]
        slot = sl[self.dma_idx[q] % len(sl)]
        self.dma_idx[q] += 1
        prev = self.slot_cnt[slot]
        if prev > 0:
            toks.append((slot, prev))
        self._wait(q, toks)
        self.eng[q].dma_start(out=out, in_=in_).then_inc(slot, 16)
        self.slot_cnt[slot] = prev + 16
        tok = (slot, prev + 16)
        self._commit(tok, reads, writes)
        return tok

    def collective(self, kind, op, in_ap, out_ap, tmp=None):
        self.barrier()
        TP = self.TP
        groups = [[g * TP + r for r in range(TP)] for g in range(self.NG)]
        R, C = in_ap.shape
        esz = 4
        if kind == "AllReduce":
            rc = max(1, CC_MAX_BYTES // (C * esz))
            for r0 in range(0, R, rc):
                r1 = min(R, r0 + rc)
                self.nc.gpsimd.collective_compute(kind, op, replica_groups=groups,
                                                  ins=[in_ap[r0:r1, :]], outs=[out_ap[r0:r1, :]]).then_inc(self.ccsem, 1)
                self.cc_cnt += 1
            self.barrier()
        else:
            rc = max(1, CC_MAX_BYTES // (C * esz * TP))
            chunks = []
            o = 0
            for r0 in range(0, R, rc):
                r1 = min(R, r0 + rc)
                n = r1 - r0
                self.nc.gpsimd.collective_compute(kind, op, replica_groups=groups,
                                                  ins=[in_ap[r0:r1, :]], outs=[tmp[o:o + TP * n, :]]).then_inc(self.ccsem, 1)
                self.cc_cnt += 1
                chunks.append((r0, n, o))
                o += TP * n
            self.barrier()
            ov = out_ap.rearrange("(r n) c -> r n c", r=TP)
            for (r0, n, o) in chunks:
                self.dma("sp", ov[:, r0:r0 + n, :], tmp[o:o + TP * n, :].rearrange("(r n) c -> r n c", r=TP))
            self.barrier()

    def wait_cc(self, e, count):
        if count > 0:
            self._wait(e, [(self.ccsem, count)])

    def barrier(self, cc=True):
        toks = [(self.psem[e], self.cnt[e]) for e in self.eng if self.cnt[e] > 0]
        toks += [(s, c) for s, c in self.slot_cnt.items() if c > 0]
        if self.cc_cnt and cc:
            toks.append((self.ccsem, self.cc_cnt))
        for e in self.eng:
            self._wait(e, toks)


def make_consts():
    j = np.arange(128)[:, None]
    t = np.arange(128)[None, :]
    c = {}
    c["ones"] = np.ones((128, 128), np.float32)
    c["negones"] = -np.ones((128, 128), np.float32)
    c["ident"] = np.eye(128, dtype=np.float32)
    c["U"] = (j <= t).astype(np.float32)
    c["negSLE"] = -(j >= t).astype(np.float32)
    c["negmask"] = np.where(j <= t, 0.0, -30000.0).astype(np.float32)
    t5 = np.arange(512)[None, :]
    for jj in range(4):
        c[f"amask{jj}"] = ((jj * 128 + j) < t5).astype(np.float32)
    return c


CONST_ORDER = ["ones", "negones", "ident", "U", "negSLE", "negmask", "amask0", "amask1", "amask2", "amask3"]


def consts_array():
    c = make_consts()
    return np.ascontiguousarray(np.concatenate([c[k] for k in CONST_ORDER], axis=1))


class Builder:
    def __init__(self, cfg, debug=()):
        self.cfg = cfg
        self.debug = set(debug)
        self.nc = bass.Bass("TRN2", target_bir_lowering=False)
        self.c = Ctx(self.nc, cfg.TP, cfg.B)
        self.inputs = {}
        self.scr = {}

    def inp(self, name, shape):
        t = self.nc.dram_tensor(name, list(shape), F32, kind="ExternalInput").ap()
        self.inputs[name] = tuple(shape)
        return t

    def scratch(self, name, shape, dt, force_internal=False):
        kind = "ExternalOutput" if (name in self.debug and not force_internal) else "Internal"
        t = self.nc.dram_tensor(name, list(shape), dt, kind=kind).ap()
        self.scr[name] = t
        return t

    def load_w(self, stack, W, K, N, name="w"):
        c = self.c
        kc_n = (K + 127) // 128
        wt = c.sb(stack, [128, kc_n, N], BF16, name)
        d = Dep()
        step = 8
        for k0 in range(0, kc_n, step):
            k1 = min(kc_n, k0 + step)
            c.dma("pool", wt[:, k0:k1, :], W[k0 * 128:k1 * 128, :].rearrange("(kc p) n -> p kc n", p=128),
                  writes=[d])
        return wt, d

    def load_f32(self, stack, src, shape, name="v"):
        t = self.c.sb(stack, shape, F32, name)
        d = Dep()
        self.c.dma("sp", t[:], src, writes=[d])
        return t, d

    def setup(self):
        cfg, c, nc = self.cfg, self.c, self.nc
        self.gs = ExitStack()
        ncst = len(CONST_ORDER)
        cw = 128 * 6 + 512 * 4
        cin = self.inp("consts", [128, cw])
        cf, d_cf = self.load_f32(self.gs, cin[:, :], [128, cw], "cf")
        self.d_const = Dep()
        off = {}
        o = 0
        for k in CONST_ORDER:
            w = 512 if k.startswith("amask") else 128
            off[k] = (o, w)
            o += w
        self.cf = cf
        self.onesf = cf[:, off["ones"][0]:off["ones"][0] + 128]
        self.Uf = cf[:, off["U"][0]:off["U"][0] + 128]
        self.negmask = cf[:, off["negmask"][0]:off["negmask"][0] + 128]
        cb = c.sb(self.gs, [128, cw], BF16, "cb")
        c.op("dve", lambda e: e.tensor_copy(cb[:], cf[:]), reads=[d_cf], writes=[self.d_const])
        self.d_const_f = d_cf
        self.ones_bf = cb[:, off["ones"][0]:off["ones"][0] + 128]
        self.negones_bf = cb[:, off["negones"][0]:off["negones"][0] + 128]
        self.ident_bf = cb[:, off["ident"][0]:off["ident"][0] + 128]
        self.negSLE_bf = cb[:, off["negSLE"][0]:off["negSLE"][0] + 128]
        self.amask_bf = [cb[:, off[f"amask{j}"][0]:off[f"amask{j}"][0] + 512] for j in range(4)]
        self.eps = c.sb(self.gs, [128, 1], F32, "eps")
        self.d_eps = Dep()
        c.op("pool", lambda e: e.memset(self.eps[:], EPS), writes=[self.d_eps])

    def norm_phase(self, src, g1, resid, g2, xn_out, TT=256):
        cfg, c = self.cfg, self.c
        KC, NT, D = cfg.KC, cfg.NT, cfg.D
        with ExitStack() as st:
            g1t = g2t = None
            if g1 is not None:
                g1t, d_g1 = self.load_f32(st, g1[:, :], [128, KC], "g1")
            if g2 is not None:
                g2t, d_g2 = self.load_f32(st, g2[:, :], [128, KC], "g2")
            two = resid is not None
            halves = src if isinstance(src, list) else None
            srct = [c.sb(st, [128, KC, TT], F32, "src") for _ in range(2)]
            d_src = [Dep(), Dep()]
            if two:
                ht = [c.sb(st, [128, KC, TT], F32, "h") for _ in range(1)]
                d_h = [Dep()]
            sqs = [c.sb(st, [128, KC, TT], BF16, "sq") for _ in range(2)]
            d_sqs = [Dep(), Dep()]
            if xn_out is not None:
                xn = [c.sb(st, [128, KC, TT], BF16, "xn") for _ in range(2)]
                d_xn = [Dep(), Dep()]
            rs = [c.sb(st, [128, TT], F32, "rs") for _ in range(2)]
            d_rs = [Dep(), Dep()]
            pss = [c.ps(st, [128, 512], F32, "nps") for _ in range(2)]
            d_ps = [Dep(), Dep()]
            HN = NT // 2
            if halves is None:
                sv = src.rearrange("(kc p) t -> p kc t", p=128)
            else:
                svh = [h_[0].rearrange("(kc p) t -> p kc t", p=128) for h_ in halves]
            gated = set()

            def load_src(it):
                t0 = it * TT
                if halves is None:
                    ap = sv[:, :, t0:t0 + TT]
                else:
                    hh = t0 // HN
                    if hh not in gated:
                        gated.add(hh)
                        c.wait_cc("sp", halves[hh][1])
                    ap = svh[hh][:, :, t0 - hh * HN:t0 - hh * HN + TT]
                c.dma("sp", srct[it % 2][:], ap, writes=[d_src[it % 2]])
            if two:
                hv = resid.rearrange("(kc p) t -> p kc t", p=128)
            if xn_out is not None:
                xv = xn_out.rearrange("(kc p) t -> p kc t", p=128)
            nt = NT // TT
            irs = 0

            def rstd_of(tile, d_tile):
                nonlocal irs
                k = irs % 2
                irs += 1
                sq, d_sq = sqs[k], d_sqs[k]
                c.op("act", lambda e: e.activation(out=sq[:], in_=tile[:], func=AF.Square),
                     reads=[d_tile], writes=[d_sq])
                for kc in range(KC):
                    c.op("pe", lambda e: e.matmul(pss[k][:, :TT], self.ones_bf, sq[:, kc, :],
                                                  start=(kc == 0), stop=(kc == KC - 1)),
                         reads=[self.d_const, d_sq], writes=[d_ps[k]])
                c.op("act", lambda e: e.activation(out=rs[k][:], in_=pss[k][:, :TT], func=AF.Sqrt,
                                                   bias=self.eps[:, 0:1], scale=1.0 / D),
                     reads=[d_ps[k], self.d_eps], writes=[d_rs[k]])
                c.op("dve", lambda e: e.reciprocal(rs[k][:], rs[k][:]), reads=[d_rs[k]], writes=[d_rs[k]])
                return rs[k], d_rs[k]

            load_src(0)
            for it in range(nt):
                b = it % 2
                t0 = it * TT
                if two:
                    c.dma("sp", ht[0][:], hv[:, :, t0:t0 + TT], writes=[d_h[0]])
                if it + 1 < nt:
                    load_src(it + 1)
                cur, d_cur = srct[b], d_src[b]
                if two:
                    r1, d_r1 = rstd_of(cur, d_cur)
                    for kc in range(KC):
                        c.op("dve", lambda e: e.scalar_tensor_tensor(cur[:, kc, :], cur[:, kc, :], g1t[:, kc:kc + 1],
                                                                     r1[:], ALU.mult, ALU.mult),
                             reads=[d_g1, d_r1], writes=[d_cur])
                    c.op("dve", lambda e: e.tensor_tensor(cur[:], cur[:], ht[0][:], ALU.add),
                         reads=[d_h[0]], writes=[d_cur])
                    c.dma("sp", hv[:, :, t0:t0 + TT], cur[:], reads=[d_cur])
                if xn_out is not None:
                    r2, d_r2 = rstd_of(cur, d_cur)
                    for kc in range(KC):
                        c.op("dve", lambda e: e.scalar_tensor_tensor(xn[b][:, kc, :], cur[:, kc, :], g2t[:, kc:kc + 1],
                                                                     r2[:], ALU.mult, ALU.mult),
                             reads=[d_cur, d_g2, d_r2], writes=[d_xn[b]])
                    c.dma("sp", xv[:, :, t0:t0 + TT], xn[b][:], reads=[d_xn[b]])
        c.barrier()

    def lin_fm(self, xT, K, W, nch_list, epilogue, TT=512, wname="w"):
        cfg, c = self.cfg, self.c
        NT = cfg.NT
        kc_n = (K + 127) // 128
        maxc = max(128, (65536 // (kc_n * 2)) // 128 * 128)
        if len(nch_list) * 128 > maxc:
            per = maxc // 128
            for i in range(0, len(nch_list), per):
                self.lin_fm(xT, K, W, nch_list[i:i + per], epilogue, TT, wname)
            return
        gc0 = nch_list[0][0]
        gc1 = nch_list[-1][0] + nch_list[-1][1]
        with ExitStack() as st:
            wt, d_w = self.load_w(st, W[:, gc0:gc1], K, gc1 - gc0, wname)
            xt = [c.sb(st, [128, kc_n, TT], BF16, "xt") for _ in range(2)]
            d_xt = [Dep(), Dep()]
            pss = [c.ps(st, [128, 512], F32, "lps") for _ in range(4)]
            d_ps = [Dep() for _ in range(4)]
            xv = xT.rearrange("(kc p) t -> p kc t", p=128)
            nt = NT // TT
            ip = 0
            c.dma("sp", xt[0][:], xv[:, :, 0:TT], writes=[d_xt[0]])
            for it in range(nt):
                b = it % 2
                t0 = it * TT
                if it + 1 < nt:
                    c.dma("sp", xt[1 - b][:], xv[:, :, t0 + TT:t0 + 2 * TT], writes=[d_xt[1 - b]])
                for idx, (c0, wd) in enumerate(nch_list):
                    p = ip % 4
                    ip += 1
                    for kc in range(kc_n):
                        c.op("pe", lambda e: e.matmul(pss[p][:wd, :TT], wt[:, kc, c0 - gc0:c0 - gc0 + wd], xt[b][:, kc, :],
                                                      start=(kc == 0), stop=(kc == kc_n - 1)),
                             reads=[d_w, d_xt[b]], writes=[d_ps[p]])
                    epilogue(idx, c0, wd, pss[p], d_ps[p], t0, TT)
        c.barrier()

    def lin_fm_ar(self, xT, K, W, part2, red2, TT=512):
        cfg, c = self.cfg, self.c
        NT, D, TP = cfg.NT, cfg.D, cfg.TP
        HN = NT // 2
        kc_n = (K + 127) // 128
        AR_R = max(128, min(D, (CC_MAX_BYTES // (HN * 4)) // 128 * 128))
        GR = min(D, max(AR_R, 1024))
        ngr = D // GR
        groups = [[g * TP + r for r in range(TP)] for g in range(cfg.B)]
        counts = []
        with ExitStack() as st:
            wts = [c.sb(st, [128, kc_n, GR], BF16, "wr") for _ in range(2)]
            d_w = [Dep(), Dep()]
            xt = [c.sb(st, [128, kc_n, TT], BF16, "xt") for _ in range(2)]
            d_xt = [Dep(), Dep()]
            pss = [c.ps(st, [128, 512], F32, "lps") for _ in range(4)]
            d_ps = [Dep() for _ in range(4)]
            ot = [c.sb(st, [128, 512], F32, "ot") for _ in range(3)]
            d_ot = [Dep() for _ in range(3)]
            xv = xT.rearrange("(kc p) t -> p kc t", p=128)
            nth = HN // TT
            seq = [(hf, g, it) for hf in range(2) for g in range(ngr) for it in range(nth)]
            wseq = [(hf, g) for hf in range(2) for g in range(ngr)]

            def loadw(wi):
                g = wseq[wi][1]
                for k0 in range(0, kc_n, 8):
                    k1 = min(kc_n, k0 + 8)
                    c.dma("pool", wts[wi % 2][:, k0:k1, :],
                          W[k0 * 128:k1 * 128, g * GR:(g + 1) * GR].rearrange("(kc p) n -> p kc n", p=128), writes=[d_w[wi % 2]])

            def loadx(i):
                hf, g, it = seq[i]
                t0 = hf * HN + it * TT
                c.dma("sp", xt[i % 2][:], xv[:, :, t0:t0 + TT], writes=[d_xt[i % 2]])
            loadw(0)
            loadx(0)
            ip = io = 0
            i = 0
            for wi, (hf, g) in enumerate(wseq):
                if wi + 1 < len(wseq):
                    loadw(wi + 1)
                wt = wts[wi % 2]
                toks = []
                for it in range(nth):
                    b = i % 2
                    t0 = it * TT
                    if i + 1 < len(seq):
                        loadx(i + 1)
                    for j in range(GR // 128):
                        p = ip % 4
                        ip += 1
                        k = io % 3
                        io += 1
                        for kc in range(kc_n):
                            c.op("pe", lambda e: e.matmul(pss[p][:, :TT], wt[:, kc, j * 128:(j + 1) * 128], xt[b][:, kc, :],
                                                          start=(kc == 0), stop=(kc == kc_n - 1)),
                                 reads=[d_w[wi % 2], d_xt[b]], writes=[d_ps[p]])
                        c.op("act", lambda e: e.copy(out=ot[k][:, :TT], in_=pss[p][:, :TT]), reads=[d_ps[p]], writes=[d_ot[k]])
                        r0 = g * GR + j * 128
                        toks.append(c.dma("sp", part2[hf][r0:r0 + 128, t0:t0 + TT], ot[k][:, :TT], reads=[d_ot[k]]))
                    i += 1
                if TP > 1:
                    c._wait("pool", toks)
                    for a0 in range(g * GR, (g + 1) * GR, AR_R):
                        self.nc.gpsimd.collective_compute("AllReduce", ALU.add, replica_groups=groups,
                                                          ins=[part2[hf][a0:a0 + AR_R, :]],
                                                          outs=[red2[hf][a0:a0 + AR_R, :]]).then_inc(c.ccsem, 1)
                        c.cc_cnt += 1
                if g == ngr - 1:
                    counts.append(c.cc_cnt)
        c.barrier(cc=False)
        return counts

    def store_epilogue(self, st, out, dt, scale=None, row_off=0, n=3):
        c = self.c
        ot = [c.sb(st, [128, 512], dt, "ot") for _ in range(n)]
        d_ot = [Dep() for _ in range(n)]
        state = {"i": 0}

        def ep(idx, c0, wd, ps, d_ps, t0, TT):
            k = state["i"] % n
            state["i"] += 1
            if scale is None:
                c.op("act", lambda e: e.copy(out=ot[k][:wd, :TT], in_=ps[:wd, :TT]), reads=[d_ps], writes=[d_ot[k]])
            else:
                c.op("act", lambda e: e.mul(out=ot[k][:wd, :TT], in_=ps[:wd, :TT], mul=scale(c0)),
                     reads=[d_ps], writes=[d_ot[k]])
            c.dma("pool", out[row_off + c0:row_off + c0 + wd, t0:t0 + TT], ot[k][:wd, :TT], reads=[d_ot[k]])
        return ep

    def lin_tm_A(self, xT, W, gb, tm_out, gates_out, TT=512):
        cfg, c = self.cfg, self.c
        KC, NT, DV, DK = cfg.KC, cfg.NT, cfg.DV, cfg.DK
        N = cfg.TMN
        groups = []
        o = 0
        while o < N:
            w = min(512, N - o)
            groups.append((o, w))
            o += w
        with ExitStack() as st:
            wt, d_w = self.load_w(st, W, cfg.D, N, "wtm")
            gbt, d_gb = self.load_f32(st, gb[:, :], [128, 2], "gb")
            gbs = c.sb(st, [128, 2], F32, "gbs")
            d_gbs = Dep()
            c.op("dve", lambda e: e.tensor_scalar(gbs[:], gbt[:], 1.0 / CAP, None, ALU.mult), reads=[d_gb], writes=[d_gbs])
            xt = [c.sb(st, [128, KC, TT], BF16, "xt") for _ in range(2)]
            d_xt = [Dep(), Dep()]
            pss = [c.ps(st, [128, 512], F32, "tps") for _ in range(4)]
            d_ps = [Dep() for _ in range(4)]
            ot = [c.sb(st, [128, cfg.TMW], BF16, "tmo") for _ in range(2)]
            d_ot = [Dep(), Dep()]
            gt = [c.sb(st, [128, 4], F32, "gto") for _ in range(2)]
            d_gt = [Dep(), Dep()]
            xv = xT.rearrange("(kc p) t -> p kc t", p=128)
            nt = NT // TT
            ip = 0
            io = 0
            c.dma("sp", xt[0][:], xv[:, :, 0:TT], writes=[d_xt[0]])
            for it in range(nt):
                b = it % 2
                t0 = it * TT
                if it + 1 < nt:
                    c.dma("sp", xt[1 - b][:], xv[:, :, t0 + TT:t0 + 2 * TT], writes=[d_xt[1 - b]])
                for tb in range(TT // 128):
                    k = io % 2
                    io += 1
                    for (c0, wd) in groups:
                        p = ip % 4
                        ip += 1
                        for kc in range(KC):
                            c.op("pe", lambda e: e.matmul(pss[p][:, :wd], xt[b][:, kc, tb * 128:(tb + 1) * 128],
                                                          wt[:, kc, c0:c0 + wd], start=(kc == 0), stop=(kc == KC - 1)),
                                 reads=[d_w, d_xt[b]], writes=[d_ps[p]])
                        for (s0, s1, kind) in ((0, DV, "v"), (DV, 2 * DV, "og"), (2 * DV, 2 * DV + DK, "k"),
                                               (cfg.TMW, cfg.TMW + 2, "g")):
                            a0, a1 = max(s0, c0), min(s1, c0 + wd)
                            if a0 >= a1:
                                continue
                            src = pss[p][:, a0 - c0:a1 - c0]
                            if kind in ("v", "k"):
                                c.op("act", lambda e: e.copy(out=ot[k][:, a0:a1], in_=src), reads=[d_ps[p]], writes=[d_ot[k]])
                            elif kind == "og":
                                c.op("act", lambda e: e.activation(out=ot[k][:, a0:a1], in_=src, func=AF.Sigmoid),
                                     reads=[d_ps[p]], writes=[d_ot[k]])
                            else:
                                c.op("act", lambda e: e.activation(out=gt[k][:, 0:1], in_=src[:, 0:1], func=AF.Tanh,
                                                                   bias=gbs[:, 0:1], scale=1.0 / CAP),
                                     reads=[d_ps[p], d_gbs], writes=[d_gt[k]])
                                c.op("act", lambda e: e.activation(out=gt[k][:, 1:2], in_=src[:, 1:2], func=AF.Tanh,
                                                                   bias=gbs[:, 1:2], scale=1.0 / CAP),
                                     reads=[d_ps[p], d_gbs], writes=[d_gt[k]])
                                c.op("act", lambda e: e.activation(out=gt[k][:, 2:3], in_=gt[k][:, 1:2], func=AF.Exp, scale=-CAP),
                                     reads=[d_gt[k]], writes=[d_gt[k]])
                                c.op("act", lambda e: e.activation(out=gt[k][:, 3:4], in_=gt[k][:, 2:3], func=AF.Ln, bias=1.0),
                                     reads=[d_gt[k]], writes=[d_gt[k]])
                                c.op("dve", lambda e: e.tensor_scalar(gt[k][:, 0:1], gt[k][:, 0:1], CAP, None, ALU.mult),
                                     reads=[d_gt[k]], writes=[d_gt[k]])
                                c.op("dve", lambda e: e.tensor_scalar(gt[k][:, 1:2], gt[k][:, 3:4], -1.0, None, ALU.mult),
                                     reads=[d_gt[k]], writes=[d_gt[k]])
                    r0 = t0 + tb * 128
                    c.dma("pool", tm_out[r0:r0 + 128, :], ot[k][:], reads=[d_ot[k]])
                    c.dma("pool", gates_out[r0:r0 + 128, :], gt[k][:, 0:2], reads=[d_gt[k]])
        c.barrier()

    def lin_tm_plain(self, xT, W, N, out, TT=512):
        cfg, c = self.cfg, self.c
        KC, NT = cfg.KC, cfg.NT
        if N > 1024:
            for o in range(0, N, 1024):
                w = min(1024, N - o)
                self.lin_tm_plain(xT, W[:, o:o + w], w, out[:, o:o + w], TT)
            return
        groups = [(o, min(512, N - o)) for o in range(0, N, 512)]
        with ExitStack() as st:
            wt, d_w = self.load_w(st, W, cfg.D, N, "wtp")
            xt = [c.sb(st, [128, KC, TT], BF16, "xt") for _ in range(2)]
            d_xt = [Dep(), Dep()]
            pss = [c.ps(st, [128, 512], F32, "tps") for _ in range(4)]
            d_ps = [Dep() for _ in range(4)]
            ot = [c.sb(st, [128, N], BF16, "tmo") for _ in range(2)]
            d_ot = [Dep(), Dep()]
            xv = xT.rearrange("(kc p) t -> p kc t", p=128)
            nt = NT // TT
            ip = io = 0
            c.dma("sp", xt[0][:], xv[:, :, 0:TT], writes=[d_xt[0]])
            for it in range(nt):
                b = it % 2
                t0 = it * TT
                if it + 1 < nt:
                    c.dma("sp", xt[1 - b][:], xv[:, :, t0 + TT:t0 + 2 * TT], writes=[d_xt[1 - b]])
                for tb in range(TT // 128):
                    k = io % 2
                    io += 1
                    for (c0, wd) in groups:
                        p = ip % 4
                        ip += 1
                        for kc in range(KC):
                            c.op("pe", lambda e: e.matmul(pss[p][:, :wd], xt[b][:, kc, tb * 128:(tb + 1) * 128],
                                                          wt[:, kc, c0:c0 + wd], start=(kc == 0), stop=(kc == KC - 1)),
                                 reads=[d_w, d_xt[b]], writes=[d_ps[p]])
                        c.op("act", lambda e: e.copy(out=ot[k][:, c0:c0 + wd], in_=pss[p][:, :wd]),
                             reads=[d_ps[p]], writes=[d_ot[k]])
                    r0 = t0 + tb * 128
                    c.dma("pool", out[r0:r0 + 128, :], ot[k][:], reads=[d_ot[k]])
        c.barrier()

    def mlstm_phase(self, qkT, tm, gates, hn, hsT):
        cfg, c = self.cfg, self.c
        DK, DV, DKC, DVC, S, B = cfg.DK, cfg.DV, cfg.DKC, cfg.DVC, cfg.S, 1
        L = 128
        NCH = S // L
        with ExitStack() as st:
            hnB, d_hn = self.load_f32(st, hn[:, :], [128, DV], "hnB")
            Cs = c.sb(st, [128, DKC, DV], F32, "C")
            Cb = c.sb(st, [128, DKC, DV], BF16, "Cb")
            ns = c.sb(st, [128, DKC, 1], F32, "n")
            nb = c.sb(st, [128, DKC, 1], BF16, "nb")
            d_C, d_Cb, d_n, d_nb = Dep(), Dep(), Dep(), Dep()
            NB = 2
            qk = [c.sb(st, [128, 2, DKC, L], BF16, "qk") for _ in range(NB)]
            tmt = [c.sb(st, [128, cfg.TMW], BF16, "tm") for _ in range(NB)]
            gt = [c.sb(st, [128, 2], F32, "g") for _ in range(NB)]
            d_qk = [Dep() for _ in range(NB)]
            d_tm = [Dep() for _ in range(NB)]
            d_g = [Dep() for _ in range(NB)]

            def T(shape, dt, name):
                return [c.sb(st, shape, dt, name) for _ in range(NB)], [Dep() for _ in range(NB)]
            lfB, d_lfB = T([128, 128], F32, "lfB")
            sm, d_sm = T([128, 16], F32, "sm")
            eB, d_eB = T([128, 128], F32, "eB")
            tmpm, d_tmpm = T([128, 128], F32, "tmpm")
            WT, d_WT = T([128, 128], F32, "WT")
            PT, d_PT = T([128, 128], BF16, "PT")
            qa, d_qa = T([128, DKC, 128], BF16, "qa")
            junk, d_junk = T([128, DV], F32, "junk")
            o1, d_o1 = T([128, DV], F32, "o1")
            o2, d_o2 = T([128, DV], BF16, "o2")
            hso, d_hso = T([128, DVC, 128], BF16, "hso")
            kw, d_kw = T([128, DK], BF16, "kw")
            ps_b = [c.ps(st, [128, 512], F32, "psb") for _ in range(2)]
            d_psb = [[Dep() for _ in range(4)] for _ in range(2)]
            ps_st = c.ps(st, [128, 512], F32, "psst")
            d_psst = Dep()
            ps_num = [c.ps(st, [128, 512], F32, "psn") for _ in range(2)]
            d_psn = [Dep(), Dep()]
            ps_tp = c.ps(st, [128, 1024], BF16, "pstp")
            d_pstp = Dep()
            ps_dc = [c.ps(st, [128, 512], F32, "psdc") for _ in range(2)]
            d_psdc = [Dep(), Dep()]
            qv = qkT.rearrange("(a j p) t -> p a j t", a=2, p=128)
            hv = hsT.rearrange("(j p) t -> p j t", p=128)
            dcst = [self.d_const, self.d_const_f]

            def load(gi):
                bb, ch = divmod(gi, NCH)
                r0 = bb * S + ch * L
                k = gi % NB
                c.dma("sp", qk[k][:], qv[:, :, :, r0:r0 + L], writes=[d_qk[k]])
                c.dma("sp", tmt[k][:], tm[r0:r0 + L, :], writes=[d_tm[k]])
                c.dma("sp", gt[k][:], gates[r0:r0 + L, :], writes=[d_g[k]])

            total = B * NCH
            load(0)
            for gi in range(total):
                bb, ch = divmod(gi, NCH)
                r0 = bb * S + ch * L
                k = gi % NB
                if gi + 1 < total:
                    load(gi + 1)
                if ch == 0:
                    c.op("pool", lambda e: e.memset(Cs[:], 0.0), writes=[d_C])
                    c.op("pool", lambda e: e.memset(Cb[:], 0.0), writes=[d_Cb])
                    c.op("pool", lambda e: e.memset(ns[:], 0.0), writes=[d_n])
                    c.op("pool", lambda e: e.memset(nb[:], 0.0), writes=[d_nb])
                pb = ps_b[k]
                dpb = d_psb[k]
                qT = qk[k][:, 0]
                kT = qk[k][:, 1]
                v = tmt[k][:, 0:DV]
                ogs = tmt[k][:, DV:2 * DV]
                kk = tmt[k][:, 2 * DV:2 * DV + DK]
                li = gt[k][:, 0:1]
                lf = gt[k][:, 1:2]
                s = sm[k]
                d_s = d_sm[k]
                c.op("dve", lambda e: e.tensor_scalar(lfB[k][:], self.onesf, lf, None, ALU.mult),
                     reads=[d_g[k]] + dcst, writes=[d_lfB[k]])
                c.op("pe", lambda e: e.matmul(pb[:, 0:128], lfB[k][:], self.Uf, start=True, stop=True),
                     reads=[d_lfB[k]] + dcst, writes=[dpb[0]])
                c.op("pe", lambda e: e.matmul(pb[:, 128:129], self.Uf, lf, start=True, stop=True),
                     reads=[d_g[k]] + dcst, writes=[dpb[1]])
                c.op("dve", lambda e: e.tensor_tensor(s[:, 0:1], li, pb[:, 128:129], ALU.subtract),
                     reads=[d_g[k], dpb[1]], writes=[d_s])
                c.op("act", lambda e: e.copy(out=s[:, 1:2], in_=pb[:, 127:128]), reads=[dpb[0]], writes=[d_s])
                c.op("act", lambda e: e.activation(out=s[:, 2:3], in_=pb[:, 127:128], func=AF.Exp), reads=[dpb[0]], writes=[d_s])
                c.op("act", lambda e: e.activation(out=s[:, 3:4], in_=s[:, 0:1], func=AF.Exp, bias=s[:, 1:2]),
                     reads=[d_s], writes=[d_s])
                c.op("act", lambda e: e.activation(out=eB[k][:], in_=pb[:, 0:128], func=AF.Exp), reads=[dpb[0]], writes=[d_eB[k]])
                c.op("dve", lambda e: e.tensor_tensor(tmpm[k][:], pb[:, 0:128], self.negmask, ALU.add),
                     reads=[dpb[0]] + dcst, writes=[d_tmpm[k]])
                c.op("act", lambda e: e.activation(out=WT[k][:], in_=tmpm[k][:], func=AF.Exp, bias=s[:, 0:1]),
                     reads=[d_tmpm[k], d_s], writes=[d_WT[k]])
                for j in range(DKC):
                    c.op("pe", lambda e: e.matmul(ps_st[:, 0:128], kT[:, j, :], qT[:, j, :], start=(j == 0), stop=(j == DKC - 1)),
                         reads=[d_qk[k]], writes=[d_psst])
                c.op("dve", lambda e: e.tensor_tensor(PT[k][:], ps_st[:, 0:128], WT[k][:], ALU.mult),
                     reads=[d_psst, d_WT[k]], writes=[d_PT[k]])
                for j in range(DKC):
                    c.op("pool", lambda e: e.tensor_tensor(qa[k][:, j, :], qT[:, j, :], eB[k][:], ALU.mult),
                         reads=[d_qk[k], d_eB[k]], writes=[d_qa[k]])
                pn = ps_num[k]
                for j in range(DKC):
                    c.op("pe", lambda e: e.matmul(pn[:, 0:DV], qa[k][:, j, :], Cb[:, j, :], start=(j == 0), stop=False),
                         reads=[d_qa[k], d_Cb], writes=[d_psn[k]])
                c.op("pe", lambda e: e.matmul(pn[:, 0:DV], PT[k][:], v, start=False, stop=True),
                     reads=[d_PT[k], d_tm[k]], writes=[d_psn[k]])
                for j in range(DKC):
                    c.op("pe", lambda e: e.matmul(pb[:, 132:133], qa[k][:, j, :], nb[:, j, :], start=(j == 0), stop=False),
                         reads=[d_qa[k], d_nb], writes=[dpb[2]])
                c.op("pe", lambda e: e.matmul(pb[:, 132:133], PT[k][:], self.ones_bf[:, 0:1], start=False, stop=True),
                     reads=[d_PT[k]] + dcst, writes=[dpb[2]])
                c.op("act", lambda e: e.activation(out=s[:, 4:5], in_=pb[:, 132:133], func=AF.Abs),
                     reads=[dpb[2]], writes=[d_s])
                c.op("dve", lambda e: e.tensor_scalar(s[:, 4:5], s[:, 4:5], 1.0, None, ALU.max),
                     reads=[d_s], writes=[d_s])
                c.op("dve", lambda e: e.reciprocal(s[:, 5:6], s[:, 4:5]), reads=[d_s], writes=[d_s])
                c.op("pool", lambda e: e.memset(s[:, 6:7], 0.0), writes=[d_s])
                c.op("act", lambda e: e.activation(out=junk[k][:], in_=pn[:, 0:DV], func=AF.Square, scale=s[:, 5:6],
                                                   accum_out=s[:, 6:7]),
                     reads=[d_psn[k], d_s], writes=[d_junk[k], d_s])
                c.op("act", lambda e: e.activation(out=s[:, 7:8], in_=s[:, 6:7], func=AF.Sqrt, bias=self.eps[:, 0:1], scale=1.0 / DV),
                     reads=[d_s, self.d_eps], writes=[d_s])
                c.op("dve", lambda e: e.reciprocal(s[:, 7:8], s[:, 7:8]), reads=[d_s], writes=[d_s])
                c.op("dve", lambda e: e.tensor_tensor(s[:, 8:9], s[:, 7:8], s[:, 5:6], ALU.mult), reads=[d_s], writes=[d_s])
                c.op("dve", lambda e: e.scalar_tensor_tensor(o1[k][:], pn[:, 0:DV], s[:, 8:9], hnB[:], ALU.mult, ALU.mult),
                     reads=[d_psn[k], d_s, d_hn], writes=[d_o1[k]])
                c.op("pool", lambda e: e.tensor_tensor(o2[k][:], o1[k][:], ogs, ALU.mult),
                     reads=[d_o1[k], d_tm[k]], writes=[d_o2[k]])
                for jj in range(DVC):
                    c.op("pe", lambda e: e.transpose(ps_tp[:, jj * 128:(jj + 1) * 128], o2[k][:, jj * 128:(jj + 1) * 128], self.ident_bf),
                         reads=[d_o2[k]] + dcst, writes=[d_pstp])
                c.op("act", lambda e: e.copy(out=hso[k][:], in_=ps_tp[:, 0:DVC * 128].rearrange("p (j t) -> p j t", j=DVC)),
                     reads=[d_pstp], writes=[d_hso[k]])
                c.dma("pool", hv[:, :, r0:r0 + L], hso[k][:], reads=[d_hso[k]])
                c.op("dve", lambda e: e.tensor_scalar(kw[k][:], kk, s[:, 3:4], None, ALU.mult),
                     reads=[d_tm[k], d_s], writes=[d_kw[k]])
                for j in range(DKC):
                    pd = ps_dc[j % 2]
                    c.op("pe", lambda e: e.matmul(pd[:, 0:DV], kw[k][:, j * 128:(j + 1) * 128], v, start=True, stop=True),
                         reads=[d_kw[k], d_tm[k]], writes=[d_psdc[j % 2]])
                    c.op("pe", lambda e: e.matmul(pb[:, 136 + j:137 + j], kw[k][:, j * 128:(j + 1) * 128], self.ones_bf[:, 0:1],
                                                  start=True, stop=True),
                         reads=[d_kw[k]] + dcst, writes=[dpb[3]])
                    c.op("dve", lambda e: e.scalar_tensor_tensor(Cs[:, j, :], Cs[:, j, :], s[:, 2:3], pd[:, 0:DV], ALU.mult, ALU.add),
                         reads=[d_s, d_psdc[j % 2]], writes=[d_C])
                    c.op("dve", lambda e: e.scalar_tensor_tensor(ns[:, j, :], ns[:, j, :], s[:, 2:3], pb[:, 136 + j:137 + j], ALU.mult, ALU.add),
                         reads=[d_s, dpb[3]], writes=[d_n])
                c.op("act", lambda e: e.copy(out=Cb[:], in_=Cs[:]), reads=[d_C], writes=[d_Cb])
                c.op("act", lambda e: e.copy(out=nb[:], in_=ns[:]), reads=[d_n], writes=[d_nb])
        c.barrier()

    def attn_phase(self, qT, kT, vtm, attT):
        cfg, c = self.cfg, self.c
        S, B, HPC = cfg.S, cfg.B, cfg.HPC
        NB = S // 128
        NSB = S // 512
        with ExitStack() as st:
            qt = [c.sb(st, [128, S], BF16, "aq") for _ in range(2)]
            kt = [c.sb(st, [128, S], BF16, "ak") for _ in range(2)]
            vt = [c.sb(st, [128, NB, 128], BF16, "av") for _ in range(2)]
            d_q = [Dep(), Dep()]
            d_k = [Dep(), Dep()]
            d_v = [Dep(), Dep()]

            def T(n, shape, dt, name):
                return [c.sb(st, shape, dt, name) for _ in range(n)], [Dep() for _ in range(n)]
            ee, d_ee = T(2, [128, 512], F32, "ee")
            Lp, d_Lp = T(2, [128, 512], BF16, "Lp")
            Lm, d_Lm = T(2, [128, 512], BF16, "Lm")
            arg, d_arg = T(2, [128, 512], F32, "arg")
            R, d_R = T(2, [128, 512], F32, "R")
            A, d_A = T(2, [128, 512], BF16, "A")
            Am, d_Am = T(2, [128, 512], BF16, "Am")
            osb, d_osb = T(2, [128, 512], BF16, "osb")
            ps_z = [c.ps(st, [128, 512], F32, "psz") for _ in range(2)]
            d_psz = [Dep(), Dep()]
            ps_1 = [c.ps(st, [128, 512], F32, "ps1") for _ in range(2)]
            d_ps1 = [Dep(), Dep()]
            ps_2 = [c.ps(st, [128, 512], F32, "ps2") for _ in range(2)]
            d_ps2 = [Dep(), Dep()]
            ps_o = [c.ps(st, [128, 512], F32, "pso") for _ in range(2)]
            d_pso = [Dep(), Dep()]
            dcst = [self.d_const]
            units = [(0, hh) for hh in range(HPC)]

            def load(ui):
                bb, hh = units[ui]
                k = ui % 2
                c.dma("sp", qt[k][:], qT[hh * 128:(hh + 1) * 128, bb * S:(bb + 1) * S], writes=[d_q[k]])
                c.dma("sp", kt[k][:], kT[hh * 128:(hh + 1) * 128, bb * S:(bb + 1) * S], writes=[d_k[k]])
                c.dma("sp", vt[k][:], vtm[bb * S:(bb + 1) * S, hh * 128:(hh + 1) * 128].rearrange("(j p) d -> p j d", p=128),
                      writes=[d_v[k]])
            tiles = []
            gi = 0
            for ui, (bb, hh) in enumerate(units):
                for I in range(NSB):
                    Jtop = 4 * I + 3
                    for J in range(Jtop, -1, -1):
                        tiles.append(dict(ui=ui, u=ui % 2, bb=bb, hh=hh, I=I, J=J, first=(J == Jtop), last=(J == 0),
                                          diag=(J >= 4 * I), jj=J - 4 * I, o=gi % 2))
                    gi += 1
            NBUF = 3
            ee3, d_ee3 = T(NBUF, [128, 512], F32, "ee3")
            Lp3, d_Lp3 = T(NBUF, [128, 512], BF16, "Lp3")
            Lm3, d_Lm3 = T(NBUF, [128, 512], BF16, "Lm3")
            A3, d_A3 = T(NBUF, [128, 512], BF16, "A3")
            Am3, d_Am3 = T(NBUF, [128, 512], BF16, "Am3")
            loaded = set()
            rstate = {"prev": None}

            def qs_ks(t):
                return (qt[t["u"]][:, t["I"] * 512:(t["I"] + 1) * 512], kt[t["u"]][:, t["J"] * 128:(t["J"] + 1) * 128])

            def stageA(n):
                t = tiles[n]
                if t["ui"] not in loaded:
                    loaded.add(t["ui"])
                    if t["ui"] == 0:
                        load(0)
                    if t["ui"] + 1 < len(units):
                        load(t["ui"] + 1)
                u, x, m = t["u"], n % 2, n % NBUF
                qs, ks = qs_ks(t)
                c.op("pe", lambda e: e.matmul(ps_z[x][:], ks, qs, start=True, stop=True),
                     reads=[d_q[u], d_k[u]], writes=[d_psz[x]])
                c.op("act", lambda e: e.activation(out=ee3[m][:], in_=ps_z[x][:], func=AF.Exp),
                     reads=[d_psz[x]], writes=[d_ee3[m]])
                c.op("act", lambda e: e.activation(out=Lp3[m][:], in_=ee3[m][:], func=AF.Ln, bias=1.0),
                     reads=[d_ee3[m]], writes=[d_Lp3[m]])
                if t["diag"]:
                    c.op("pool", lambda e: e.tensor_tensor(Lm3[m][:], Lp3[m][:], self.amask_bf[t["jj"]], ALU.mult),
                         reads=[d_Lp3[m]] + dcst, writes=[d_Lm3[m]])

            def stageB(n):
                t = tiles[n]
                u, x, m = t["u"], n % 2, n % NBUF
                qs, ks = qs_ks(t)
                lm, d_lm = (Lm3[m], d_Lm3[m]) if t["diag"] else (Lp3[m], d_Lp3[m])
                c.op("pe", lambda e: e.matmul(ps_1[x][:], ks, qs, start=True, stop=False),
                     reads=[d_q[u], d_k[u]], writes=[d_ps1[x]])
                c.op("pe", lambda e: e.matmul(ps_1[x][:], self.negSLE_bf, lm[:], start=False, stop=True),
                     reads=[d_lm] + dcst, writes=[d_ps1[x]])
                if not t["last"]:
                    c.op("pe", lambda e: e.matmul(ps_2[x][:], self.negones_bf, lm[:], start=True, stop=True),
                         reads=[d_lm] + dcst, writes=[d_ps2[x]])
                if t["first"]:
                    c.op("act", lambda e: e.activation(out=A3[m][:], in_=ps_1[x][:], func=AF.Exp),
                         reads=[d_ps1[x]], writes=[d_A3[m]])
                    rn = 0
                    if not t["last"]:
                        c.op("dve", lambda e: e.tensor_copy(R[rn][:], ps_2[x][:]), reads=[d_ps2[x]], writes=[d_R[rn]])
                else:
                    rp = rstate["prev"]
                    c.op("dve", lambda e: e.tensor_tensor(arg[x][:], ps_1[x][:], R[rp][:], ALU.add),
                         reads=[d_ps1[x], d_R[rp]], writes=[d_arg[x]])
                    c.op("act", lambda e: e.activation(out=A3[m][:], in_=arg[x][:], func=AF.Exp),
                         reads=[d_arg[x]], writes=[d_A3[m]])
                    rn = 1 - rp
                    if not t["last"]:
                        c.op("dve", lambda e: e.tensor_tensor(R[rn][:], ps_2[x][:], R[rp][:], ALU.add),
                             reads=[d_ps2[x], d_R[rp]], writes=[d_R[rn]])
                rstate["prev"] = rn
                if t["diag"]:
                    c.op("pool", lambda e: e.tensor_tensor(Am3[m][:], A3[m][:], self.amask_bf[t["jj"]], ALU.mult),
                         reads=[d_A3[m]] + dcst, writes=[d_Am3[m]])

            def stageC(n):
                t = tiles[n]
                u, m, o = t["u"], n % NBUF, t["o"]
                am, d_am = (Am3[m], d_Am3[m]) if t["diag"] else (A3[m], d_A3[m])
                c.op("pe", lambda e: e.matmul(ps_o[o][:], vt[u][:, t["J"], :], am[:], start=t["first"], stop=t["last"]),
                     reads=[d_v[u], d_am], writes=[d_pso[o]])
                if t["last"]:
                    bb, hh, I = t["bb"], t["hh"], t["I"]
                    c.op("act", lambda e: e.copy(out=osb[o][:], in_=ps_o[o][:]), reads=[d_pso[o]], writes=[d_osb[o]])
                    c.dma("pool", attT[hh * 128:(hh + 1) * 128, bb * S + I * 512:bb * S + (I + 1) * 512], osb[o][:],
                          reads=[d_osb[o]])

            NTL = len(tiles)
            for n in range(NTL + 2):
                if n < NTL:
                    stageA(n)
                if 0 <= n - 1 < NTL:
                    stageB(n - 1)
                if 0 <= n - 2 < NTL:
                    stageC(n - 2)
        c.barrier()

    def ffn_up_phase(self, xT, Wup, cw, aT, TT=512, G=3):
        cfg, c = self.cfg, self.c
        KC, NT, S, FCH = cfg.KC, cfg.NT, cfg.S, cfg.FCH
        cwt_stack = ExitStack()
        cwt, d_cw = self.load_f32(cwt_stack, cw[:, :], [128, FCH * 8], "cw")
        xv = xT.rearrange("(kc p) t -> p kc t", p=128)
        for g0 in range(0, FCH, G):
            g1 = min(FCH, g0 + G)
            ng = g1 - g0
            with ExitStack() as st:
                wt, d_w = self.load_w(st, Wup[:, g0 * 256:g1 * 256], cfg.D, ng * 256, "wup")
                xt = [c.sb(st, [128, KC, TT], BF16, "xt") for _ in range(2)]
                d_xt = [Dep(), Dep()]
                pss = [c.ps(st, [128, 512], F32, "fps") for _ in range(4)]
                d_ps = [Dep() for _ in range(4)]
                ub = [[c.sb(st, [128, TT + 2], F32, "u") for _ in range(2)] for _ in range(ng)]
                d_ub = [[Dep(), Dep()] for _ in range(ng)]
                cv = [[c.sb(st, [128, TT], F32, "cv") for _ in range(2)] for _ in range(2)]
                d_cv = [[Dep(), Dep()] for _ in range(2)]
                sg = [c.sb(st, [128, TT], F32, "sg") for _ in range(2)]
                d_sg = [Dep(), Dep()]
                ao = [c.sb(st, [128, TT], BF16, "ao") for _ in range(2)]
                d_ao = [Dep(), Dep()]
                nt = NT // TT
                ip = io = 0
                c.dma("sp", xt[0][:], xv[:, :, 0:TT], writes=[d_xt[0]])
                for it in range(nt):
                    b = it % 2
                    t0 = it * TT
                    seq_start = (t0 % S == 0)
                    if it + 1 < nt:
                        c.dma("sp", xt[1 - b][:], xv[:, :, t0 + TT:t0 + 2 * TT], writes=[d_xt[1 - b]])
                    for ch in range(ng):
                        k = io % 2
                        io += 1
                        for gv in range(2):
                            p = ip % 4
                            ip += 1
                            c0 = ch * 256 + gv * 128
                            for kc in range(KC):
                                c.op("pe", lambda e: e.matmul(pss[p][:, :TT], wt[:, kc, c0:c0 + 128], xt[b][:, kc, :],
                                                              start=(kc == 0), stop=(kc == KC - 1)),
                                     reads=[d_w, d_xt[b]], writes=[d_ps[p]])
                            u, d_u = ub[ch][gv], d_ub[ch][gv]
                            if seq_start:
                                c.op("pool", lambda e: e.memset(u[:, 0:2], 0.0), writes=[d_u])
                            else:
                                c.op("pool", lambda e: e.tensor_copy(u[:, 0:2], u[:, TT:TT + 2]), reads=[d_u], writes=[d_u])
                            c.op("act", lambda e: e.copy(out=u[:, 2:TT + 2], in_=pss[p][:, :TT]), reads=[d_ps[p]], writes=[d_u])
                            base = ((g0 + ch) * 2 + gv) * 4
                            cc, d_cc = cv[gv][k], d_cv[gv][k]
                            c.op("dve", lambda e: e.tensor_scalar(cc[:], u[:, 2:TT + 2], cwt[:, base + 2:base + 3],
                                                                  cwt[:, base + 3:base + 4], ALU.mult, ALU.add),
                                 reads=[d_u, d_cw], writes=[d_cc])
                            c.op("dve", lambda e: e.scalar_tensor_tensor(cc[:], u[:, 1:TT + 1], cwt[:, base + 1:base + 2], cc[:],
                                                                         ALU.mult, ALU.add),
                                 reads=[d_u, d_cw], writes=[d_cc])
                            c.op("dve", lambda e: e.scalar_tensor_tensor(cc[:], u[:, 0:TT], cwt[:, base + 0:base + 1], cc[:],
                                                                         ALU.mult, ALU.add),
                                 reads=[d_u, d_cw], writes=[d_cc])
                        c.op("act", lambda e: e.activation(out=sg[k][:], in_=cv[0][k][:], func=AF.Silu),
                             reads=[d_cv[0][k]], writes=[d_sg[k]])
                        c.op("pool", lambda e: e.tensor_tensor(ao[k][:], sg[k][:], cv[1][k][:], ALU.mult),
                             reads=[d_sg[k], d_cv[1][k]], writes=[d_ao[k]])
                        r0 = (g0 + ch) * 128
                        c.dma("pool", aT[r0:r0 + 128, t0:t0 + TT], ao[k][:], reads=[d_ao[k]])
            c.barrier()
        cwt_stack.close()

    def ple_phase(self, xT, Wg, Wpe, pT, hT, out, TT=512):
        cfg, c = self.cfg, self.c
        KC, NT, DC = cfg.KC, cfg.NT, cfg.DC
        xv = xT.rearrange("(kc p) t -> p kc t", p=128)
        pv = pT.rearrange("(kc p) t -> p kc t", p=128)
        if cfg.TP == 1:
            own = hT
        else:
            if getattr(self, "_own_hT", None) is None:
                pid = self.nc.sync.partition_id()
                self._own_hT = hT[bass.ds((pid % cfg.TP) * DC, DC), :]
            c.dma("sp", self.h_own[:, :], self._own_hT[:, :])
            c.barrier()
            own = self.h_own
        for g0 in range(0, DC, 1024):
            g1 = min(DC, g0 + 1024)
            nj = (g1 - g0) // 128
            with ExitStack() as st:
                wg, d_wg = self.load_w(st, Wg[:, g0:g1], cfg.D, g1 - g0, "wg")
                wp, d_wp = self.load_w(st, Wpe[:, g0:g1], 256, g1 - g0, "wp")
                xt = [c.sb(st, [128, KC, TT], BF16, "xt") for _ in range(2)]
                d_xt = [Dep(), Dep()]
                pt = [c.sb(st, [128, 2, TT], BF16, "pt") for _ in range(2)]
                d_pt = [Dep(), Dep()]
                hr = [c.sb(st, [128, nj, TT], F32, "hr") for _ in range(2)]
                d_hrj = [[Dep() for _ in range(nj)] for _ in range(2)]
                pss = [c.ps(st, [128, 512], F32, "pps") for _ in range(4)]
                d_ps = [Dep() for _ in range(4)]
                gs = [c.sb(st, [128, TT], F32, "gs") for _ in range(2)]
                d_gs = [Dep(), Dep()]
                ot = [c.sb(st, [128, TT], F32, "po") for _ in range(2)]
                d_ot = [Dep(), Dep()]
                nt = NT // TT
                ip = io = 0

                def load(it):
                    b = it % 2
                    t0 = it * TT
                    c.dma("sp", xt[b][:], xv[:, :, t0:t0 + TT], writes=[d_xt[b]])
                    c.dma("pool", pt[b][:], pv[:, :, t0:t0 + TT], writes=[d_pt[b]])
                    for j in range(nj):
                        src = own[g0 + j * 128:g0 + (j + 1) * 128, t0:t0 + TT]
                        c.dma("sp", hr[b][:, j, :], src, writes=[d_hrj[b][j]])
                load(0)
                for it in range(nt):
                    b = it % 2
                    t0 = it * TT
                    if it + 1 < nt:
                        load(it + 1)
                    for j in range(nj):
                        pg = ip % 4
                        pe_ = (ip + 1) % 4
                        ip += 2
                        k = io % 2
                        io += 1
                        for kc in range(KC):
                            c.op("pe", lambda e: e.matmul(pss[pg][:, :TT], wg[:, kc, j * 128:(j + 1) * 128], xt[b][:, kc, :],
                                                          start=(kc == 0), stop=(kc == KC - 1)),
                                 reads=[d_wg, d_xt[b]], writes=[d_ps[pg]])
                        for kc in range(2):
                            c.op("pe", lambda e: e.matmul(pss[pe_][:, :TT], wp[:, kc, j * 128:(j + 1) * 128], pt[b][:, kc, :],
                                                          start=(kc == 0), stop=(kc == 1)),
                                 reads=[d_wp, d_pt[b]], writes=[d_ps[pe_]])
                        c.op("act", lambda e: e.activation(out=gs[k][:], in_=pss[pg][:, :TT], func=AF.Sigmoid),
                             reads=[d_ps[pg]], writes=[d_gs[k]])
                        c.op("dve", lambda e: e.tensor_tensor(ot[k][:], pss[pe_][:, :TT], gs[k][:], ALU.mult),
                             reads=[d_ps[pe_], d_gs[k]], writes=[d_ot[k]])
                        c.op("dve", lambda e: e.tensor_tensor(ot[k][:], ot[k][:], hr[b][:, j, :], ALU.add),
                             reads=[d_hrj[b][j]], writes=[d_ot[k]])
                        c.dma("pool", out[g0 + j * 128:g0 + (j + 1) * 128, t0:t0 + TT], ot[k][:], reads=[d_ot[k]])
            c.barrier()

    def build(self, stop_after=None):
        cfg, c, nc = self.cfg, self.c, self.nc
        D, NT, KC, DC, TP = cfg.D, cfg.NT, cfg.KC, cfg.DC, cfg.TP
        self.setup()
        xs = self.inp("xs", [D, NT])
        ps_in = self.inp("ps", [cfg.depth * 256, NT])
        y = nc.dram_tensor("y", [DC, NT], F32, kind="ExternalOutput").ap()
        hT = self.scratch("hT", [D, NT], F32, force_internal=True)
        xs_i = self.scratch("xs_i", [DC, NT], F32, force_internal=True)
        ps_i = self.scratch("ps_i", [cfg.PR, NT], F32, force_internal=True)
        pT = ps_in
        xnT = self.scratch("xnT", [D, NT], BF16)
        part2 = [self.scratch(f"part{i}", [D, NT // 2], F32, force_internal=True) for i in range(2)]
        red2 = part2 if TP == 1 else [self.scratch(f"red{i}", [D, NT // 2], F32, force_internal=True) for i in range(2)]
        ple_in = self.scratch("ple_in", [DC, NT], F32, force_internal=True)
        agtmp = self.scratch("agtmp", [D, NT], F32, force_internal=True)
        self.h_own = self.scratch("h_own", [DC, NT], F32, force_internal=True)
        qkT = self.scratch("qkT", [2 * cfg.DK, NT], BF16)
        tmA = self.scratch("tmA", [NT, cfg.TMW], BF16)
        gatesA = self.scratch("gatesA", [NT, 2], F32)
        hsT = self.scratch("hsT", [cfg.HA * cfg.DV, NT], BF16)
        aT = self.scratch("aT", [cfg.FP, NT], BF16)
        kshT = self.scratch("kshT", [cfg.AW, NT], BF16)
        vsh = self.scratch("vsh", [NT, cfg.AW], BF16)
        qT = self.scratch("qT", [cfg.AW, NT], BF16)
        attT = self.scratch("attT", [cfg.AW, NT], BF16)

        def snapshot(name):
            if name in self.debug:
                t = nc.dram_tensor(name, [D, NT], F32, kind="ExternalOutput").ap()
                c.dma("sp", t[:, :], hT[:, :])
                c.barrier()

        def allreduce():
            if TP > 1:
                c.collective("AllReduce", ALU.add, part[:, :], red[:, :])

        for r0 in range(0, D, 512):
            c.dma("sp", hT[r0:r0 + 512, :], xs[r0:r0 + 512, :])
        c.barrier()
        snapshot("h_in")

        def gvec(name):
            return self.inp(name, [128, KC])

        for layer in range(cfg.depth):
            L = f"{layer}"
            if layer < cfg.NA:
                self.norm_phase(hT, None, None, gvec("a_norm_pre" + L), xnT)
                for hl in range(cfg.HA):
                    Lh = f"{layer}_{hl}"
                    wqk = self.inp("a_wqk" + Lh, [D, 2 * cfg.DK])
                    with ExitStack() as st:
                        ep = self.store_epilogue(st, qkT, BF16, scale=lambda c0: (cfg.DK ** -0.5 if c0 < cfg.DK else 1.0))
                        self.lin_fm(xnT, D, wqk, [(i * 128, 128) for i in range(2 * cfg.DKC)], ep)
                    self.lin_tm_A(xnT, self.inp("a_wtm" + Lh, [D, cfg.TMN]), self.inp("a_gb" + Lh, [128, 2]), tmA, gatesA)
                    self.mlstm_phase(qkT, tmA, gatesA, self.inp("a_hn" + Lh, [128, cfg.DV]),
                                     hsT[hl * cfg.DV:(hl + 1) * cfg.DV, :])
                mixK, mixX, wo = cfg.HA * cfg.DV, hsT, self.inp("a_wout" + L, [cfg.HA * cfg.DV, D])
                gpost = gvec("a_norm_post" + L)
            else:
                j = layer - cfg.NA
                if j == 0:
                    self.norm_phase(hT, None, None, gvec("kv_norm"), xnT)
                    with ExitStack() as st:
                        ep = self.store_epilogue(st, kshT, BF16)
                        self.lin_fm(xnT, D, self.inp("kv_wk", [D, cfg.AW]), [(i * 128, 128) for i in range(cfg.HPC)], ep)
                    self.lin_tm_plain(xnT, self.inp("kv_wv", [D, cfg.AW]), cfg.AW, vsh)
                self.norm_phase(hT, None, None, gvec("b_norm_pre" + L), xnT)
                with ExitStack() as st:
                    ep = self.store_epilogue(st, qT, BF16, scale=lambda c0: 128 ** -0.5)
                    self.lin_fm(xnT, D, self.inp("b_wq" + L, [D, cfg.AW]), [(i * 128, 128) for i in range(cfg.HPC)], ep)
                self.attn_phase(qT, kshT, vsh, attT)
                mixK, mixX, wo = cfg.AW, attT, self.inp("b_wout" + L, [cfg.AW, D])
                gpost = gvec("b_norm_post" + L)
            if stop_after == ("mix", layer):
                break
            cnts = self.lin_fm_ar(mixX, mixK, wo, part2, red2)
            self.norm_phase([(red2[0], cnts[0]), (red2[1], cnts[1])], gpost, hT, gvec("f_norm_pre" + L), xnT)
            snapshot("h_mix" + L)
            if stop_after == ("n2", layer):
                break
            self.ffn_up_phase(xnT, self.inp("f_wup" + L, [D, cfg.FCH * 256]), self.inp("f_cw" + L, [128, cfg.FCH * 8]), aT)
            cnts = self.lin_fm_ar(aT, cfg.FP, self.inp("f_wdown" + L, [cfg.FP, D]), part2, red2,
                                  TT=(512 if cfg.FCH <= 32 else 256))
            self.norm_phase([(red2[0], cnts[0]), (red2[1], cnts[1])], gvec("f_norm_post" + L), hT,
                            gvec("ple_gate_norm" + L), xnT)
            snapshot("h_ffn" + L)
            if stop_after == ("n3", layer):
                break
            last = layer == cfg.depth - 1
            if last:
                dst = y
            elif TP == 1:
                dst = hT
            else:
                dst = ple_in
            self.ple_phase(xnT, self.inp("ple_gw" + L, [D, DC]), self.inp("ple_w" + L, [256, DC]),
                           pT[layer * 256:(layer + 1) * 256, :], hT, dst)
            if not last:
                if TP > 1:
                    c.collective("AllGather", ALU.bypass, ple_in[:, :], hT[:, :], agtmp)
                snapshot("h_out" + L)
            if stop_after == ("layer", layer):
                break
        c.barrier()
        self.gs.close()
        c.es.close()
        return nc


def _gv(v, KC):
    return np.ascontiguousarray(np.asarray(v, np.float32).reshape(KC, 128).T)


def shard_inputs(cfg, inp):
    D, NT, KC, DC, TP = cfg.D, cfg.NT, cfg.KC, cfg.DC, cfg.TP
    DK, DV, QKW, VW, HA = cfg.DK, cfg.DV, cfg.QKW, cfg.VW, cfg.HA
    f = lambda a: np.asarray(a, np.float32)
    x = f(inp["x"])
    p = f(inp["p"])
    consts = consts_array()
    shared = []
    for r in range(TP):
        m = {"consts": consts}
        for layer in range(cfg.depth):
            L = f"{layer}"
            if layer < cfg.NA:
                a = layer
                w_in = f(inp["a_w_in"][a])
                gb = f(inp["a_gate_bias"][a])
                hn = f(inp["a_head_norm"][a])
                for hl in range(HA):
                    h = r * HA + hl
                    Lh = f"{layer}_{hl}"
                    q = w_in[:, h * DK:(h + 1) * DK]
                    k = w_in[:, QKW + h * DK:QKW + (h + 1) * DK]
                    v = w_in[:, 2 * QKW + h * DV:2 * QKW + (h + 1) * DV]
                    og = w_in[:, 2 * QKW + VW + h * DV:2 * QKW + VW + (h + 1) * DV]
                    ig = w_in[:, 2 * QKW + 2 * VW + h:2 * QKW + 2 * VW + h + 1]
                    fg = w_in[:, 2 * QKW + 2 * VW + 8 + h:2 * QKW + 2 * VW + 8 + h + 1]
                    m["a_wqk" + Lh] = np.ascontiguousarray(np.concatenate([q, k], 1))
                    m["a_wtm" + Lh] = np.ascontiguousarray(np.concatenate([v, og, k, ig, fg], 1))
                    m["a_gb" + Lh] = np.ascontiguousarray(np.broadcast_to(np.array([gb[h], gb[8 + h]], np.float32)[None, :], (128, 2)))
                    m["a_hn" + Lh] = np.ascontiguousarray(np.broadcast_to(hn[h * DV:(h + 1) * DV][None, :], (128, DV)))
                m["a_norm_pre" + L] = _gv(inp["a_norm_pre"][a], KC)
                m["a_wout" + L] = np.ascontiguousarray(f(inp["a_w_out"][a])[r * HA * DV:(r + 1) * HA * DV, :])
                m["a_norm_post" + L] = _gv(inp["a_norm_post"][a], KC)
            else:
                j = layer - cfg.NA
                AW = cfg.AW
                if j == 0:
                    m["kv_norm"] = _gv(inp["kv_norm"], KC)
                    kvw = f(inp["kv_w"])
                    m["kv_wk"] = np.ascontiguousarray(kvw[:, r * AW:(r + 1) * AW])
                    m["kv_wv"] = np.ascontiguousarray(kvw[:, cfg.H * 128 + r * AW:cfg.H * 128 + (r + 1) * AW])
                m["b_norm_pre" + L] = _gv(inp["b_norm_pre"][j], KC)
                m["b_wq" + L] = np.ascontiguousarray(f(inp["b_w_q"][j])[:, r * AW:(r + 1) * AW])
                m["b_wout" + L] = np.ascontiguousarray(f(inp["b_w_out"][j])[r * AW:(r + 1) * AW, :])
                m["b_norm_post" + L] = _gv(inp["b_norm_post"][j], KC)
            F, FC, FCH, FP = cfg.F, cfg.FC, cfg.FCH, cfg.FP
            m["f_norm_pre" + L] = _gv(inp["f_norm_pre"][layer], KC)
            wup = f(inp["f_w_up"][layer])
            wu = np.zeros((D, FCH, 2, 128), np.float32)
            gpad = np.zeros((D, FP), np.float32)
            vpad = np.zeros((D, FP), np.float32)
            gpad[:, :FC] = wup[:, r * FC:(r + 1) * FC]
            vpad[:, :FC] = wup[:, F + r * FC:F + (r + 1) * FC]
            wu[:, :, 0, :] = gpad.reshape(D, FCH, 128)
            wu[:, :, 1, :] = vpad.reshape(D, FCH, 128)
            del gpad, vpad
            m["f_wup" + L] = wu.reshape(D, FCH * 256)
            cwl = np.zeros((FCH, 128, 2, 4), np.float32)
            convw = f(inp["f_conv_w"][layer])
            convb = f(inp["f_conv_b"][layer])
            for gv in range(2):
                sl = slice(gv * F + r * FC, gv * F + (r + 1) * FC)
                tmp = np.zeros((FP, 4), np.float32)
                tmp[:FC, 0:3] = convw[:, sl].T
                tmp[:FC, 3] = convb[sl]
                cwl[:, :, gv, :] = tmp.reshape(FCH, 128, 4)
            m["f_cw" + L] = np.ascontiguousarray(cwl.transpose(1, 0, 2, 3)).reshape(128, FCH * 8)
            wd = np.zeros((FP, D), np.float32)
            wd[:FC] = f(inp["f_w_down"][layer])[r * FC:(r + 1) * FC]
            m["f_wdown" + L] = wd
            m["f_norm_post" + L] = _gv(inp["f_norm_post"][layer], KC)
            m["ple_gate_norm" + L] = _gv(inp["ple_gate_norm"][layer], KC)
            m["ple_gw" + L] = np.ascontiguousarray(f(inp["ple_gate_w"][layer])[:, r * DC:(r + 1) * DC])
            m["ple_w" + L] = np.ascontiguousarray(f(inp["ple_w"][layer])[:, r * DC:(r + 1) * DC])
        shared.append(m)
    maps = []
    for g in range(cfg.B):
        xT = np.ascontiguousarray(x[g].T)
        pT = np.ascontiguousarray(p[:, g].transpose(0, 2, 1)).reshape(cfg.depth * 256, NT)
        for r in range(TP):
            m = dict(shared[r])
            m["xs"] = xT
            m["ps"] = pT
            maps.append(m)
    return maps


def run(cfg, inputs, debug=(), stop_after=None, trace=False):
    b = Builder(cfg, debug=debug)
    nc = b.build(stop_after=stop_after)
    maps = shard_inputs(cfg, inputs)
    maps = [{k: v for k, v in m.items() if k in b.inputs} for m in maps]
    for m in maps:
        for k, shp in b.inputs.items():
            assert m[k].shape == shp, (k, m[k].shape, shp)
    res = run_bass_kernel_spmd(nc, maps, core_ids=list(range(cfg.NCU)), trace=trace)
    return res


def assemble(cfg, results, key="y"):
    out = np.empty((cfg.B, cfg.S, cfg.D), np.float32)
    for g in range(cfg.B):
        for r in range(cfg.TP):
            out[g, :, r * cfg.DC:(r + 1) * cfg.DC] = results[g * cfg.TP + r][key].T
    return out


def kernel(**inputs):
    cfg = Cfg()
    res = run(cfg, inputs)
    return assemble(cfg, res.results)
```
